# Optimizing a Trainium2 kernel written in Bass

```python
import functools
import jax, jax.numpy as jnp
from jax import lax
import numpy as np

D_MODEL = 1024
BATCH = 2
SEQ = 8192
DEPTH = 1

GRID_W = 64
CTX_LEN = 256
EPS = 1e-6

GLA_HEADS = 4
GLA_DK = D_MODEL // 2 // GLA_HEADS
GLA_DV = D_MODEL // GLA_HEADS
GLA_QK = GLA_HEADS * GLA_DK
GLA_V = GLA_HEADS * GLA_DV
GLA_RANK = 16
GLA_TAU = 16.0
GLA_CHUNK = 64

SSM_INNER = 2 * D_MODEL
SSM_HEADDIM = 64
SSM_HEADS = SSM_INNER // SSM_HEADDIM
SSM_GROUPS = 4
SSM_HPG = SSM_HEADS // SSM_GROUPS
SSM_STATE = 128
SSM_BC = SSM_GROUPS * SSM_STATE
SSM_CONV = 4
CONV_LEFT = 2
SSM_CONV_DIM = SSM_INNER + 2 * SSM_BC
SSM_CHUNK = 128

D_FF = ((8 * D_MODEL // 3 + 255) // 256) * 256

IN_WIDTHS = (GLA_QK, GLA_QK, GLA_V, GLA_V, GLA_RANK, GLA_RANK,
             SSM_INNER, SSM_INNER, SSM_BC, SSM_BC, SSM_HEADS, SSM_HEADS,
             D_MODEL, D_MODEL)
D_IN = sum(IN_WIDTHS)

kernel_name = "hybrid_gla_ssd_prefix_dit"


def rms_norm(x, w):
    xf = x.astype(jnp.float32)
    y = xf * lax.rsqrt(jnp.mean(xf * xf, axis=-1, keepdims=True) + EPS)
    return (y * w.astype(jnp.float32)).astype(x.dtype)


def modulate(h, shift, scale):
    return h * (1 + scale) + shift


def _split_in(proj):
    idx = np.cumsum(IN_WIDTHS)[:-1].tolist()
    return jnp.split(proj, idx, axis=-1)


def _chunks(t, size):
    Bn, L = t.shape[:2]
    return jnp.moveaxis(t.reshape(Bn, L // size, size, *t.shape[2:]), 1, 0)


def _unchunks(t):
    n, Bn, C = t.shape[:3]
    return jnp.moveaxis(t, 0, 1).reshape(Bn, n * C, *t.shape[3:])


def _dwconv_centred(u, w, b):
    W = u.shape[1]
    up = jnp.pad(u, ((0, 0), (CONV_LEFT, SSM_CONV - 1 - CONV_LEFT), (0, 0)))
    out = b + up[:, 0:W] * w[0]
    for j in range(1, SSM_CONV):
        out = out + up[:, j:j + W] * w[j]
    return out


def _seq_conv(u, w, b):
    return _dwconv_centred(u, w, b)


def _row_conv(u, w, b, rows):
    Bn, L, Cc = u.shape
    return _dwconv_centred(u.reshape(Bn * rows, GRID_W, Cc), w, b).reshape(Bn, L, Cc)


def _gla_scan(q, k, v, log_g, s0):
    out_dtype = v.dtype
    f32 = jnp.float32
    qc, kc, vc, gc = (_chunks(t.astype(f32), GLA_CHUNK) for t in (q, k, v, log_g))
    mask = jnp.tril(jnp.ones((GLA_CHUNK, GLA_CHUNK), bool))[None, :, :, None, None]

    def step(S, inp):
        qi, ki, vi, gi = inp
        b = jnp.cumsum(gi, axis=1)
        b_last = b[:, -1]
        diff = b[:, :, None] - b[:, None, :]
        decay = jnp.exp(jnp.where(mask, diff, -jnp.inf))
        att = jnp.einsum('bthk,bshk,btshk->bhts', qi, ki, decay)
        o = jnp.einsum('bhts,bshv->bthv', att, vi) + jnp.einsum('bthk,bhkv->bthv', qi * jnp.exp(b), S)
        S_new = jnp.exp(b_last)[..., None] * S + jnp.einsum(
            'bshk,bshv->bhkv', ki * jnp.exp(b_last[:, None] - b), vi)
        return S_new, o

    S_fin, o = lax.scan(step, s0, (qc, kc, vc, gc))
    return _unchunks(o).astype(out_dtype), S_fin


def _gla_state(k, v, log_g):
    b = jnp.cumsum(log_g.astype(jnp.float32), axis=1)
    return jnp.einsum('blhk,blhv->bhkv', k.astype(jnp.float32) * jnp.exp(b[:, -1:] - b), v.astype(jnp.float32))


def _ssd_scan(x, bm, cm, a, s0):
    out_dtype = bm.dtype
    f32 = jnp.float32
    xc, bc, cc, ac = (_chunks(t.astype(f32), SSM_CHUNK) for t in (x, bm, cm, a))
    mask = jnp.tril(jnp.ones((SSM_CHUNK, SSM_CHUNK), bool))[None, :, :, None, None]

    def step(S, inp):
        xi, bi, ci, ai = inp
        cum = jnp.cumsum(ai, axis=1)
        seg = cum[:, :, None] - cum[:, None, :]
        Lm = jnp.exp(jnp.where(mask, seg, -jnp.inf))
        cb = jnp.einsum('btgn,bsgn->btsg', ci, bi)
        y = jnp.einsum('btsg,btsge,bsgep->btgep', cb, Lm, xi)
        y = y + jnp.einsum('btgn,bgenp->btgep', ci, S) * jnp.exp(cum)[..., None]
        S_new = jnp.exp(cum[:, -1])[..., None, None] * S + jnp.einsum(
            'bsgn,bsge,bsgep->bgenp', bi, jnp.exp(cum[:, -1:] - cum), xi)
        return S_new, y

    S_fin, y = lax.scan(step, s0, (xc, bc, cc, ac))
    return _unchunks(y).astype(out_dtype), S_fin


def _ssd_state(x, bm, a):
    cum = jnp.cumsum(a.astype(jnp.float32), axis=1)
    return jnp.einsum('blgn,blge,blgep->bgenp', bm.astype(jnp.float32),
                      jnp.exp(cum[:, -1:] - cum), x.astype(jnp.float32))


def _flip(t):
    return jnp.flip(t, axis=1)


def _bidir(scan_fn, fwd_in, bwd_in, s_f, s_b):
    o_f, st_f = scan_fn(*fwd_in, s_f)
    o_b, st_b = scan_fn(*[_flip(t) for t in bwd_in], s_b)
    return o_f + _flip(o_b), st_f, st_b


def _gla_features(q, k, v, lr_f, lr_b, lp):
    Bn, L, _ = q.shape
    heads = lambda t, d: t.reshape(Bn, L, GLA_HEADS, d)
    q = heads(q, GLA_DK) * (GLA_DK ** -0.5)
    k = heads(k, GLA_DK)
    v = heads(v, GLA_DV)

    def log_gate(lr, up, bias):
        pre = (lr @ up + bias).astype(jnp.float32)
        return heads(jax.nn.log_sigmoid(pre) / GLA_TAU, GLA_DK)

    return (q, k, v, log_gate(lr_f, lp['gla_up_f'], lp['gla_bias_f']),
            log_gate(lr_b, lp['gla_up_b'], lp['gla_bias_b']))


def _ssm_features(xs, bm, cm, dt_f, dt_b, lp, conv_fn):
    xbc = jax.nn.silu(conv_fn(jnp.concatenate([xs, bm, cm], axis=-1), lp['conv_w'], lp['conv_b']))
    xs, bm, cm = jnp.split(xbc, [SSM_INNER, SSM_INNER + SSM_BC], axis=-1)
    Bn, L, _ = xs.shape
    xh = xs.reshape(Bn, L, SSM_GROUPS, SSM_HPG, SSM_HEADDIM)
    bm = bm.reshape(Bn, L, SSM_GROUPS, SSM_STATE)
    cm = cm.reshape(Bn, L, SSM_GROUPS, SSM_STATE)

    def direction(dt_raw, dt_bias, a_log):
        dt = jax.nn.softplus((dt_raw + dt_bias).astype(jnp.float32)).reshape(Bn, L, SSM_GROUPS, SSM_HPG)
        a = -jnp.exp(a_log.astype(jnp.float32)).reshape(SSM_GROUPS, SSM_HPG) * dt
        return xh * dt[..., None], a

    return (xh, bm, cm, direction(dt_f, lp['dt_bias_f'], lp['a_log_f']),
            direction(dt_b, lp['dt_bias_b'], lp['a_log_b']))


def _mixer(parts, lp, init, conv_fn):
    q, k, v, r, lr_f, lr_b, z, xs, bm, cm, dt_f, dt_b, gate_a, gate_b = parts
    Bn, L, _ = q.shape
    gq, gk, gv, lg_f, lg_b = _gla_features(q, k, v, lr_f, lr_b, lp)
    o_a, st_af, st_ab = _bidir(_gla_scan, (gq, gk, gv, lg_f), (gq, gk, gv, lg_b), init[0], init[1])
    o_a = rms_norm(o_a, lp['gla_norm_w']) * jax.nn.silu(r).reshape(Bn, L, GLA_HEADS, GLA_DV)
    y_a = o_a.reshape(Bn, L, GLA_V) @ lp['w_pa']
    xh, sb, sc, (x_f, a_f), (x_b, a_b) = _ssm_features(xs, bm, cm, dt_f, dt_b, lp, conv_fn)
    o_b, st_bf, st_bb = _bidir(_ssd_scan, (x_f, sb, sc, a_f), (x_b, sb, sc, a_b), init[2], init[3])
    o_b = o_b + lp['d_skip'].reshape(SSM_GROUPS, SSM_HPG, 1) * xh
    o_b = o_b.reshape(Bn, L, SSM_INNER) * jax.nn.silu(z)
    o_b = rms_norm(o_b.reshape(Bn, L, SSM_GROUPS, SSM_INNER // SSM_GROUPS),
                   lp['ssm_norm_w'].reshape(SSM_GROUPS, SSM_INNER // SSM_GROUPS)).reshape(Bn, L, SSM_INNER)
    y_b = o_b @ lp['w_pb']
    merged = jax.nn.sigmoid(gate_a) * y_a + jax.nn.sigmoid(gate_b) * y_b
    return merged @ lp['w_out'], (st_af, st_ab, st_bf, st_bb)


def _context_states(parts, lp):
    k, v, lr_f, lr_b = parts[1], parts[2], parts[4], parts[5]
    xs, bm, cm, dt_f, dt_b = parts[7], parts[8], parts[9], parts[10], parts[11]
    _, gk, gv, lg_f, lg_b = _gla_features(parts[0], k, v, lr_f, lr_b, lp)
    _, sb, _, (x_f, a_f), (x_b, a_b) = _ssm_features(xs, bm, cm, dt_f, dt_b, lp, _seq_conv)
    return (_gla_state(gk, gv, lg_f), _gla_state(_flip(gk), _flip(gv), _flip(lg_b)),
            _ssd_state(x_f, sb, a_f), _ssd_state(_flip(x_b), _flip(sb), _flip(a_b)))


def _zero_states(Bn):
    f32 = jnp.float32
    g = jnp.zeros((Bn, GLA_HEADS, GLA_DK, GLA_DV), f32)
    s = jnp.zeros((Bn, SSM_GROUPS, SSM_HPG, SSM_STATE, SSM_HEADDIM), f32)
    return (g, g, s, s)


def _swiglu(h, w_gate, w_up, w_down):
    return (jax.nn.silu(h @ w_gate) * (h @ w_up)) @ w_down


def setup_inputs(seed: int = 0) -> dict:
    key = jax.random.key(seed)
    ks = iter(jax.random.split(key, 40))
    nrm = lambda shape, s: jax.random.normal(next(ks), shape, jnp.float32) * s
    L_ = DEPTH
    dt = jnp.exp(jax.random.uniform(next(ks), (2, L_, SSM_HEADS), jnp.float32,
                                    np.log(1e-3), np.log(1e-1)))
    dt_bias = dt + jnp.log(-jnp.expm1(-dt))
    a_log = jnp.log(jax.random.uniform(next(ks), (2, L_, SSM_HEADS), jnp.float32, 1.0, 16.0))
    return {
        'x': nrm((BATCH, SEQ, D_MODEL), 1.0),
        'c': nrm((BATCH, D_MODEL), 1.0),
        'ctx': nrm((BATCH, CTX_LEN, D_MODEL), 1.0),
        'c_ctx': nrm((D_MODEL,), 1.0),
        'w_ada': nrm((L_, D_MODEL, 6 * D_MODEL), 0.02),
        'b_ada': nrm((L_, 6 * D_MODEL), 0.01),
        'norm1_w': 1.0 + nrm((L_, D_MODEL), 0.02),
        'w_in': nrm((L_, D_MODEL, D_IN), D_MODEL ** -0.5),
        'gla_up_f': nrm((L_, GLA_RANK, GLA_QK), GLA_RANK ** -0.5),
        'gla_bias_f': nrm((L_, GLA_QK), 0.1),
        'gla_up_b': nrm((L_, GLA_RANK, GLA_QK), GLA_RANK ** -0.5),
        'gla_bias_b': nrm((L_, GLA_QK), 0.1),
        'gla_norm_w': 1.0 + nrm((L_, GLA_DV), 0.02),
        'conv_w': nrm((L_, SSM_CONV, SSM_CONV_DIM), SSM_CONV ** -0.5),
        'conv_b': nrm((L_, SSM_CONV_DIM), 0.02),
        'dt_bias_f': dt_bias[0],
        'dt_bias_b': dt_bias[1],
        'a_log_f': a_log[0],
        'a_log_b': a_log[1],
        'd_skip': 1.0 + nrm((L_, SSM_HEADS), 0.02),
        'ssm_norm_w': 1.0 + nrm((L_, SSM_INNER), 0.02),
        'w_pa': nrm((L_, GLA_V, D_MODEL), GLA_V ** -0.5),
        'w_pb': nrm((L_, SSM_INNER, D_MODEL), SSM_INNER ** -0.5),
        'w_out': nrm((L_, D_MODEL, D_MODEL), D_MODEL ** -0.5),
        'norm2_w': 1.0 + nrm((L_, D_MODEL), 0.02),
        'w_gate': nrm((L_, D_MODEL, D_FF), D_MODEL ** -0.5),
        'w_up': nrm((L_, D_MODEL, D_FF), D_MODEL ** -0.5),
        'w_down': nrm((L_, D_FF, D_MODEL), D_FF ** -0.5),
        'final_norm_w': 1.0 + nrm((D_MODEL,), 0.02),
    }


def reference(x, c, ctx, c_ctx, w_ada, b_ada, norm1_w, w_in, gla_up_f, gla_bias_f, gla_up_b, gla_bias_b,
              gla_norm_w, conv_w, conv_b, dt_bias_f, dt_bias_b, a_log_f, a_log_b, d_skip, ssm_norm_w,
              w_pa, w_pb, w_out, norm2_w, w_gate, w_up, w_down, final_norm_w):
    rows = x.shape[1] // GRID_W
    latent_conv = functools.partial(_row_conv, rows=rows)
    h_lat, h_ctx = x, ctx
    for layer in range(DEPTH):
        lp = {
            'gla_up_f': gla_up_f[layer], 'gla_bias_f': gla_bias_f[layer],
            'gla_up_b': gla_up_b[layer], 'gla_bias_b': gla_bias_b[layer],
            'gla_norm_w': gla_norm_w[layer], 'conv_w': conv_w[layer], 'conv_b': conv_b[layer],
            'dt_bias_f': dt_bias_f[layer], 'dt_bias_b': dt_bias_b[layer],
            'a_log_f': a_log_f[layer], 'a_log_b': a_log_b[layer], 'd_skip': d_skip[layer],
            'ssm_norm_w': ssm_norm_w[layer], 'w_pa': w_pa[layer], 'w_pb': w_pb[layer], 'w_out': w_out[layer],
        }
        ada = jax.nn.silu(c)[:, None, :] @ w_ada[layer] + b_ada[layer]
        sh1, sc1, g1, sh2, sc2, g2 = jnp.split(ada, 6, axis=-1)
        ada_c = jax.nn.silu(c_ctx) @ w_ada[layer] + b_ada[layer]
        csh1, csc1, cg1, csh2, csc2, cg2 = jnp.split(ada_c, 6, axis=-1)

        parts_c = _split_in(modulate(rms_norm(h_ctx, norm1_w[layer]), csh1, csc1) @ w_in[layer])
        if layer == DEPTH - 1:
            states = _context_states(parts_c, lp)
        else:
            mix_c, states = _mixer(parts_c, lp, _zero_states(h_ctx.shape[0]), _seq_conv)
            h_ctx = h_ctx + cg1 * mix_c
            h_ctx = h_ctx + cg2 * _swiglu(modulate(rms_norm(h_ctx, norm2_w[layer]), csh2, csc2),
                                          w_gate[layer], w_up[layer], w_down[layer])

        parts = _split_in(modulate(rms_norm(h_lat, norm1_w[layer]), sh1, sc1) @ w_in[layer])
        mix, _ = _mixer(parts, lp, states, latent_conv)
        h_lat = h_lat + g1 * mix
        h_lat = h_lat + g2 * _swiglu(modulate(rms_norm(h_lat, norm2_w[layer]), sh2, sc2),
                                     w_gate[layer], w_up[layer], w_down[layer])
    return rms_norm(h_lat, final_norm_w)
```

```python
import numpy as np
import ml_dtypes
import concourse.bass as bass
import concourse.mybir as mybir
from concourse.bass_utils import run_bass_kernel_spmd

F32 = mybir.dt.float32
BF16 = mybir.dt.bfloat16
AF = mybir.ActivationFunctionType
ALU = mybir.AluOpType

D = 1024
SEQ = 8192
CTX = 256
NCORE = 8
DFF = 2816
EPS = 1e-6
OQ, OK_, OV, OR, OLF, OLB, OZ, OXS, OB, OC, ODF, ODB = 0, 128, 256, 512, 768, 784, 800, 1312, 1824, 1952, 2080, 2088
NW1 = 2096
BIG = 1.0e30


class DmaSem:
    def __init__(self, sem):
        self.sem = sem
        self.count = 0


class Prog:
    ENGS = ("pe", "act", "dve", "pool", "sp")

    def __init__(self, nc, sems):
        self.nc = nc
        self.ops = {e: [] for e in self.ENGS}
        self.seq = {e: 0 for e in self.ENGS}
        self.sem = sems
        self.waited = {e: {} for e in self.ENGS}
        self.last_w = {}
        self.readers = {}
        self.final_tokens = []

    def _deps(self, eng, reads, writes):
        toks = []
        for r in reads:
            t = self.last_w.get(r)
            if t is not None:
                toks.append(t)
        for w in writes:
            t = self.last_w.get(w)
            if t is not None:
                toks.append(t)
            toks.extend(self.readers.get(w, ()))
        waits = []
        wd = self.waited[eng]
        best = {}
        for (sk, sem, val, teng) in toks:
            if teng == eng and eng == "pe":
                continue
            if wd.get(sk, -1) >= val:
                continue
            if best.get(sk, (None, -1))[1] < val:
                best[sk] = (sem, val)
        for sk, (sem, val) in best.items():
            wd[sk] = val
            waits.append((sem, val))
        return waits

    def _commit(self, tok, reads, writes):
        for r in reads:
            self.readers.setdefault(r, []).append(tok)
        for w in writes:
            self.last_w[w] = tok
            self.readers[w] = []

    PS_ALIAS = {"pG0": "pG", "pG1": "pG", "pG2": "pG", "pG3": "pG", "pO0": "pO", "pO1": "pO", "pSs": "pS", "pSc": "pS"}
    PS_KEYS = {"pT", "pA", "pB", "pG", "pO", "pC", "pS", "pY"}

    def _norm(self, reads, writes):
        r2, w2 = [], []
        for r in reads:
            if r is None:
                continue
            r = self.PS_ALIAS.get(r, r) if isinstance(r, str) else r
            (w2 if r in self.PS_KEYS else r2).append(r)
        for w in writes:
            if w is None:
                continue
            w = self.PS_ALIAS.get(w, w) if isinstance(w, str) else w
            w2.append(w)
        return r2, w2

    def op(self, eng, fn, reads=(), writes=()):
        reads, writes = self._norm(reads, writes)
        waits = self._deps(eng, reads, writes)
        self.seq[eng] += 1
        tok = (eng, self.sem[eng], self.seq[eng], eng)
        self.ops[eng].append((waits, fn, self.sem[eng], 1))
        self._commit(tok, reads, writes)
        return tok

    def dma(self, eng, fn, dsem, reads=(), writes=()):
        waits = self._deps(eng, list(reads), list(writes))
        dsem.count += 1
        tok = (id(dsem), dsem.sem, 16 * dsem.count, "dma")
        self.ops[eng].append((waits, fn, dsem.sem, 16))
        self._commit(tok, reads, writes)
        return tok

    def cc(self, eng, fn, csem, reads=(), writes=()):
        waits = self._deps(eng, list(reads), list(writes))
        csem.count += 1
        tok = (id(csem), csem.sem, csem.count, "cc")
        self.ops[eng].append((waits, fn, csem.sem, None))
        self._commit(tok, reads, writes)
        return tok

    def seal(self, dsem):
        final = (id(dsem), dsem.sem, 16 * dsem.count, "dma")
        for k, t in list(self.last_w.items()):
            if t[0] == id(dsem):
                self.last_w[k] = final

    def barrier(self, extra):
        for eng in self.ENGS:
            waits = []
            for e2 in self.ENGS:
                if e2 != eng and self.seq[e2] > 0 and self.waited[eng].get(e2, -1) < self.seq[e2]:
                    waits.append((self.sem[e2], self.seq[e2]))
                    self.waited[eng][e2] = self.seq[e2]
            for (key, sem, val) in extra:
                if val > 0 and self.waited[eng].get(key, -1) < val:
                    waits.append((sem, val))
                    self.waited[eng][key] = val
            self.ops[eng].append((waits, None, None, None))

    def emit(self, eng, handle):
        for (waits, fn, sem, inc) in self.ops[eng]:
            for (s, v) in waits:
                handle.wait_ge(s, v)
            if fn is None:
                continue
            ins = fn(handle)
            if inc is None:
                ins.then_inc(sem)
            else:
                ins.then_inc(sem, inc)
        self.ops[eng] = []


def bcast_free(ap2d, n):
    return ap2d.broadcast_to([ap2d.shape[0], n])


def build_program(dbg=None):
    nc = bass.Bass("TRN2", target_bir_lowering=False)
    dt_in = lambda name, shape, dt=F32: nc.dram_tensor(name, list(shape), dt, kind="ExternalInput")
    x_d = dt_in("x", [SEQ, D])
    ctx_d = dt_in("ctx", [CTX, D])
    cvec_d = dt_in("cvec", [2, D])
    wada_d = dt_in("w_ada", [D, 6 * D])
    bada_d = dt_in("b_ada", [6 * D])
    n1w_d = dt_in("norm1_w", [D])
    w1_d = dt_in("w1", [D, NW1])
    up_d = dt_in("gla_up", [2, 16, 128])
    gb_d = dt_in("gla_bias", [2, 128])
    gnw_d = dt_in("gla_norm_w", [256])
    cw_d = dt_in("conv_w", [4, 768])
    cb_d = dt_in("conv_b", [768])
    dtb_d = dt_in("dt_bias", [2, 8])
    alog_d = dt_in("a_log", [2, 8])
    dsk_d = dt_in("d_skip", [8])
    snw_d = dt_in("ssm_norm_w", [512])
    wg_d = dt_in("w_gates", [D, 2 * D])
    wpa_d = dt_in("w_pa", [D, D])
    wpb_d = dt_in("w_pb", [2 * D, D])
    wout_d = dt_in("w_out", [D, D])
    n2w_d = dt_in("norm2_w", [D])
    wgate_d = dt_in("w_gate", [D, DFF])
    wup_d = dt_in("w_up", [D, DFF])
    wdown_d = dt_in("w_down", [DFF, D])
    fnw_d = dt_in("final_norm_w", [D])
    consts_d = dt_in("consts", [128, 11 * 128])
    xq_d = dt_in("xq", [2048, D])
    out_d = nc.dram_tensor("out", [2048, D], F32, kind="ExternalOutput")
    prev_d = nc.dram_tensor("prev_scr", [SEQ, 768], F32)
    ex_d = nc.dram_tensor("ex_scr", [16, 768, 512], BF16)
    gx_d = nc.dram_tensor("gx_scr", [16, 4 * 768, 512], BF16)
    h_d = nc.dram_tensor("h_scr", [2048, D], F32)
    gxq_d = nc.dram_tensor("gxq_scr", [4, 4 * 768, 512], BF16)
    dbg_t = {}
    if dbg:
        for name, shape in dbg.items():
            if name.startswith("_"):
                continue
            dbg_t[name] = nc.dram_tensor("dbg_" + name, list(shape), BF16 if name.endswith("_bf") else F32, kind="ExternalOutput")

    from contextlib import ExitStack
    es = ExitStack()
    with es:
        cur = [es]

        def sb(name, shape, dt=F32):
            return cur[0].enter_context(nc.sbuf_tensor("s_" + name, list(shape), dt))

        def ps(name, shape, dt=F32):
            return es.enter_context(nc.psum_tensor("p_" + name, list(shape), dt))

        def newsem(name):
            return es.enter_context(nc.semaphore(name))

        sems = {e: newsem("sem_" + e) for e in Prog.ENGS}
        P = Prog(nc, sems)
        dsems = {}

        def DS(name):
            if name not in dsems:
                dsems[name] = DmaSem(newsem("d_" + name))
            return dsems[name]

        def tap(name, src_ap, reads, idx=None):
            if name not in dbg_t:
                return
            dst = dbg_t[name].ap() if idx is None else dbg_t[name][idx]
            P.dma("sp", lambda e: e.dma_start(out=dst, in_=src_ap), DS("tap"), reads, [])

        cst = sb("cst", [128, 11, 128], BF16)
        cstf = sb("cstf", [128, 2, 128], F32)
        IDN, MF, MB, SF, SB_, MF16, MB16, SF16, SB16, ONES = range(10)
        ada_col = sb("ada_col", [128, 48, 2], F32)
        wmod = sb("wmod", [128, 8, 3], F32)
        shbf = sb("shbf", [128, 8, 3], BF16)
        n1col = sb("n1col", [128, 8], F32)
        n2col = sb("n2col", [128, 8], F32)
        ccol = sb("ccol", [128, 2, 8], F32)
        scol = sb("scol", [128, 2, 8], BF16)
        badac = sb("badac", [128, 48], F32)
        onesrow = sb("onesrow", [1, 512], BF16)
        onesrowf = sb("onesrowf", [1, 128], F32)
        g_bc = sb("g_bc", [128, 2, D], F32)
        fnw_bc = sb("fnw_bc", [128, D], F32)
        xt = [sb("xt%d" % i, [128, D], F32) for i in range(2)]
        sq_junk = sb("sq_junk", [128, D], BF16)
        ss = sb("ss", [128, 8], F32)
        xs_ = [sb("xs%d" % i, [128, D], BF16) for i in range(2)]
        epsc = sb("epsc", [128, 2], F32)
        es1 = ExitStack()
        es1.__enter__()
        cur[0] = es1
        w1m = sb("w1m", [128, 8, NW1], BF16)
        c1col = sb("c1col", [128, 9, 2], F32)
        c1row = sb("c1row", [1, 2, 1160 + 8], BF16)
        dtlo = sb("dtlo", [1, 2, 2, 8], BF16)
        dthi = sb("dthi", [1, 2, 2, 8], BF16)
        upsb = sb("upsb", [16, 2, 128], BF16)
        gbrow = sb("gbrow", [1, 2, 128], BF16)
        gnw_bc = sb("gnw_bc", [128, 256], F32)
        snw_bc = sb("snw_bc", [128, 512], F32)
        dsk_bc = sb("dsk_bc", [128, 8], F32)
        negA = sb("negA", [128, 2, 8], F32)
        cbcol = sb("cbcol", [128, 6], F32)
        cdiag = sb("cdiag", [128, 24, 128], BF16)
        c1lr = sb("c1lr", [16, 2, 2], F32)
        pT = ps("pT", [128, 8, 128], BF16)
        pA = ps("pA", [128, 512], F32)
        pB = ps("pB", [128, 512], F32)
        pG = ps("pG", [128, 4, 128], F32)
        pO = ps("pO", [128, 2, 256], F32)
        pC = ps("pC", [128, 4, 128], F32)
        pY = ps("pY", [128, 512], F32)
        pS = ps("pS", [128, 512], F32)
        pSx = pS[:, 192:512].bitcast(BF16)
        pCb = pC[:, :, :].rearrange("p a b -> p (a b)").bitcast(BF16)
        pTf = pT[:, :, :].rearrange("p a b -> p (a b)").bitcast(F32)
        ccs = DmaSem(newsem("ccsem"))
        def all_dma_tokens():
            toks = [(id(d), d.sem, 16 * d.count) for d in dsems.values()]
            toks.append((id(ccs), ccs.sem, ccs.count))
            return toks

        def flush():
            P.barrier(all_dma_tokens())
            with nc.Block() as block:
                @block.sync
                def _(e):
                    P.emit("sp", e)

                @block.tensor
                def _(e):
                    P.emit("pe", e)

                @block.scalar
                def _(e):
                    P.emit("act", e)

                @block.vector
                def _(e):
                    P.emit("dve", e)

                @block.gpsimd
                def _(e):
                    P.emit("pool", e)

        es_s = ExitStack()
        es_s.__enter__()
        cur[0] = es_s
        c1rowf = sb("c1rowf", [1, 2, 32], F32)
        dtbrow = sb("dtbrow", [1, 2, 8], F32)
        alog_bc = sb("alog_bc", [128, 2, 8], F32)
        cwcol = sb("cwcol", [128, 4, 6], F32)
        grow = sb("grow", [1, 2, D], F32)
        badar = sb("badar", [1, 2, D], F32)


        def act(fn, reads, writes):
            return P.op("act", fn, reads, writes)

        def dve(fn, reads, writes):
            return P.op("dve", fn, reads, writes)

        def pool(fn, reads, writes):
            return P.op("pool", fn, reads, writes)

        def pe(fn, reads, writes):
            return P.op("pe", fn, reads, writes)

        def mm(out, lhsT, rhs, start, stop, reads, writes):
            return pe(lambda e: e.matmul(out, lhsT=lhsT, rhs=rhs, start=start, stop=stop), reads, writes)

        def tp(out, in_, reads, writes):
            return pe(lambda e: e.transpose(out, in_, cst[:, IDN, :]), reads + ["cst"], writes)

        def dma(q, out, in_, dsem, reads, writes, **kw):
            return P.dma(q, lambda e: e.dma_start(out=out, in_=in_, **kw), dsem, reads, writes)

        def activation(out, in_, func, reads, writes, bias=None, scale=None, accum_out=None):
            kw = {}
            if bias is not None:
                kw["bias"] = bias
            if scale is not None:
                kw["scale"] = scale
            if accum_out is not None:
                kw["accum_out"] = accum_out
            return act(lambda e: e.activation(out=out, in_=in_, func=func, **kw), reads, writes)

        dma("pool", cst[:, :, :], consts_d.ap().rearrange("p (a b) -> p a b", b=128), DS("cst"), [], ["cst"])
        dma("sp", cstf[:, 0, :], consts_d[:, 0:128], DS("cstf"), [], ["cstf"])
        dma("sp", cstf[:, 1, :], consts_d[:, 9 * 128:10 * 128], DS("cstf"), [], ["cstf"])
        dve(lambda e: e.memset(onesrow[:, :], 1.0), [], ["onesrow"])
        dve(lambda e: e.memset(onesrowf[:, :], 1.0), [], ["onesrowf"])
        sp_ = DS("small")
        for r_ in range(2):
            dma("sp", ccol[:, r_, :], cvec_d[r_].rearrange("(c p) -> p c", p=128), sp_, [], ["ccol"], allow_slow_non_contiguous=True)
        dma("sp", badac[:, :], bada_d.ap().rearrange("(c p) -> p c", p=128), sp_, [], ["badac"], allow_slow_non_contiguous=True)
        dma("sp", n1col[:, :], n1w_d.ap().rearrange("(c p) -> p c", p=128), sp_, [], ["n1col"], allow_slow_non_contiguous=True)
        dma("sp", n2col[:, :], n2w_d.ap().rearrange("(c p) -> p c", p=128), sp_, [], ["n2col"], allow_slow_non_contiguous=True)
        for j_ in range(4):
            dma("sp", cwcol[:, j_, :], cw_d[j_].rearrange("(c p) -> p c", p=128), sp_, [], ["cwcol"], allow_slow_non_contiguous=True)
        dma("sp", cbcol[:, :], cb_d.ap().rearrange("(c p) -> p c", p=128), sp_, [], ["cbcol"], allow_slow_non_contiguous=True)
        dma("sp", gnw_bc[:, :], gnw_d.ap().partition_broadcast(128), sp_, [], ["gnw_bc"])
        dma("sp", snw_bc[:, :], snw_d.ap().partition_broadcast(128), sp_, [], ["snw_bc"])
        dma("sp", dsk_bc[:, :], dsk_d.ap().partition_broadcast(128), sp_, [], ["dsk_bc"])
        dma("sp", alog_bc[:, :, :], alog_d.ap().partition_broadcast(128), sp_, [], ["alog_bc"])
        dma("sp", fnw_bc[:, :], fnw_d.ap().partition_broadcast(128), sp_, [], ["fnw_bc"])
        dma("sp", dtbrow[:, :, :], dtb_d.ap().rearrange("(o a) b -> o a b", o=1), sp_, [], ["dtbrow"])
        dma("sp", badar[:, 0, :], bada_d.ap().rearrange("(o n) -> o n", o=1)[:, 2 * D:3 * D], sp_, [], ["badar"])
        dma("sp", badar[:, 1, :], bada_d.ap().rearrange("(o n) -> o n", o=1)[:, 5 * D:6 * D], sp_, [], ["badar"])
        dma("pool", upsb[:, :, :], up_d.ap().rearrange("a r k -> r a k"), DS("cst"), [], ["upsb"])
        dma("pool", gbrow[:, :, :], gb_d.ap().rearrange("(o a) k -> o a k", o=1), DS("cst"), [], ["gbrow"])
        dma("pool", w1m[:, :, :], w1_d.ap().rearrange("(c p) n -> p c n", p=128), DS("w1"), [], ["w1m"])

        for nm_ in ("small", "cst", "cstf"):
            P.seal(DS(nm_))
        activation(scol[:, :, :], ccol[:, :, :], AF.Silu, ["ccol"], ["scol"])
        wab = [sb("wab%d" % i, [128, 8, 512], BF16) for i in range(2)]
        for blk in range(12):
            bi = blk % 2
            dma("pool", wab[bi][:, :, :], wada_d[:, blk * 512:(blk + 1) * 512].rearrange("(c p) n -> p c n", p=128),
                DS("wab%d" % bi), [], ["wab%d" % bi])
            for c4 in range(4):
                cc = blk * 4 + c4
                for kc in range(8):
                    mm(pA[:, cc * 2:cc * 2 + 2], wab[bi][:, kc, c4 * 128:(c4 + 1) * 128], scol[:, :, kc],
                       kc == 0, kc == 7, ["wab%d" % bi, "scol"], ["pA"])
            if blk in (4, 5, 10, 11):
                gi = 0 if blk < 6 else 1
                half = blk % 2
                for kc in range(8):
                    mm(pB[0:1, :], scol[:, 0, kc:kc + 1], wab[bi][:, kc, :], kc == 0, kc == 7, ["wab%d" % bi, "scol"], ["pB"])
                dve(lambda e, gi=gi, half=half: e.tensor_tensor(out=grow[:, gi, half * 512:(half + 1) * 512], in0=pB[0:1, :],
                                                               in1=badar[:, gi, half * 512:(half + 1) * 512], op=ALU.add),
                    ["pB", "badar"], ["grow"])
        dve(lambda e: e.tensor_tensor(out=ada_col[:, :, :], in0=pA[:, 0:96].rearrange("p (c r) -> p c r", r=2),
                                      in1=badac[:, :].unsqueeze(2).broadcast_to([128, 48, 2]), op=ALU.add),
            ["pA", "badac"], ["ada_col"])
        for mi, (ncol, scb, shb, r) in enumerate([(n1col, 8, 0, 0), (n1col, 8, 0, 1), (n2col, 32, 24, 0)]):
            dve(lambda e, mi=mi, ncol=ncol, scb=scb, r=r: e.scalar_tensor_tensor(
                out=wmod[:, :, mi], in0=ada_col[:, scb:scb + 8, r], scalar=1.0, in1=ncol[:, :], op0=ALU.add, op1=ALU.mult),
                ["ada_col", "n1col", "n2col"], ["wmod"])
            dve(lambda e, mi=mi, shb=shb, r=r: e.tensor_copy(out=shbf[:, :, mi], in_=ada_col[:, shb:shb + 8, r]),
                ["ada_col"], ["shbf"])
        for gi in range(2):
            for half in range(2):
                mm(pA[:, :], onesrowf[:, :], grow[:, gi, half * 512:(half + 1) * 512], True, True, ["onesrowf", "grow"], ["pA"])
                act(lambda e, gi=gi, half=half: e.copy(out=g_bc[:, gi, half * 512:(half + 1) * 512], in_=pA[:, :]), ["pA"], ["g_bc"])
        activation(negA[:, :, :], alog_bc[:, :, :], AF.Exp, ["alog_bc"], ["negA"])
        dve(lambda e: e.tensor_scalar(out=negA[:, :, :], in0=negA[:, :, :], scalar1=-1.0, scalar2=None, op0=ALU.mult), ["negA"], ["negA"])
        for cc in range(6):
            for j in range(4):
                dve(lambda e, cc=cc, j=j: e.tensor_scalar(out=cdiag[:, cc * 4 + j, :], in0=cst[:, IDN, :],
                                                          scalar1=cwcol[:, j, cc:cc + 1], scalar2=None, op0=ALU.mult),
                    ["cst", "cwcol"], ["cdiag"])
        FM_OFFS = [OQ, OK_, OXS, OXS + 128, OXS + 256, OXS + 384, OB, OC]
        for m in range(2):
            for gi_, off in enumerate(FM_OFFS):
                for kc in range(8):
                    mm(pG[:, 0, gi_ * 2 + m:gi_ * 2 + m + 1], w1m[:, kc, off:off + 128], shbf[:, kc, m:m + 1], kc == 0, kc == 7,
                       ["w1m", "shbf"], ["pG"])
        for m in range(2):
            dve(lambda e, m=m: e.tensor_copy(out=c1col[:, 0:8, m], in_=pG[:, 0, 0:16].rearrange("p (g m) -> p g m", m=2)[:, :, m]),
                ["pG"], ["c1col"])
        for m in range(2):
            for d_ in range(2):
                off = OLF if d_ == 0 else OLB
                for kc in range(8):
                    mm(pG[0:16, 1, m * 2 + d_:m * 2 + d_ + 1], w1m[:, kc, off:off + 16], shbf[:, kc, m:m + 1], kc == 0, kc == 7,
                       ["w1m", "shbf"], ["pG"])
        dve(lambda e: e.tensor_copy(out=c1lr[:, :, :], in_=pG[0:16, 1, 0:4].rearrange("p (m d) -> p m d", d=2)), ["pG"], ["c1lr"])
        for m in range(2):
            for (o0, n0, dst) in [(OK_, 384, 0), (OR, 256, 384), (OZ, 512, 640)]:
                for kc in range(8):
                    mm(pA[0:1, 0:n0], shbf[:, kc, m:m + 1], w1m[:, kc, o0:o0 + n0], kc == 0, kc == 7, ["w1m", "shbf"], ["pA"])
                act(lambda e, m=m, n0=n0, dst=dst: e.copy(out=c1row[:, m, dst:dst + n0], in_=pA[0:1, 0:n0]), ["pA"], ["c1row"])
            for kc in range(8):
                mm(pA[0:1, 0:16], shbf[:, kc, m:m + 1], w1m[:, kc, ODF:ODF + 16], kc == 0, kc == 7, ["w1m", "shbf"], ["pA"])
            dve(lambda e, m=m: e.tensor_tensor(out=c1rowf[:, m, 0:16].rearrange("o (a b) -> o a b", b=8),
                                               in0=pA[0:1, 0:16].rearrange("o (a b) -> o a b", b=8), in1=dtbrow[:, :, :], op=ALU.add),
                ["pA", "dtbrow"], ["c1rowf"])
            dve(lambda e, m=m: e.tensor_copy(out=dthi[:, m, :, :], in_=c1rowf[:, m, 0:16].rearrange("o (a b) -> o a b", b=8)),
                ["c1rowf"], ["dthi"])
            dve(lambda e, m=m: e.tensor_tensor(out=c1rowf[:, m, 16:32].rearrange("o (a b) -> o a b", b=8),
                                               in0=c1rowf[:, m, 0:16].rearrange("o (a b) -> o a b", b=8), in1=dthi[:, m, :, :], op=ALU.subtract),
                ["c1rowf", "dthi"], ["c1rowf2"])
            dve(lambda e, m=m: e.tensor_copy(out=dtlo[:, m, :, :], in_=c1rowf[:, m, 16:32].rearrange("o (a b) -> o a b", b=8)),
                ["c1rowf2"], ["dtlo"])
        def set_mod(m, reload):
            if reload:
                dma("pool", w1m[:, :, :], w1_d.ap().rearrange("(c p) n -> p c n", p=128), DS("w1"), [], ["w1m"])
            for kc in range(8):
                act(lambda e, m=m, kc=kc: e.activation(out=w1m[:, kc, :], in_=w1m[:, kc, :], func=AF.Identity, scale=wmod[:, kc, m:m + 1]),
                    ["w1m", "wmod"], ["w1m"])
        flush()
        es_s.close()
        cur[0] = es1
        STOP = dbg.get("_stop", (99,))[0] if dbg else 99
        xnT = sb("xnT", [128, 8, 512], BF16)
        qT = [sb("qT%d" % i_, [128, 512], BF16) for i_ in range(2)]
        kT = [sb("kT%d" % i_, [128, 512], BF16) for i_ in range(2)]
        lrT = [sb("lrT%d" % i_, [16, 512], BF16) for i_ in range(2)]
        upad = sb("upad", [128, 6, 8 * 67 + 8], BF16)
        xcT = [sb("xcT%d" % i_, [128, 4, 512], BF16) for i_ in range(2)]
        BT = [sb("BT%d" % i_, [128, 512], BF16) for i_ in range(2)]
        CT = [sb("CT%d" % i_, [128, 512], BF16) for i_ in range(2)]
        kv = [sb("kv%d" % i_, [128, 4, 384], BF16) for i_ in range(2)]
        r_s = [sb("r_s%d" % i_, [128, 4, 256], BF16) for i_ in range(2)]
        z_s = [sb("z_s%d" % i_, [128, 4, 512], BF16) for i_ in range(2)]
        dtr = [sb("dtr%d" % i_, [128, 4, 8], F32) for i_ in range(2)]
        e1 = sb("e1", [128, 128], F32)
        sp = sb("sp", [128, 128], BF16)
        Eq = sb("Eq", [128, 128], F32)
        Ek = sb("Ek", [128, 128], F32)
        Er = sb("Er", [128, 128], F32)
        dcol = sb("dcol", [128, 1], F32)
        qt_ = sb("qt_", [128, 128], BF16)
        kt_ = sb("kt_", [128, 128], BF16)
        kh_ = sb("kh_", [128, 128], BF16)
        attm = sb("attm", [128, 128], BF16)
        Sg = sb("Sg", [128, 256], F32)
        Sgb = sb("Sgb", [128, 256], BF16)
        e2 = sb("e2", [128, 8], F32)
        dtv = sb("dtv", [128, 8], F32)
        ldt = sb("ldt", [128, 8], F32)
        abf = sb("abf", [128, 8], BF16)
        nb = sb("nb", [128, 8], F32)
        E3 = sb("E3", [128, 3, 8], F32)
        gsc = sb("gsc", [128, 8], F32)
        Wt = sb("Wt", [128, 8, 128], F32)
        cbm = sb("cbm", [128, 128], F32)
        MT = sb("MT", [128, 8, 128], BF16)
        xtm = sb("xtm", [128, 512], BF16)
        xh = sb("xh", [128, 512], BF16)
        Btm = sb("Btm", [128, 128], BF16)
        t1 = sb("t1", [128, 512], F32)
        Ss = sb("Ss", [128, 512], F32)
        Ssb = sb("Ssb", [128, 512], BF16)
        outp = [sb("outp%d" % i, [128, 768], F32) for i in range(2)]
        prevt = [sb("prevt%d" % i, [128, 768], F32) for i in range(2)]
        og = sb("og", [128, 256], F32)
        ys = sb("ys", [128, 512], F32)
        xd = sb("xd", [128, 512], F32)
        oa = sb("oa", [128, 768], BF16)
        nrm = sb("nrm", [128, 4], F32)
        ext = [sb("ext%d" % i, [128, 6, 512], BF16) for i in range(2)]
        dve(lambda e: e.memset(upad[:, :, :], 0.0), [], ["upad"])
        state = {"xi": 0, "oi": 0, "pi": 0, "ei": 0}

        def norm_tile(src_ap, dst_T, col0, nsub_key):
            i = state["xi"] % 2
            state["xi"] += 1
            dma("sp", xt[i][:, :], src_ap, DS("xt%d" % i), [], ["xt%d" % i])
            activation(sq_junk[:, :], xt[i][:, :], AF.Square, ["xt%d" % i], ["sq_junk", "ss"], accum_out=ss[:, 0:1])
            activation(ss[:, 1:2], ss[:, 0:1], AF.Ln, ["ss", "epsc"], ["ss"], scale=1.0 / D, bias=epsc[:, 0:1])
            activation(ss[:, 2:3], ss[:, 1:2], AF.Exp, ["ss"], ["ss"], scale=-0.5)
            act(lambda e, i=i: e.activation(out=xs_[i][:, :], in_=xt[i][:, :], func=AF.Identity, scale=ss[:, 2:3]),
                 ["xt%d" % i, "ss"], ["xs%d" % i])
            for kc in range(8):
                tp(pT[:, kc, :], xs_[i][:, kc * 128:(kc + 1) * 128], ["xs%d" % i], ["pT"])
            act(lambda e: e.copy(out=dst_T[:, :, col0:col0 + 128], in_=pT[:, :, :]), ["pT"], [nsub_key])

        dve(lambda e: e.memset(epsc[:, 0:1], EPS), [], ["epsc"])
        dve(lambda e: e.memset(epsc[:, 1:2], 1.0), [], ["epsc"])

        def project_block(m, dir_, ntok, rowlen, need_rz, bset):
            W = w1m
            wk = "w1m"
            nrow = ntok // rowlen
            banks = [pA, pTf]
            bk = [0]

            def nb_():
                b = banks[bk[0] % 2]
                k = "pA" if bk[0] % 2 == 0 else "pT"
                bk[0] += 1
                return b, k
            for gi_, off in enumerate(FM_OFFS):
                b, k = nb_()
                for kc in range(8):
                    mm(b[:, 0:ntok], W[:, kc, off:off + 128], xnT[:, kc, 0:ntok], kc == 0, kc == 7, [wk, "xnT"], [k])
                if gi_ < 2:
                    dst = (qT[bset] if gi_ == 0 else kT[bset])
                    activation(dst[:, 0:ntok], b[:, 0:ntok], AF.Identity, [k, "c1col"], ["qT%d" % bset if gi_ == 0 else "kT%d" % bset],
                               bias=c1col[:, gi_, m:m + 1])
                else:
                    cc = gi_ - 2
                    if rowlen == 64:
                        dst = upad[:, cc, 0:nrow * 67].rearrange("p (r c) -> p r c", c=67)[:, :, 2:66]
                        src = b[:, 0:ntok].rearrange("p (r c) -> p r c", c=64)
                    else:
                        dst = upad[:, cc, 2:2 + ntok]
                        src = b[:, 0:ntok]
                    activation(dst, src, AF.Identity, [k, "c1col"], ["upad"], bias=c1col[:, gi_, m:m + 1])
                yield
            off = OLF if dir_ == 0 else OLB
            b, k = nb_()
            for kc in range(8):
                mm(b[0:16, 0:ntok], W[:, kc, off:off + 16], xnT[:, kc, 0:ntok], kc == 0, kc == 7, [wk, "xnT"], [k])
            activation(lrT[bset][:, 0:ntok], b[0:16, 0:ntok], AF.Identity, [k, "c1lr"], ["lrT%d" % bset], bias=c1lr[:, m, dir_:dir_ + 1])
            yield
            for cc in range(6):
                b, k = nb_()
                for j in range(4):
                    if rowlen == 64:
                        rhs = upad[:, cc, 0:nrow * 67].rearrange("p (r c) -> p r c", c=67)[:, :, j:j + 64]
                        o_ = b[:, 0:ntok].rearrange("p (r c) -> p r c", c=64)
                    else:
                        rhs = upad[:, cc, j:j + ntok]
                        o_ = b[:, 0:ntok]
                    mm(o_, cdiag[:, cc * 4 + j, :], rhs, j == 0, j == 3, ["cdiag", "upad"], [k])
                dst = xcT[bset][:, cc, 0:ntok] if cc < 4 else (BT[bset][:, 0:ntok] if cc == 4 else CT[bset][:, 0:ntok])
                dk = "xcT%d" % bset if cc < 4 else ("BT%d" % bset if cc == 4 else "CT%d" % bset)
                activation(dst, b[:, 0:ntok], AF.Silu, [k, "cbcol"], [dk], bias=cbcol[:, cc:cc + 1])
                yield
            for s_ in range(ntok // 128):
                tsl = slice(s_ * 128, (s_ + 1) * 128)
                groups = [(OK_, 384, 0, "kv%d" % bset)]
                if need_rz:
                    groups += [(OR, 256, 384, "r"), (OZ, 512, 640, "z")]
                for (o0, n0, c0, kind) in groups:
                    b, k = nb_()
                    for kc in range(8):
                        mm(b[:, 0:n0], xnT[:, kc, tsl], W[:, kc, o0:o0 + n0], kc == 0, False, [wk, "xnT"], [k])
                    mm(b[:, 0:n0], onesrow[:, 0:128], c1row[:, m, c0:c0 + n0], False, True, ["onesrow", "c1row"], [k])
                    if kind == "kv%d" % bset:
                        act(lambda e, b=b, s_=s_: e.copy(out=kv[bset][:, s_, :], in_=b[:, 0:384]), [k], ["kv%d" % bset])
                    elif kind == "r":
                        activation(r_s[bset][:, s_, :], b[:, 0:256], AF.Silu, [k], ["r_s%d" % bset])
                    else:
                        activation(z_s[bset][:, s_, :], b[:, 0:512], AF.Silu, [k], ["z_s%d" % bset])
                    yield
                b, k = nb_()
                od = ODF if dir_ == 0 else ODB
                for kc in range(8):
                    mm(b[:, 0:8], xnT[:, kc, tsl], W[:, kc, od:od + 8], kc == 0, False, [wk, "xnT"], [k])
                mm(b[:, 0:8], onesrow[:, 0:128], dthi[:, m, dir_, :], False, False, ["onesrow", "dthi"], [k])
                mm(b[:, 0:8], onesrow[:, 0:128], dtlo[:, m, dir_, :], False, True, ["onesrow", "dtlo"], [k])
                dve(lambda e, b=b, s_=s_: e.tensor_copy(out=dtr[bset][:, s_, :], in_=b[:, 0:8]), [k], ["dtr%d" % bset])
                yield

        def gla_chunk(c, dir_, bset):
            tsl = slice(c * 128, (c + 1) * 128)
            M16 = cst[:, MF16 if dir_ == 0 else MB16, :]
            S16 = cst[:, SF16 if dir_ == 0 else SB16, :]
            MK = cst[:, MF if dir_ == 0 else MB, :]
            last = 127 if dir_ == 0 else 0
            mm(pG[:, 0, :], lrT[bset][:, tsl], upsb[:, dir_, :], True, False, ["lrT%d" % bset, "upsb"], ["pG0"])
            mm(pG[:, 0, :], onesrow[:, 0:128], gbrow[:, dir_, :], False, True, ["onesrow", "gbrow"], ["pG0"])
            yield
            activation(e1[:, :], pG[:, 0, :], AF.Exp, ["pG0"], ["e1"], scale=-1.0)
            activation(sp[:, :], e1[:, :], AF.Ln, ["e1", "epsc"], ["sp"], bias=epsc[:, 1:2])
            yield
            mm(pG[:, 1, :], sp[:, :], M16, True, True, ["sp", "cst"], ["pG1"])
            mm(pG[:, 2, :], S16, sp[:, :], True, True, ["sp", "cst"], ["pG2"])
            yield
            activation(Eq[:, :], pG[:, 1, :], AF.Exp, ["pG1"], ["Eq"])
            activation(Ek[:, :], pG[:, 1, :], AF.Exp, ["pG1"], ["Ek"], scale=-1.0)
            activation(Er[:, :], pG[:, 2, :], AF.Exp, ["pG2"], ["Er"])
            activation(dcol[:, :], pG[:, 1, last:last + 1], AF.Exp, ["pG1"], ["dcol"])
            yield
            dve(lambda e: e.scalar_tensor_tensor(out=qt_[:, :], in0=qT[bset][:, tsl], scalar=128.0 ** -0.5, in1=Eq[:, :], op0=ALU.mult, op1=ALU.mult),
                ["qT%d" % bset, "Eq"], ["qt_"])
            dve(lambda e: e.tensor_tensor(out=kt_[:, :], in0=kT[bset][:, tsl], in1=Ek[:, :], op=ALU.mult), ["kT%d" % bset, "Ek"], ["kt_"])
            dve(lambda e: e.tensor_tensor(out=kh_[:, :], in0=kv[bset][:, c, 0:128], in1=Er[:, :], op=ALU.mult), ["kv%d" % bset, "Er"], ["kh_"])
            yield
            mm(pG[:, 3, :], kt_[:, :], qt_[:, :], True, True, ["kt_", "qt_"], ["pG3"])
            yield
            dve(lambda e: e.tensor_tensor(out=attm[:, :], in0=pG[:, 3, :], in1=MK, op=ALU.mult), ["pG3", "cst"], ["attm"])
            yield
            mm(pO[:, 0, :], attm[:, :], kv[bset][:, c, 128:384], True, False, ["attm", "kv%d" % bset], ["pO0"])
            mm(pO[:, 0, :], qt_[:, :], Sgb[:, :], False, True, ["qt_", "Sgb"], ["pO0"])
            mm(pO[:, 1, :], kh_[:, :], kv[bset][:, c, 128:384], True, True, ["kh_", "kv%d" % bset], ["pO1"])
            yield
            dve(lambda e: e.scalar_tensor_tensor(out=Sg[:, :], in0=Sg[:, :], scalar=dcol[:, 0:1], in1=pO[:, 1, :], op0=ALU.mult, op1=ALU.add),
                ["Sg", "dcol", "pO1"], ["Sg"])
            yield
            act(lambda e: e.copy(out=Sgb[:, :], in_=Sg[:, :]), ["Sg"], ["Sgb"])
            yield

        pSf = pS
        pSb = None

        def ssd_chunk(c, dir_, bset):
            tsl = slice(c * 128, (c + 1) * 128)
            MK = cst[:, MF if dir_ == 0 else MB, :]
            SK = cst[:, SF if dir_ == 0 else SB_, :]
            activation(e2[:, :], dtr[bset][:, c, :], AF.Exp, ["dtr%d" % bset], ["e2"])
            activation(dtv[:, :], e2[:, :], AF.Ln, ["e2", "epsc"], ["dtv"], bias=epsc[:, 1:2])
            activation(ldt[:, :], dtv[:, :], AF.Ln, ["dtv"], ["ldt"])
            yield
            dve(lambda e: e.tensor_tensor(out=abf[:, :], in0=dtv[:, :], in1=negA[:, dir_, :], op=ALU.mult), ["dtv", "negA"], ["abf"])
            yield
            for h in range(4):
                mm(pC[:, h, :], abf[:, h:h + 1].broadcast_to([128, 128]), MK, True, True, ["abf", "cst"], ["pC"])
            yield
            mm(pS[:, 128:136], MK, abf[:, :], True, True, ["abf", "cst"], ["pSs"])
            mm(pS[:, 136:144], SK, abf[:, :], True, True, ["abf", "cst"], ["pSs"])
            mm(pS[:, 144:152], cst[:, ONES, :], abf[:, :], True, True, ["abf", "cst"], ["pSs"])
            yield
            dve(lambda e: e.tensor_tensor(out=nb[:, :], in0=ldt[:, :], in1=pS[:, 128:136], op=ALU.subtract), ["ldt", "pSs"], ["nb"])
            yield
            activation(E3[:, :, :], pS[:, 128:152].rearrange("p (a b) -> p a b", b=8), AF.Exp, ["pSs"], ["E3"])
            yield
            dve(lambda e: e.tensor_tensor(out=gsc[:, :], in0=E3[:, 1, :], in1=dtv[:, :], op=ALU.mult), ["E3", "dtv"], ["gsc"])
            yield
            for h in range(4):
                activation(Wt[:, h, :], pC[:, h, :], AF.Exp, ["pC", "nb"], ["Wt"], bias=nb[:, h:h + 1])
            yield
            for h in range(4, 8):
                mm(pC[:, h - 4, :], abf[:, h:h + 1].broadcast_to([128, 128]), MK, True, True, ["abf", "cst"], ["pC"])
            yield
            for h in range(4, 8):
                activation(Wt[:, h, :], pC[:, h - 4, :], AF.Exp, ["pC", "nb"], ["Wt"], bias=nb[:, h:h + 1])
            yield
            mm(pS[:, 0:128], BT[bset][:, tsl], CT[bset][:, tsl], True, True, ["BT%d" % bset, "CT%d" % bset], ["pSc"])
            yield
            dve(lambda e: e.tensor_tensor(out=cbm[:, :], in0=pS[:, 0:128], in1=MK, op=ALU.mult), ["pSc", "cst"], ["cbm"])
            dve(lambda e: e.scalar_tensor_tensor(out=MT[:, :, :], in0=Wt[:, :, :], scalar=BIG,
                                                 in1=cbm[:, :].unsqueeze(1).broadcast_to([128, 8, 128]), op0=ALU.min, op1=ALU.mult),
                ["Wt", "cbm"], ["MT"])
            yield
            for cc in range(4):
                tp(pSx[:, cc * 128:(cc + 1) * 128], xcT[bset][:, cc, tsl], ["xcT%d" % bset], ["pS"])
            yield
            tp(pSx[:, 512:640], BT[bset][:, tsl], ["BT%d" % bset], ["pS"])
            yield
            act(lambda e: e.copy(out=xtm[:, :], in_=pSx[:, 0:512]), ["pS"], ["xtm"])
            yield
            dve(lambda e: e.tensor_tensor(out=xh[:, :].rearrange("p (h q) -> p h q", q=64),
                                          in0=pSx[:, 0:512].rearrange("p (h q) -> p h q", q=64),
                                          in1=gsc[:, :].unsqueeze(2).broadcast_to([128, 8, 64]), op=ALU.mult), ["pS", "gsc"], ["xh"])
            yield
            act(lambda e: e.copy(out=Btm[:, :], in_=pSx[:, 512:640]), ["pS"], ["Btm"])
            yield
            for h in range(8):
                mm(pY[:, h * 64:(h + 1) * 64], MT[:, h, :], xtm[:, h * 64:(h + 1) * 64], True, True, ["MT", "xtm"], ["pY"])
            yield
            mm(pB[:, :], CT[bset][:, tsl], Ssb[:, :], True, True, ["CT%d" % bset, "Ssb"], ["pB"])
            yield
            dve(lambda e: e.tensor_tensor(out=t1[:, :].rearrange("p (h q) -> p h q", q=64),
                                          in0=pB[:, :].rearrange("p (h q) -> p h q", q=64),
                                          in1=E3[:, 0, :].unsqueeze(2).broadcast_to([128, 8, 64]), op=ALU.mult), ["pB", "E3"], ["t1"])
            yield
            mm(pB[:, :], Btm[:, :], xh[:, :], True, True, ["Btm", "xh"], ["pB"])
            yield
            dve(lambda e: e.tensor_tensor(out=Ss[:, :].rearrange("p (h q) -> p h q", q=64),
                                           in0=Ss[:, :].rearrange("p (h q) -> p h q", q=64),
                                           in1=E3[:, 2, :].unsqueeze(2).broadcast_to([128, 8, 64]), op=ALU.mult), ["Ss", "E3"], ["Ss"])
            dve(lambda e: e.tensor_tensor(out=Ss[:, :], in0=Ss[:, :], in1=pB[:, :], op=ALU.add), ["Ss", "pB"], ["Ss"])
            yield
            act(lambda e: e.copy(out=Ssb[:, :], in_=Ss[:, :]), ["Ss"], ["Ssb"])

        def interleave(*gens):
            gens = list(gens)
            while gens:
                for g in list(gens):
                    try:
                        next(g)
                    except StopIteration:
                        gens.remove(g)

        def out_pass1(c, tok0):
            i = state["oi"] % 2
            state["oi"] += 1
            act(lambda e: e.copy(out=outp[i][:, 0:256], in_=pO[:, 0, :]), ["pO0"], ["outp%d" % i])
            dve(lambda e: e.tensor_tensor(out=outp[i][:, 256:768], in0=t1[:, :], in1=pY[:, :], op=ALU.add), ["t1", "pY"], ["outp%d" % i])
            dma("sp", prev_d[tok0:tok0 + 128, :], outp[i][:, :], DS("outp%d" % i), ["outp%d" % i], [("prev", tok0)])

        def out_pass2(c, tok0, st_idx, bset):
            i = state["pi"] % 2
            state["pi"] += 1
            ei = st_idx % 2
            dma("sp", prevt[i][:, :], prev_d[tok0:tok0 + 128, :], DS("prevt%d" % i), [("prev", tok0)], ["prevt%d" % i])
            dve(lambda e: e.tensor_tensor(out=og[:, :], in0=pO[:, 0, :], in1=prevt[i][:, 0:256], op=ALU.add), ["pO0", "prevt%d" % i], ["og"])
            activation(sq_junk[:, 0:256], og[:, :], AF.Square, ["og"], ["sq_junk", "nrm"], accum_out=nrm[:, 0:1])
            activation(nrm[:, 1:2], nrm[:, 0:1], AF.Ln, ["nrm", "epsc"], ["nrm"], scale=1.0 / 256, bias=epsc[:, 0:1])
            activation(nrm[:, 1:2], nrm[:, 1:2], AF.Exp, ["nrm"], ["nrm"], scale=-0.5)
            dve(lambda e: e.scalar_tensor_tensor(out=og[:, :], in0=og[:, :], scalar=nrm[:, 1:2], in1=gnw_bc[:, :], op0=ALU.mult, op1=ALU.mult),
                ["og", "nrm", "gnw_bc"], ["og"])
            dve(lambda e: e.tensor_tensor(out=oa[:, 0:256], in0=og[:, :], in1=r_s[bset][:, c, :], op=ALU.mult), ["og", "r_s%d" % bset], ["oa"])
            dve(lambda e: e.tensor_tensor(out=xd[:, :].rearrange("p (h q) -> p h q", q=64),
                                           in0=xtm[:, :].rearrange("p (h q) -> p h q", q=64),
                                           in1=dsk_bc[:, :].unsqueeze(2).broadcast_to([128, 8, 64]), op=ALU.mult), ["xtm", "dsk_bc"], ["xd"])
            dve(lambda e: e.tensor_tensor(out=xd[:, :], in0=xd[:, :], in1=prevt[i][:, 256:768], op=ALU.add), ["xd", "prevt%d" % i], ["xd"])
            dve(lambda e: e.tensor_tensor(out=ys[:, :], in0=t1[:, :], in1=pY[:, :], op=ALU.add), ["t1", "pY"], ["ys"])
            dve(lambda e: e.tensor_tensor(out=ys[:, :], in0=ys[:, :], in1=xd[:, :], op=ALU.add), ["ys", "xd"], ["ys"])
            dve(lambda e: e.tensor_tensor(out=ys[:, :], in0=ys[:, :], in1=z_s[bset][:, c, :], op=ALU.mult), ["ys", "z_s%d" % bset], ["ys"])
            activation(sq_junk[:, 0:512], ys[:, :], AF.Square, ["ys"], ["sq_junk", "nrm"], accum_out=nrm[:, 2:3])
            activation(nrm[:, 3:4], nrm[:, 2:3], AF.Ln, ["nrm", "epsc"], ["nrm"], scale=1.0 / 512, bias=epsc[:, 0:1])
            activation(nrm[:, 3:4], nrm[:, 3:4], AF.Exp, ["nrm"], ["nrm"], scale=-0.5)
            dve(lambda e: e.scalar_tensor_tensor(out=oa[:, 256:768], in0=ys[:, :], scalar=nrm[:, 3:4], in1=snw_bc[:, :], op0=ALU.mult, op1=ALU.mult),
                ["ys", "nrm", "snw_bc"], ["oa"])
            if tok0 // 512 == 11:
                tap("oa_bf", oa[:, :], ["oa"], idx=c)
                tap("ogys", og[:, :], ["og"], idx=(c, slice(None), slice(0, 256)))
            for cc in range(6):
                tp(pCb[:, cc * 128:(cc + 1) * 128], oa[:, cc * 128:(cc + 1) * 128], ["oa"], ["pC"])
            act(lambda e: e.copy(out=ext[ei][:, :, c * 128:(c + 1) * 128], in_=pCb[:, 0:768].rearrange("p (a b) -> p a b", b=128)), ["pC"], ["ext%d" % ei])

        Sg0s = sb("Sg0s", [128, 256], F32)
        Ss0s = sb("Ss0s", [128, 512], F32)

        def drain(g):
            if g is None:
                return
            for _ in g:
                pass

        def proj_gen(m, dir_, rows_fn, ntiles, rowlen, need_rz, bset):
            for s_ in range(ntiles):
                norm_tile(rows_fn(s_), xnT, s_ * 128, "xnT")
                yield
            yield from project_block(m, dir_, ntiles * 128, rowlen, need_rz, bset)

        def zero_states():
            dve(lambda e: e.memset(Sg[:, :], 0.0), ["Sg"], ["Sg"])
            dve(lambda e: e.memset(Sgb[:, :], 0.0), ["Sgb"], ["Sgb"])
            dve(lambda e: e.memset(Ss[:, :], 0.0), ["Ss"], ["Ss"])
            dve(lambda e: e.memset(Ssb[:, :], 0.0), ["Ssb"], ["Ssb"])

        def run_ctx(dir_):
            zero_states()
            drain(proj_gen(1, dir_, lambda s_: ctx_d[s_ * 128:(s_ + 1) * 128, :], 2, 256, False, 0))
            for c in ([0, 1] if dir_ == 0 else [1, 0]):
                interleave(gla_chunk(c, dir_, 0), ssd_chunk(c, dir_, 0))
            tap("Sg_ctx%d" % dir_, Sg[:, :], ["Sg"])
            tap("Ss_ctx%d" % dir_, Ss[:, :], ["Ss"])

        def run_latent(dir_):
            sts = list(range(NST_TOTAL)) if dir_ == 0 else list(range(NST_TOTAL - 1, -1, -1))
            sts = sts[:NST_RUN]
            rows = lambda st: (lambda s_: x_d[st * 512 + s_ * 128:st * 512 + (s_ + 1) * 128, :])
            bset = 0
            drain(proj_gen(0, dir_, rows(sts[0]), 4, 64, dir_ == 0, bset))
            for i, st in enumerate(sts):
                tok0 = st * 512
                bg = proj_gen(0, dir_, rows(sts[i + 1]), 4, 64, dir_ == 0, 1 - bset) if i + 1 < len(sts) else None
                for c in ([0, 1, 2, 3] if dir_ == 0 else [3, 2, 1, 0]):
                    gens = [gla_chunk(c, dir_, bset), ssd_chunk(c, dir_, bset)]
                    while gens:
                        for g in list(gens):
                            try:
                                next(g)
                            except StopIteration:
                                gens.remove(g)
                        if bg is not None:
                            try:
                                next(bg)
                            except StopIteration:
                                bg = None
                    if dir_ == 1:
                        out_pass1(c, tok0 + c * 128)
                    else:
                        out_pass2(c, tok0 + c * 128, st, bset)
                drain(bg)
                if dir_ == 0:
                    ei = st % 2
                    dma("sp", ex_d[st].rearrange("(c p) t -> p c t", p=128), ext[ei][:, :, :], DS("ext%d" % ei),
                        ["ext%d" % ei], [("ex", st)])
                    if STOP >= 3:
                        P.cc("pool", lambda e, st=st: e.collective_compute(
                            "AllGather", ALU.bypass, replica_groups=[[0, 1, 2, 3], [4, 5, 6, 7]],
                            ins=[ex_d[st]], outs=[gx_d[st]]), ccs, [("ex", st)], ["gx"])
                bset = 1 - bset

        def run_all():
            set_mod(1, False)
            dve(lambda e: e.memset(upad[:, :, :], 0.0), ["upad"], ["upad"])
            run_ctx(0)
            dve(lambda e: e.tensor_copy(out=Sg0s[:, :], in_=Sg[:, :]), ["Sg"], ["Sg0s"])
            dve(lambda e: e.tensor_copy(out=Ss0s[:, :], in_=Ss[:, :]), ["Ss"], ["Ss0s"])
            run_ctx(1)
            set_mod(0, True)
            dve(lambda e: e.memset(upad[:, :, :], 0.0), ["upad"], ["upad"])
            if STOP >= 1:
                run_latent(1)
                tap("prev", prev_d[7680:8192, :], [("prev", 7680 + i * 128) for i in range(4)])
            if STOP >= 2:
                dve(lambda e: e.tensor_copy(out=Sg[:, :], in_=Sg0s[:, :]), ["Sg0s", "Sg"], ["Sg"])
                dve(lambda e: e.tensor_copy(out=Ss[:, :], in_=Ss0s[:, :]), ["Ss0s", "Ss"], ["Ss"])
                dve(lambda e: e.tensor_copy(out=Sgb[:, :], in_=Sg0s[:, :]), ["Sg0s", "Sgb"], ["Sgb"])
                dve(lambda e: e.tensor_copy(out=Ssb[:, :], in_=Ss0s[:, :]), ["Ss0s", "Ssb"], ["Ssb"])
                run_latent(0)

        NST_TOTAL = SEQ // 512
        NST_RUN = NST_TOTAL if not dbg else dbg.get("_nst", (NST_TOTAL,))[0]
        tap("ada_col", ada_col[:, :, :].rearrange("p a b -> p (a b)"), ["ada_col"])
        run_all()

        flush()
        es1.close()
        if STOP < 4:
            return nc
        es2 = ExitStack()
        es2.__enter__()
        cur[0] = es2
        wgs = sb("wgs", [128, 8, 2 * D], BF16)
        wpa = sb("wpa", [128, 8, D], BF16)
        wpb = sb("wpb", [128, 16, D], BF16)
        wo = sb("wo", [128, 8, D], BF16)
        cgc = sb("cgc", [128, 16], F32)
        xnT2 = sb("xnT2", [128, 8, 512], BF16)
        oaT = sb("oaT", [128, 8, 512], BF16)
        obT = sb("obT", [128, 16, 512], BF16)
        gT = sb("gT", [128, 16, 512], BF16)
        mg = sb("mg", [128, 8, 512], BF16)
        tA = sb("tA", [128, 512], F32)
        tB = sb("tB", [128, 512], F32)
        hrow = xt
        xrow = sb("xrow", [128, 4, D], F32)

        dma("pool", wgs[:, :, :], wg_d.ap().rearrange("(c p) n -> p c n", p=128), DS("wgs"), [], ["wgs"])
        dma("pool", wpa[:, :, :], wpa_d.ap().rearrange("(c p) n -> p c n", p=128), DS("wpa"), [], ["wpa"])
        dma("pool", wpb[:, :, :], wpb_d.ap().rearrange("(c p) n -> p c n", p=128), DS("wpb"), [], ["wpb"])
        dma("pool", wo[:, :, :], wout_d.ap().rearrange("(c p) n -> p c n", p=128), DS("wo"), [], ["wo"])
        for fc in range(16):
            for kc in range(8):
                mm(pG[:, 0, fc:fc + 1], wgs[:, kc, fc * 128:(fc + 1) * 128], shbf[:, kc, 0:1], kc == 0, kc == 7, ["wgs", "shbf"], ["pG"])
        dve(lambda e: e.tensor_copy(out=cgc[:, :], in_=pG[:, 0, 0:16]), ["pG"], ["cgc"])
        for kc in range(8):
            act(lambda e, kc=kc: e.activation(out=wgs[:, kc, :], in_=wgs[:, kc, :], func=AF.Identity, scale=wmod[:, kc, 0:1]),
                 ["wgs", "wmod", "cgc"], ["wgs"])

        def norm_rows(src_tile, key, dst_T, col0, dkey):
            activation(sq_junk[:, :], src_tile, AF.Square, [key], ["sq_junk", "ss"], accum_out=ss[:, 0:1])
            activation(ss[:, 1:2], ss[:, 0:1], AF.Ln, ["ss", "epsc"], ["ss"], scale=1.0 / D, bias=epsc[:, 0:1])
            activation(ss[:, 2:3], ss[:, 1:2], AF.Exp, ["ss"], ["ss"], scale=-0.5)
            act(lambda e: e.activation(out=xs_[0][:, :], in_=src_tile, func=AF.Identity, scale=ss[:, 2:3]), [key, "ss"], ["xs0"])
            for kc in range(8):
                tp(pT[:, kc, :], xs_[0][:, kc * 128:(kc + 1) * 128], ["xs0"], ["pT"])
            act(lambda e: e.copy(out=dst_T[:, :, col0:col0 + 128], in_=pT[:, :, :]), ["pT"], [dkey])

        for s4 in range(4):
            def fn_q(e, s4=s4):
                q = nc.partition_id([mybir.EngineType.SP]) % 4
                return e.dma_start(out=gxq_d[s4], in_=gx_d[q * 4 + s4])
            P.dma("sp", fn_q, DS("gxq"), ["gx"], ["gxq"])
        for blk in range(4):
            t0 = blk * 512
            for s_ in range(4):
                dma("sp", xrow[:, s_, :], xq_d[t0 + s_ * 128:t0 + (s_ + 1) * 128, :], DS("xrow"), [], ["xrow"])
            for s_ in range(4):
                norm_rows(xrow[:, s_, :], "xrow", xnT2, s_ * 128, "xnT2")
            for r in range(4):
                dma("sp", oaT[:, 2 * r:2 * r + 2, :], gxq_d[blk, r * 768:r * 768 + 256, :].rearrange("(c p) t -> p c t", p=128),
                    DS("oaT"), ["gxq"], ["oaT"])
                dma("sp", obT[:, 4 * r:4 * r + 4, :], gxq_d[blk, r * 768 + 256:r * 768 + 768, :].rearrange("(c p) t -> p c t", p=128),
                    DS("obT"), ["gxq"], ["obT"])
            if blk == 3:
                tap("oaT_bf", oaT[:, :, :], ["oaT"])
                tap("obT_bf", obT[:, :, :], ["obT"])
                tap("xnT2_bf", xnT2[:, :, :], ["xnT2"])
            for fc in range(16):
                b, k = (pA, "pA") if fc % 2 == 0 else (pB, "pB")
                for kc in range(8):
                    mm(b[:, :], wgs[:, kc, fc * 128:(fc + 1) * 128], xnT2[:, kc, :], kc == 0, kc == 7, ["wgs", "xnT2"], [k])
                activation(gT[:, fc, :], b[:, :], AF.Sigmoid, [k, "cgc"], ["gT"], bias=cgc[:, fc:fc + 1])
            for mc in range(8):
                for kc in range(8):
                    mm(pA[:, :], wpa[:, kc, mc * 128:(mc + 1) * 128], oaT[:, kc, :], kc == 0, kc == 7, ["wpa", "oaT"], ["pA"])
                for kc in range(16):
                    mm(pB[:, :], wpb[:, kc, mc * 128:(mc + 1) * 128], obT[:, kc, :], kc == 0, kc == 15, ["wpb", "obT"], ["pB"])
                dve(lambda e, mc=mc: e.tensor_tensor(out=tA[:, :], in0=pA[:, :], in1=gT[:, mc, :], op=ALU.mult), ["pA", "gT"], ["tA"])
                dve(lambda e, mc=mc: e.tensor_tensor(out=tB[:, :], in0=pB[:, :], in1=gT[:, 8 + mc, :], op=ALU.mult), ["pB", "gT"], ["tB"])
                dve(lambda e, mc=mc: e.tensor_tensor(out=mg[:, mc, :], in0=tA[:, :], in1=tB[:, :], op=ALU.add), ["tA", "tB"], ["mg"])
            if blk == 3:
                tap("gT_bf", gT[:, :, :], ["gT"])
                tap("mg_bf", mg[:, :, :], ["mg"])
            for s_ in range(4):
                hi = (blk * 4 + s_) % 2
                for half in range(2):
                    b, k = (pA, "pA") if half == 0 else (pB, "pB")
                    for kc in range(8):
                        mm(b[:, :], mg[:, kc, s_ * 128:(s_ + 1) * 128], wo[:, kc, half * 512:(half + 1) * 512], kc == 0, kc == 7, ["mg", "wo"], [k])
                    dve(lambda e, b=b, half=half, hi=hi: e.tensor_tensor(out=hrow[hi][:, half * 512:(half + 1) * 512], in0=b[:, :],
                                                                        in1=g_bc[:, 0, half * 512:(half + 1) * 512], op=ALU.mult),
                        [k, "g_bc"], ["xt%d" % hi])
                    dve(lambda e, half=half, hi=hi, s_=s_: e.tensor_tensor(out=hrow[hi][:, half * 512:(half + 1) * 512],
                                                                           in0=hrow[hi][:, half * 512:(half + 1) * 512],
                                                                           in1=xrow[:, s_, half * 512:(half + 1) * 512], op=ALU.add),
                         ["xt%d" % hi, "xrow"], ["xt%d" % hi])
                dma("sp", h_d[t0 + s_ * 128:t0 + (s_ + 1) * 128, :], hrow[hi][:, :], DS("xt%d" % hi), ["xt%d" % hi], [("h", t0 + s_ * 128)])

        tap("hq", h_d[1536:2048, :], [("h", 1536 + i * 128) for i in range(4)])
        flush()
        es2.close()
        if STOP < 5:
            return nc
        es3 = ExitStack()
        es3.__enter__()
        cur[0] = es3
        xnT2 = sb("xnT2b", [128, 8, 256], BF16)
        xrow = sb("xrowb", [128, 2, D], F32)
        wga = sb("wga", [128, 8, DFF], BF16)
        wu = sb("wu", [128, 8, DFF], BF16)
        wd = sb("wd", [128, 22, D], BF16)
        c2c = sb("c2c", [128, 22, 2], F32)
        uT = sb("uT", [128, 22, 256], BF16)
        sgt = sb("sgt", [128, 256], F32)
        yrow = sb("yrow", [128, D], F32)
        dma("pool", wga[:, :, :], wgate_d.ap().rearrange("(c p) n -> p c n", p=128), DS("wga"), [], ["wga"])
        dma("pool", wu[:, :, :], wup_d.ap().rearrange("(c p) n -> p c n", p=128), DS("wu"), [], ["wu"])
        dma("pool", wd[:, :, :], wdown_d.ap().rearrange("(c p) n -> p c n", p=128), DS("wd"), [], ["wd"])
        for wi, (wt_, wk) in enumerate([(wga, "wga"), (wu, "wu")]):
            for fc in range(22):
                for kc in range(8):
                    mm(pG[:, 0, fc * 2 + wi:fc * 2 + wi + 1], wt_[:, kc, fc * 128:(fc + 1) * 128], shbf[:, kc, 2:3], kc == 0, kc == 7,
                       [wk, "shbf"], ["pG"])
        dve(lambda e: e.tensor_copy(out=c2c[:, :, :], in_=pG[:, 0, 0:44].rearrange("p (f w) -> p f w", w=2)), ["pG"], ["c2c"])
        for (wt_, wk) in [(wga, "wga"), (wu, "wu")]:
            for kc in range(8):
                act(lambda e, wt_=wt_, kc=kc: e.activation(out=wt_[:, kc, :], in_=wt_[:, kc, :], func=AF.Identity, scale=wmod[:, kc, 2:3]),
                     [wk, "wmod", "c2c"], [wk])
        for blk in range(8):
            t0 = blk * 256
            for s_ in range(2):
                dma("sp", xrow[:, s_, :], h_d[t0 + s_ * 128:t0 + (s_ + 1) * 128, :], DS("xrow"), [("h", t0 + s_ * 128)], ["xrow"])
            for s_ in range(2):
                norm_rows(xrow[:, s_, :], "xrow", xnT2, s_ * 128, "xnT2")
            if blk == 7:
                tap("xnB_bf", xnT2[:, :, :], ["xnT2"])
                tap("hB", xrow[:, :, :], ["xrow"])
            for fc in range(22):
                for kc in range(8):
                    mm(pA[:, 0:256], wga[:, kc, fc * 128:(fc + 1) * 128], xnT2[:, kc, :], kc == 0, kc == 7, ["wga", "xnT2"], ["pA"])
                for kc in range(8):
                    mm(pB[:, 0:256], wu[:, kc, fc * 128:(fc + 1) * 128], xnT2[:, kc, :], kc == 0, kc == 7, ["wu", "xnT2"], ["pB"])
                activation(sgt[:, :], pA[:, 0:256], AF.Silu, ["pA", "c2c"], ["sgt"], bias=c2c[:, fc, 0:1])
                dve(lambda e, fc=fc: e.scalar_tensor_tensor(out=uT[:, fc, :], in0=pB[:, 0:256], scalar=c2c[:, fc, 1:2], in1=sgt[:, :], op0=ALU.add, op1=ALU.mult),
                    ["pB", "c2c", "sgt"], ["uT"])
            if blk == 7:
                tap("uT_bf", uT[:, :, :], ["uT"])
            for s_ in range(2):
                for half in range(2):
                    b, k = (pA, "pA") if half == 0 else (pB, "pB")
                    for fc in range(22):
                        mm(b[:, :], uT[:, fc, s_ * 128:(s_ + 1) * 128], wd[:, fc, half * 512:(half + 1) * 512], fc == 0, fc == 21, ["uT", "wd"], [k])
                    dve(lambda e, b=b, half=half: e.tensor_tensor(out=yrow[:, half * 512:(half + 1) * 512], in0=b[:, :],
                                                                 in1=g_bc[:, 1, half * 512:(half + 1) * 512], op=ALU.mult), [k, "g_bc"], ["yrow"])
                dve(lambda e, s_=s_: e.tensor_tensor(out=yrow[:, :], in0=yrow[:, :], in1=xrow[:, s_, :], op=ALU.add), ["yrow", "xrow"], ["yrow"])
                activation(sq_junk[:, :], yrow[:, :], AF.Square, ["yrow"], ["sq_junk", "ss"], accum_out=ss[:, 4:5])
                activation(ss[:, 5:6], ss[:, 4:5], AF.Ln, ["ss", "epsc"], ["ss"], scale=1.0 / D, bias=epsc[:, 0:1])
                activation(ss[:, 6:7], ss[:, 5:6], AF.Exp, ["ss"], ["ss"], scale=-0.5)
                hi = (blk * 2 + s_) % 2
                dve(lambda e, hi=hi: e.scalar_tensor_tensor(out=hrow[hi][:, :], in0=yrow[:, :], scalar=ss[:, 6:7], in1=fnw_bc[:, :], op0=ALU.mult, op1=ALU.mult),
                    ["yrow", "ss", "fnw_bc"], ["xt%d" % hi])
                tok = dma("sp", out_d[t0 + s_ * 128:t0 + (s_ + 1) * 128, :], hrow[hi][:, :], DS("xt%d" % hi), ["xt%d" % hi], ["out"])

        flush()
        es3.close()
    return nc


def _consts():
    p = np.arange(128)[:, None]
    f = np.arange(128)[None, :]
    I = (p == f).astype(np.float32)
    Mf = (p <= f).astype(np.float32)
    Mb = (p >= f).astype(np.float32)
    Sf = (p > f).astype(np.float32)
    Sb = (p < f).astype(np.float32)
    ones = np.ones((128, 128), np.float32)
    zeros = np.zeros((128, 128), np.float32)
    mats = [I, Mf, Mb, Sf, Sb, -Mf / 16.0, -Mb / 16.0, -Sf / 16.0, -Sb / 16.0, ones, zeros]
    return np.ascontiguousarray(np.concatenate(mats, axis=1).astype(np.float32))


def kernel(x, c, ctx, c_ctx, w_ada, b_ada, norm1_w, w_in, gla_up_f, gla_bias_f, gla_up_b, gla_bias_b,
           gla_norm_w, conv_w, conv_b, dt_bias_f, dt_bias_b, a_log_f, a_log_b, d_skip, ssm_norm_w,
           w_pa, w_pb, w_out, norm2_w, w_gate, w_up, w_down, final_norm_w, _return_maps=False):
    f = lambda a: np.ascontiguousarray(np.asarray(a, dtype=np.float32))
    x, c, ctx, c_ctx, w_in = f(x), f(c), f(ctx), f(c_ctx), f(w_in)
    win = w_in[0]
    consts = _consts()
    in_maps = []
    for core in range(NCORE):
        b, j = core // 4, core % 4
        cols = np.concatenate([
            np.arange(j * 128, (j + 1) * 128), 512 + np.arange(j * 128, (j + 1) * 128),
            1024 + np.arange(j * 256, (j + 1) * 256), 2048 + np.arange(j * 256, (j + 1) * 256),
            np.arange(3072, 3104), 3104 + np.arange(j * 512, (j + 1) * 512), 5152 + np.arange(j * 512, (j + 1) * 512),
            7200 + np.arange(j * 128, (j + 1) * 128), 7712 + np.arange(j * 128, (j + 1) * 128),
            8224 + np.arange(j * 8, (j + 1) * 8), 8256 + np.arange(j * 8, (j + 1) * 8)])
        assert cols.size == NW1
        cch = np.concatenate([np.arange(j * 512, (j + 1) * 512), 2048 + np.arange(j * 128, (j + 1) * 128),
                              2560 + np.arange(j * 128, (j + 1) * 128)])
        hs = slice(j * 128, (j + 1) * 128)
        h8 = slice(j * 8, (j + 1) * 8)
        in_maps.append({
            "x": f(x[b]), "ctx": f(ctx[b]), "cvec": f(np.stack([c[b], c_ctx])),
            "w_ada": f(w_ada[0]), "b_ada": f(b_ada[0]), "norm1_w": f(norm1_w[0]),
            "w1": f(win[:, cols]),
            "gla_up": f(np.stack([np.asarray(gla_up_f)[0][:, hs], np.asarray(gla_up_b)[0][:, hs]])),
            "gla_bias": f(np.stack([np.asarray(gla_bias_f)[0][hs], np.asarray(gla_bias_b)[0][hs]])),
            "gla_norm_w": f(gla_norm_w[0]),
            "conv_w": f(np.asarray(conv_w)[0][:, cch]), "conv_b": f(np.asarray(conv_b)[0][cch]),
            "dt_bias": f(np.stack([np.asarray(dt_bias_f)[0][h8], np.asarray(dt_bias_b)[0][h8]])),
            "a_log": f(np.stack([np.asarray(a_log_f)[0][h8], np.asarray(a_log_b)[0][h8]])),
            "d_skip": f(np.asarray(d_skip)[0][h8]),
            "ssm_norm_w": f(np.asarray(ssm_norm_w)[0][j * 512:(j + 1) * 512]),
            "w_gates": f(win[:, 8288:10336]), "w_pa": f(w_pa[0]), "w_pb": f(w_pb[0]), "w_out": f(w_out[0]),
            "norm2_w": f(norm2_w[0]), "w_gate": f(w_gate[0]), "w_up": f(w_up[0]), "w_down": f(w_down[0]),
            "final_norm_w": f(final_norm_w), "consts": consts,
            "xq": f(x[b, j * 2048:(j + 1) * 2048]),
        })
    if _return_maps:
        return in_maps
    nc = build_program(None)
    res = run_bass_kernel_spmd(nc, in_maps, core_ids=list(range(NCORE)))
    out = np.zeros((2, SEQ, D), np.float32)
    for core in range(NCORE):
        b, j = core // 4, core % 4
        out[b, j * 2048:(j + 1) * 2048] = np.asarray(res.results[core]["out"], dtype=np.float32)
    return out
```

```python
import numpy as np
import ml_dtypes
import concourse.bass as bass
import concourse.mybir as mybir
from concourse.bass_utils import run_bass_kernel_spmd

F32 = mybir.dt.float32
BF16 = mybir.dt.bfloat16
AF = mybir.ActivationFunctionType
ALU = mybir.AluOpType

D = 1024
SEQ = 8192
CTX = 256
NCORE = 8
DFF = 2816
EPS = 1e-6
OQ, OK_, OV, OR, OLF, OLB, OZ, OXS, OB, OC, ODF, ODB = 0, 128, 256, 512, 768, 784, 800, 1312, 1824, 1952, 2080, 2088
NW1 = 2096
BIG = 1.0e30


class DmaSem:
    def __init__(self, sem):
        self.sem = sem
        self.count = 0


class Prog:
    ENGS = ("pe", "act", "dve", "pool", "sp")

    def __init__(self, nc, sems):
        self.nc = nc
        self.ops = {e: [] for e in self.ENGS}
        self.seq = {e: 0 for e in self.ENGS}
        self.sem = sems
        self.waited = {e: {} for e in self.ENGS}
        self.last_w = {}
        self.readers = {}
        self.final_tokens = []

    def _deps(self, eng, reads, writes):
        toks = []
        for r in reads:
            t = self.last_w.get(r)
            if t is not None:
                toks.append(t)
        for w in writes:
            t = self.last_w.get(w)
            if t is not None:
                toks.append(t)
            toks.extend(self.readers.get(w, ()))
        waits = []
        wd = self.waited[eng]
        best = {}
        for (sk, sem, val, teng) in toks:
            if teng == eng and eng == "pe":
                continue
            if wd.get(sk, -1) >= val:
                continue
            if best.get(sk, (None, -1))[1] < val:
                best[sk] = (sem, val)
        for sk, (sem, val) in best.items():
            wd[sk] = val
            waits.append((sem, val))
        return waits

    def _commit(self, tok, reads, writes):
        for r in reads:
            self.readers.setdefault(r, []).append(tok)
        for w in writes:
            self.last_w[w] = tok
            self.readers[w] = []

    PS_ALIAS = {"pG0": "pG", "pG1": "pG", "pG2": "pG", "pG3": "pG", "pO0": "pO", "pO1": "pO", "pSs": "pS", "pSc": "pS"}
    PS_KEYS = {"pT", "pA", "pB", "pG", "pO", "pC", "pS", "pY"}

    def _norm(self, reads, writes):
        r2, w2 = [], []
        for r in reads:
            if r is None:
                continue
            r = self.PS_ALIAS.get(r, r) if isinstance(r, str) else r
            (w2 if r in self.PS_KEYS else r2).append(r)
        for w in writes:
            if w is None:
                continue
            w = self.PS_ALIAS.get(w, w) if isinstance(w, str) else w
            w2.append(w)
        return r2, w2

    def op(self, eng, fn, reads=(), writes=()):
        reads, writes = self._norm(reads, writes)
        waits = self._deps(eng, reads, writes)
        self.seq[eng] += 1
        tok = (eng, self.sem[eng], self.seq[eng], eng)
        self.ops[eng].append((waits, fn, self.sem[eng], 1))
        self._commit(tok, reads, writes)
        return tok

    def dma(self, eng, fn, dsem, reads=(), writes=()):
        waits = self._deps(eng, list(reads), list(writes))
        dsem.count += 1
        tok = (id(dsem), dsem.sem, 16 * dsem.count, "dma")
        self.ops[eng].append((waits, fn, dsem.sem, 16))
        self._commit(tok, reads, writes)
        return tok

    def cc(self, eng, fn, csem, reads=(), writes=()):
        waits = self._deps(eng, list(reads), list(writes))
        csem.count += 1
        tok = (id(csem), csem.sem, csem.count, "cc")
        self.ops[eng].append((waits, fn, csem.sem, None))
        self._commit(tok, reads, writes)
        return tok

    def seal(self, dsem):
        final = (id(dsem), dsem.sem, 16 * dsem.count, "dma")
        for k, t in list(self.last_w.items()):
            if t[0] == id(dsem):
                self.last_w[k] = final

    def barrier(self, extra):
        for eng in self.ENGS:
            waits = []
            for e2 in self.ENGS:
                if e2 != eng and self.seq[e2] > 0 and self.waited[eng].get(e2, -1) < self.seq[e2]:
                    waits.append((self.sem[e2], self.seq[e2]))
                    self.waited[eng][e2] = self.seq[e2]
            for (key, sem, val) in extra:
                if val > 0 and self.waited[eng].get(key, -1) < val:
                    waits.append((sem, val))
                    self.waited[eng][key] = val
            self.ops[eng].append((waits, None, None, None))

    def emit(self, eng, handle):
        for (waits, fn, sem, inc) in self.ops[eng]:
            for (s, v) in waits:
                handle.wait_ge(s, v)
            if fn is None:
                continue
            ins = fn(handle)
            if inc is None:
                ins.then_inc(sem)
            else:
                ins.then_inc(sem, inc)
        self.ops[eng] = []


def bcast_free(ap2d, n):
    return ap2d.broadcast_to([ap2d.shape[0], n])


def build_program(dbg=None):
    nc = bass.Bass("TRN2", target_bir_lowering=False)
    dt_in = lambda name, shape, dt=F32: nc.dram_tensor(name, list(shape), dt, kind="ExternalInput")
    x_d = dt_in("x", [SEQ, D])
    ctx_d = dt_in("ctx", [CTX, D])
    cvec_d = dt_in("cvec", [2, D])
    wada_d = dt_in("w_ada", [D, 6 * D])
    bada_d = dt_in("b_ada", [6 * D])
    n1w_d = dt_in("norm1_w", [D])
    w1_d = dt_in("w1", [D, NW1])
    up_d = dt_in("gla_up", [2, 16, 128])
    gb_d = dt_in("gla_bias", [2, 128])
    gnw_d = dt_in("gla_norm_w", [256])
    cw_d = dt_in("conv_w", [4, 768])
    cb_d = dt_in("conv_b", [768])
    dtb_d = dt_in("dt_bias", [2, 8])
    alog_d = dt_in("a_log", [2, 8])
    dsk_d = dt_in("d_skip", [8])
    snw_d = dt_in("ssm_norm_w", [512])
    wg_d = dt_in("w_gates", [D, 2 * D])
    wpa_d = dt_in("w_pa", [D, D])
    wpb_d = dt_in("w_pb", [2 * D, D])
    wout_d = dt_in("w_out", [D, D])
    n2w_d = dt_in("norm2_w", [D])
    wgate_d = dt_in("w_gate", [D, DFF])
    wup_d = dt_in("w_up", [D, DFF])
    wdown_d = dt_in("w_down", [DFF, D])
    fnw_d = dt_in("final_norm_w", [D])
    consts_d = dt_in("consts", [128, 11 * 128])
    xq_d = dt_in("xq", [2048, D])
    out_d = nc.dram_tensor("out", [2048, D], F32, kind="ExternalOutput")
    prev_d = nc.dram_tensor("prev_scr", [SEQ, 768], F32)
    ex_d = nc.dram_tensor("ex_scr", [16, 768, 512], BF16)
    gx_d = nc.dram_tensor("gx_scr", [16, 4 * 768, 512], BF16)
    h_d = nc.dram_tensor("h_scr", [2048, D], F32)
    gxq_d = nc.dram_tensor("gxq_scr", [4, 4 * 768, 512], BF16)
    dbg_t = {}
    if dbg:
        for name, shape in dbg.items():
            if name.startswith("_"):
                continue
            dbg_t[name] = nc.dram_tensor("dbg_" + name, list(shape), BF16 if name.endswith("_bf") else F32, kind="ExternalOutput")

    from contextlib import ExitStack
    es = ExitStack()
    with es:
        cur = [es]

        def sb(name, shape, dt=F32):
            return cur[0].enter_context(nc.sbuf_tensor("s_" + name, list(shape), dt))

        def ps(name, shape, dt=F32):
            return es.enter_context(nc.psum_tensor("p_" + name, list(shape), dt))

        def newsem(name):
            return es.enter_context(nc.semaphore(name))

        sems = {e: newsem("sem_" + e) for e in Prog.ENGS}
        P = Prog(nc, sems)
        dsems = {}

        def DS(name):
            if name not in dsems:
                dsems[name] = DmaSem(newsem("d_" + name))
            return dsems[name]

        def tap(name, src_ap, reads, idx=None):
            if name not in dbg_t:
                return
            dst = dbg_t[name].ap() if idx is None else dbg_t[name][idx]
            P.dma("sp", lambda e: e.dma_start(out=dst, in_=src_ap), DS("tap"), reads, [])

        cst = sb("cst", [128, 11, 128], BF16)
        cstf = sb("cstf", [128, 2, 128], F32)
        IDN, MF, MB, SF, SB_, MF16, MB16, SF16, SB16, ONES = range(10)
        ada_col = sb("ada_col", [128, 48, 2], F32)
        wmod = sb("wmod", [128, 8, 3], F32)
        shbf = sb("shbf", [128, 8, 3], BF16)
        n1col = sb("n1col", [128, 8], F32)
        n2col = sb("n2col", [128, 8], F32)
        ccol = sb("ccol", [128, 2, 8], F32)
        scol = sb("scol", [128, 2, 8], BF16)
        badac = sb("badac", [128, 48], F32)
        onesrow = sb("onesrow", [1, 512], BF16)
        onesrowf = sb("onesrowf", [1, 128], F32)
        g_bc = sb("g_bc", [128, 2, D], F32)
        fnw_bc = sb("fnw_bc", [128, D], F32)
        xt = [sb("xt%d" % i, [128, D], F32) for i in range(2)]
        sq_junk = sb("sq_junk", [128, D], BF16)
        ss = sb("ss", [128, 8], F32)
        xs_ = [sb("xs%d" % i, [128, D], BF16) for i in range(2)]
        epsc = sb("epsc", [128, 2], F32)
        es1 = ExitStack()
        es1.__enter__()
        cur[0] = es1
        w1m = sb("w1m", [128, 8, NW1], BF16)
        c1col = sb("c1col", [128, 9, 2], F32)
        c1row = sb("c1row", [1, 2, 1160 + 8], BF16)
        dtlo = sb("dtlo", [1, 2, 2, 8], BF16)
        dthi = sb("dthi", [1, 2, 2, 8], BF16)
        upsb = sb("upsb", [16, 2, 128], BF16)
        gbrow = sb("gbrow", [1, 2, 128], BF16)
        gnw_bc = sb("gnw_bc", [128, 256], F32)
        snw_bc = sb("snw_bc", [128, 512], F32)
        dsk_bc = sb("dsk_bc", [128, 8], F32)
        negA = sb("negA", [128, 2, 8], F32)
        cbcol = sb("cbcol", [128, 6], F32)
        cdiag = sb("cdiag", [128, 24, 128], BF16)
        c1lr = sb("c1lr", [16, 2, 2], F32)
        pT = ps("pT", [128, 8, 128], BF16)
        pA = ps("pA", [128, 512], F32)
        pB = ps("pB", [128, 512], F32)
        pG = ps("pG", [128, 4, 128], F32)
        pO = ps("pO", [128, 2, 256], F32)
        pC = ps("pC", [128, 4, 128], F32)
        pY = ps("pY", [128, 512], F32)
        pS = ps("pS", [128, 512], F32)
        pSx = pS[:, 192:512].bitcast(BF16)
        pCb = pC[:, :, :].rearrange("p a b -> p (a b)").bitcast(BF16)
        pTf = pT[:, :, :].rearrange("p a b -> p (a b)").bitcast(F32)
        ccs = DmaSem(newsem("ccsem"))
        def all_dma_tokens():
            toks = [(id(d), d.sem, 16 * d.count) for d in dsems.values()]
            toks.append((id(ccs), ccs.sem, ccs.count))
            return toks

        def flush():
            P.barrier(all_dma_tokens())
            with nc.Block() as block:
                @block.sync
                def _(e):
                    P.emit("sp", e)

                @block.tensor
                def _(e):
                    P.emit("pe", e)

                @block.scalar
                def _(e):
                    P.emit("act", e)

                @block.vector
                def _(e):
                    P.emit("dve", e)

                @block.gpsimd
                def _(e):
                    P.emit("pool", e)

        es_s = ExitStack()
        es_s.__enter__()
        cur[0] = es_s
        c1rowf = sb("c1rowf", [1, 2, 32], F32)
        dtbrow = sb("dtbrow", [1, 2, 8], F32)
        alog_bc = sb("alog_bc", [128, 2, 8], F32)
        cwcol = sb("cwcol", [128, 4, 6], F32)
        grow = sb("grow", [1, 2, D], F32)
        badar = sb("badar", [1, 2, D], F32)


        def act(fn, reads, writes):
            return P.op("act", fn, reads, writes)

        def dve(fn, reads, writes):
            return P.op("dve", fn, reads, writes)

        def pool(fn, reads, writes):
            return P.op("pool", fn, reads, writes)

        def pe(fn, reads, writes):
            return P.op("pe", fn, reads, writes)

        def mm(out, lhsT, rhs, start, stop, reads, writes):
            return pe(lambda e: e.matmul(out, lhsT=lhsT, rhs=rhs, start=start, stop=stop), reads, writes)

        def tp(out, in_, reads, writes):
            return pe(lambda e: e.transpose(out, in_, cst[:, IDN, :]), reads + ["cst"], writes)

        def dma(q, out, in_, dsem, reads, writes, **kw):
            return P.dma(q, lambda e: e.dma_start(out=out, in_=in_, **kw), dsem, reads, writes)

        def activation(out, in_, func, reads, writes, bias=None, scale=None, accum_out=None):
            kw = {}
            if bias is not None:
                kw["bias"] = bias
            if scale is not None:
                kw["scale"] = scale
            if accum_out is not None:
                kw["accum_out"] = accum_out
            return act(lambda e: e.activation(out=out, in_=in_, func=func, **kw), reads, writes)

        dma("pool", cst[:, :, :], consts_d.ap().rearrange("p (a b) -> p a b", b=128), DS("cst"), [], ["cst"])
        dma("sp", cstf[:, 0, :], consts_d[:, 0:128], DS("cstf"), [], ["cstf"])
        dma("sp", cstf[:, 1, :], consts_d[:, 9 * 128:10 * 128], DS("cstf"), [], ["cstf"])
        dve(lambda e: e.memset(onesrow[:, :], 1.0), [], ["onesrow"])
        dve(lambda e: e.memset(onesrowf[:, :], 1.0), [], ["onesrowf"])
        sp_ = DS("small")
        for r_ in range(2):
            dma("sp", ccol[:, r_, :], cvec_d[r_].rearrange("(c p) -> p c", p=128), sp_, [], ["ccol"], allow_slow_non_contiguous=True)
        dma("sp", badac[:, :], bada_d.ap().rearrange("(c p) -> p c", p=128), sp_, [], ["badac"], allow_slow_non_contiguous=True)
        dma("sp", n1col[:, :], n1w_d.ap().rearrange("(c p) -> p c", p=128), sp_, [], ["n1col"], allow_slow_non_contiguous=True)
        dma("sp", n2col[:, :], n2w_d.ap().rearrange("(c p) -> p c", p=128), sp_, [], ["n2col"], allow_slow_non_contiguous=True)
        for j_ in range(4):
            dma("sp", cwcol[:, j_, :], cw_d[j_].rearrange("(c p) -> p c", p=128), sp_, [], ["cwcol"], allow_slow_non_contiguous=True)
        dma("sp", cbcol[:, :], cb_d.ap().rearrange("(c p) -> p c", p=128), sp_, [], ["cbcol"], allow_slow_non_contiguous=True)
        dma("sp", gnw_bc[:, :], gnw_d.ap().partition_broadcast(128), sp_, [], ["gnw_bc"])
        dma("sp", snw_bc[:, :], snw_d.ap().partition_broadcast(128), sp_, [], ["snw_bc"])
        dma("sp", dsk_bc[:, :], dsk_d.ap().partition_broadcast(128), sp_, [], ["dsk_bc"])
        dma("sp", alog_bc[:, :, :], alog_d.ap().partition_broadcast(128), sp_, [], ["alog_bc"])
        dma("sp", fnw_bc[:, :], fnw_d.ap().partition_broadcast(128), sp_, [], ["fnw_bc"])
        dma("sp", dtbrow[:, :, :], dtb_d.ap().rearrange("(o a) b -> o a b", o=1), sp_, [], ["dtbrow"])
        dma("sp", badar[:, 0, :], bada_d.ap().rearrange("(o n) -> o n", o=1)[:, 2 * D:3 * D], sp_, [], ["badar"])
        dma("sp", badar[:, 1, :], bada_d.ap().rearrange("(o n) -> o n", o=1)[:, 5 * D:6 * D], sp_, [], ["badar"])
        dma("pool", upsb[:, :, :], up_d.ap().rearrange("a r k -> r a k"), DS("cst"), [], ["upsb"])
        dma("pool", gbrow[:, :, :], gb_d.ap().rearrange("(o a) k -> o a k", o=1), DS("cst"), [], ["gbrow"])
        dma("pool", w1m[:, :, :], w1_d.ap().rearrange("(c p) n -> p c n", p=128), DS("w1"), [], ["w1m"])

        for nm_ in ("small", "cst", "cstf"):
            P.seal(DS(nm_))
        activation(scol[:, :, :], ccol[:, :, :], AF.Silu, ["ccol"], ["scol"])
        wab = [sb("wab%d" % i, [128, 8, 512], BF16) for i in range(2)]
        for blk in range(12):
            bi = blk % 2
            dma("pool", wab[bi][:, :, :], wada_d[:, blk * 512:(blk + 1) * 512].rearrange("(c p) n -> p c n", p=128),
                DS("wab%d" % bi), [], ["wab%d" % bi])
            for c4 in range(4):
                cc = blk * 4 + c4
                for kc in range(8):
                    mm(pA[:, cc * 2:cc * 2 + 2], wab[bi][:, kc, c4 * 128:(c4 + 1) * 128], scol[:, :, kc],
                       kc == 0, kc == 7, ["wab%d" % bi, "scol"], ["pA"])
            if blk in (4, 5, 10, 11):
                gi = 0 if blk < 6 else 1
                half = blk % 2
                for kc in range(8):
                    mm(pB[0:1, :], scol[:, 0, kc:kc + 1], wab[bi][:, kc, :], kc == 0, kc == 7, ["wab%d" % bi, "scol"], ["pB"])
                dve(lambda e, gi=gi, half=half: e.tensor_tensor(out=grow[:, gi, half * 512:(half + 1) * 512], in0=pB[0:1, :],
                                                               in1=badar[:, gi, half * 512:(half + 1) * 512], op=ALU.add),
                    ["pB", "badar"], ["grow"])
        dve(lambda e: e.tensor_tensor(out=ada_col[:, :, :], in0=pA[:, 0:96].rearrange("p (c r) -> p c r", r=2),
                                      in1=badac[:, :].unsqueeze(2).broadcast_to([128, 48, 2]), op=ALU.add),
            ["pA", "badac"], ["ada_col"])
        for mi, (ncol, scb, shb, r) in enumerate([(n1col, 8, 0, 0), (n1col, 8, 0, 1), (n2col, 32, 24, 0)]):
            dve(lambda e, mi=mi, ncol=ncol, scb=scb, r=r: e.scalar_tensor_tensor(
                out=wmod[:, :, mi], in0=ada_col[:, scb:scb + 8, r], scalar=1.0, in1=ncol[:, :], op0=ALU.add, op1=ALU.mult),
                ["ada_col", "n1col", "n2col"], ["wmod"])
            dve(lambda e, mi=mi, shb=shb, r=r: e.tensor_copy(out=shbf[:, :, mi], in_=ada_col[:, shb:shb + 8, r]),
                ["ada_col"], ["shbf"])
        for gi in range(2):
            for half in range(2):
                mm(pA[:, :], onesrowf[:, :], grow[:, gi, half * 512:(half + 1) * 512], True, True, ["onesrowf", "grow"], ["pA"])
                act(lambda e, gi=gi, half=half: e.copy(out=g_bc[:, gi, half * 512:(half + 1) * 512], in_=pA[:, :]), ["pA"], ["g_bc"])
        activation(negA[:, :, :], alog_bc[:, :, :], AF.Exp, ["alog_bc"], ["negA"])
        dve(lambda e: e.tensor_scalar(out=negA[:, :, :], in0=negA[:, :, :], scalar1=-1.0, scalar2=None, op0=ALU.mult), ["negA"], ["negA"])
        for cc in range(6):
            for j in range(4):
                dve(lambda e, cc=cc, j=j: e.tensor_scalar(out=cdiag[:, cc * 4 + j, :], in0=cst[:, IDN, :],
                                                          scalar1=cwcol[:, j, cc:cc + 1], scalar2=None, op0=ALU.mult),
                    ["cst", "cwcol"], ["cdiag"])
        FM_OFFS = [OQ, OK_, OXS, OXS + 128, OXS + 256, OXS + 384, OB, OC]
        for m in range(2):
            for gi_, off in enumerate(FM_OFFS):
                for kc in range(8):
                    mm(pG[:, 0, gi_ * 2 + m:gi_ * 2 + m + 1], w1m[:, kc, off:off + 128], shbf[:, kc, m:m + 1], kc == 0, kc == 7,
                       ["w1m", "shbf"], ["pG"])
        for m in range(2):
            dve(lambda e, m=m: e.tensor_copy(out=c1col[:, 0:8, m], in_=pG[:, 0, 0:16].rearrange("p (g m) -> p g m", m=2)[:, :, m]),
                ["pG"], ["c1col"])
        for m in range(2):
            for d_ in range(2):
                off = OLF if d_ == 0 else OLB
                for kc in range(8):
                    mm(pG[0:16, 1, m * 2 + d_:m * 2 + d_ + 1], w1m[:, kc, off:off + 16], shbf[:, kc, m:m + 1], kc == 0, kc == 7,
                       ["w1m", "shbf"], ["pG"])
        dve(lambda e: e.tensor_copy(out=c1lr[:, :, :], in_=pG[0:16, 1, 0:4].rearrange("p (m d) -> p m d", d=2)), ["pG"], ["c1lr"])
        for m in range(2):
            for (o0, n0, dst) in [(OK_, 384, 0), (OR, 256, 384), (OZ, 512, 640)]:
                for kc in range(8):
                    mm(pA[0:1, 0:n0], shbf[:, kc, m:m + 1], w1m[:, kc, o0:o0 + n0], kc == 0, kc == 7, ["w1m", "shbf"], ["pA"])
                act(lambda e, m=m, n0=n0, dst=dst: e.copy(out=c1row[:, m, dst:dst + n0], in_=pA[0:1, 0:n0]), ["pA"], ["c1row"])
            for kc in range(8):
                mm(pA[0:1, 0:16], shbf[:, kc, m:m + 1], w1m[:, kc, ODF:ODF + 16], kc == 0, kc == 7, ["w1m", "shbf"], ["pA"])
            dve(lambda e, m=m: e.tensor_tensor(out=c1rowf[:, m, 0:16].rearrange("o (a b) -> o a b", b=8),
                                               in0=pA[0:1, 0:16].rearrange("o (a b) -> o a b", b=8), in1=dtbrow[:, :, :], op=ALU.add),
                ["pA", "dtbrow"], ["c1rowf"])
            dve(lambda e, m=m: e.tensor_copy(out=dthi[:, m, :, :], in_=c1rowf[:, m, 0:16].rearrange("o (a b) -> o a b", b=8)),
                ["c1rowf"], ["dthi"])
            dve(lambda e, m=m: e.tensor_tensor(out=c1rowf[:, m, 16:32].rearrange("o (a b) -> o a b", b=8),
                                               in0=c1rowf[:, m, 0:16].rearrange("o (a b) -> o a b", b=8), in1=dthi[:, m, :, :], op=ALU.subtract),
                ["c1rowf", "dthi"], ["c1rowf2"])
            dve(lambda e, m=m: e.tensor_copy(out=dtlo[:, m, :, :], in_=c1rowf[:, m, 16:32].rearrange("o (a b) -> o a b", b=8)),
                ["c1rowf2"], ["dtlo"])
        def set_mod(m, reload):
            if reload:
                dma("pool", w1m[:, :, :], w1_d.ap().rearrange("(c p) n -> p c n", p=128), DS("w1"), [], ["w1m"])
            for kc in range(8):
                act(lambda e, m=m, kc=kc: e.activation(out=w1m[:, kc, :], in_=w1m[:, kc, :], func=AF.Identity, scale=wmod[:, kc, m:m + 1]),
                    ["w1m", "wmod"], ["w1m"])
        flush()
        es_s.close()
        cur[0] = es1
        STOP = dbg.get("_stop", (99,))[0] if dbg else 99
        xnT = sb("xnT", [128, 8, 512], BF16)
        qT = [sb("qT%d" % i_, [128, 512], BF16) for i_ in range(2)]
        kT = [sb("kT%d" % i_, [128, 512], BF16) for i_ in range(2)]
        lrT = [sb("lrT%d" % i_, [16, 512], BF16) for i_ in range(2)]
        upad = sb("upad", [128, 6, 8 * 67 + 8], BF16)
        xcT = [sb("xcT%d" % i_, [128, 4, 512], BF16) for i_ in range(2)]
        BT = [sb("BT%d" % i_, [128, 512], BF16) for i_ in range(2)]
        CT = [sb("CT%d" % i_, [128, 512], BF16) for i_ in range(2)]
        kv = [sb("kv%d" % i_, [128, 4, 384], BF16) for i_ in range(2)]
        r_s = [sb("r_s%d" % i_, [128, 4, 256], BF16) for i_ in range(2)]
        z_s = [sb("z_s%d" % i_, [128, 4, 512], BF16) for i_ in range(2)]
        dtr = [sb("dtr%d" % i_, [128, 4, 8], F32) for i_ in range(2)]
        e1 = sb("e1", [128, 128], F32)
        sp = sb("sp", [128, 128], BF16)
        Eq = sb("Eq", [128, 128], F32)
        Ek = sb("Ek", [128, 128], F32)
        Er = sb("Er", [128, 128], F32)
        dcol = sb("dcol", [128, 1], F32)
        qt_ = sb("qt_", [128, 128], BF16)
        kt_ = sb("kt_", [128, 128], BF16)
        kh_ = sb("kh_", [128, 128], BF16)
        attm = sb("attm", [128, 128], BF16)
        Sg = sb("Sg", [128, 256], F32)
        Sgb = sb("Sgb", [128, 256], BF16)
        e2 = sb("e2", [128, 8], F32)
        dtv = sb("dtv", [128, 8], F32)
        ldt = sb("ldt", [128, 8], F32)
        abf = sb("abf", [128, 8], BF16)
        nb = sb("nb", [128, 8], F32)
        E3 = sb("E3", [128, 3, 8], F32)
        gsc = sb("gsc", [128, 8], F32)
        Wt = sb("Wt", [128, 8, 128], F32)
        Lseg = sb("Lseg", [128, 8, 128], BF16)
        cbm = sb("cbm", [128, 128], F32)
        MT = sb("MT", [128, 8, 128], BF16)
        xtm = sb("xtm", [128, 512], BF16)
        xh = sb("xh", [128, 512], BF16)
        Btm = sb("Btm", [128, 128], BF16)
        t1 = sb("t1", [128, 512], F32)
        Ss = sb("Ss", [128, 512], F32)
        Ssb = sb("Ssb", [128, 512], BF16)
        outp = [sb("outp%d" % i, [128, 768], F32) for i in range(2)]
        prevt = [sb("prevt%d" % i, [128, 768], F32) for i in range(2)]
        og = sb("og", [128, 256], F32)
        ys = sb("ys", [128, 512], F32)
        xd = sb("xd", [128, 512], F32)
        oa = sb("oa", [128, 768], BF16)
        nrm = sb("nrm", [128, 4], F32)
        ext = [sb("ext%d" % i, [128, 6, 512], BF16) for i in range(2)]
        dve(lambda e: e.memset(upad[:, :, :], 0.0), [], ["upad"])
        state = {"xi": 0, "oi": 0, "pi": 0, "ei": 0}

        def norm_tile(src_ap, dst_T, col0, nsub_key):
            i = state["xi"] % 2
            state["xi"] += 1
            dma("sp", xt[i][:, :], src_ap, DS("xt%d" % i), [], ["xt%d" % i])
            activation(sq_junk[:, :], xt[i][:, :], AF.Square, ["xt%d" % i], ["sq_junk", "ss"], accum_out=ss[:, 0:1])
            activation(ss[:, 1:2], ss[:, 0:1], AF.Ln, ["ss", "epsc"], ["ss"], scale=1.0 / D, bias=epsc[:, 0:1])
            activation(ss[:, 2:3], ss[:, 1:2], AF.Exp, ["ss"], ["ss"], scale=-0.5)
            act(lambda e, i=i: e.activation(out=xs_[i][:, :], in_=xt[i][:, :], func=AF.Identity, scale=ss[:, 2:3]),
                 ["xt%d" % i, "ss"], ["xs%d" % i])
            for kc in range(8):
                tp(pT[:, kc, :], xs_[i][:, kc * 128:(kc + 1) * 128], ["xs%d" % i], ["pT"])
            act(lambda e: e.copy(out=dst_T[:, :, col0:col0 + 128], in_=pT[:, :, :]), ["pT"], [nsub_key])

        dve(lambda e: e.memset(epsc[:, 0:1], EPS), [], ["epsc"])
        dve(lambda e: e.memset(epsc[:, 1:2], 1.0), [], ["epsc"])

        def project_block(m, dir_, ntok, rowlen, need_rz, bset):
            W = w1m
            wk = "w1m"
            nrow = ntok // rowlen
            banks = [pA, pTf]
            bk = [0]

            def nb_():
                b = banks[bk[0] % 2]
                k = "pA" if bk[0] % 2 == 0 else "pT"
                bk[0] += 1
                return b, k
            for gi_, off in enumerate(FM_OFFS):
                b, k = nb_()
                for kc in range(8):
                    mm(b[:, 0:ntok], W[:, kc, off:off + 128], xnT[:, kc, 0:ntok], kc == 0, kc == 7, [wk, "xnT"], [k])
                if gi_ < 2:
                    dst = (qT[bset] if gi_ == 0 else kT[bset])
                    activation(dst[:, 0:ntok], b[:, 0:ntok], AF.Identity, [k, "c1col"], ["qT%d" % bset if gi_ == 0 else "kT%d" % bset],
                               bias=c1col[:, gi_, m:m + 1])
                else:
                    cc = gi_ - 2
                    if rowlen == 64:
                        dst = upad[:, cc, 0:nrow * 67].rearrange("p (r c) -> p r c", c=67)[:, :, 2:66]
                        src = b[:, 0:ntok].rearrange("p (r c) -> p r c", c=64)
                    else:
                        dst = upad[:, cc, 2:2 + ntok]
                        src = b[:, 0:ntok]
                    activation(dst, src, AF.Identity, [k, "c1col"], ["upad"], bias=c1col[:, gi_, m:m + 1])
                yield
            off = OLF if dir_ == 0 else OLB
            b, k = nb_()
            for kc in range(8):
                mm(b[0:16, 0:ntok], W[:, kc, off:off + 16], xnT[:, kc, 0:ntok], kc == 0, kc == 7, [wk, "xnT"], [k])
            activation(lrT[bset][:, 0:ntok], b[0:16, 0:ntok], AF.Identity, [k, "c1lr"], ["lrT%d" % bset], bias=c1lr[:, m, dir_:dir_ + 1])
            yield
            for cc in range(6):
                b, k = nb_()
                for j in range(4):
                    if rowlen == 64:
                        rhs = upad[:, cc, 0:nrow * 67].rearrange("p (r c) -> p r c", c=67)[:, :, j:j + 64]
                        o_ = b[:, 0:ntok].rearrange("p (r c) -> p r c", c=64)
                    else:
                        rhs = upad[:, cc, j:j + ntok]
                        o_ = b[:, 0:ntok]
                    mm(o_, cdiag[:, cc * 4 + j, :], rhs, j == 0, j == 3, ["cdiag", "upad"], [k])
                dst = xcT[bset][:, cc, 0:ntok] if cc < 4 else (BT[bset][:, 0:ntok] if cc == 4 else CT[bset][:, 0:ntok])
                dk = "xcT%d" % bset if cc < 4 else ("BT%d" % bset if cc == 4 else "CT%d" % bset)
                activation(dst, b[:, 0:ntok], AF.Silu, [k, "cbcol"], [dk], bias=cbcol[:, cc:cc + 1])
                yield
            for s_ in range(ntok // 128):
                tsl = slice(s_ * 128, (s_ + 1) * 128)
                groups = [(OK_, 384, 0, "kv%d" % bset)]
                if need_rz:
                    groups += [(OR, 256, 384, "r"), (OZ, 512, 640, "z")]
                for (o0, n0, c0, kind) in groups:
                    b, k = nb_()
                    for kc in range(8):
                        mm(b[:, 0:n0], xnT[:, kc, tsl], W[:, kc, o0:o0 + n0], kc == 0, False, [wk, "xnT"], [k])
                    mm(b[:, 0:n0], onesrow[:, 0:128], c1row[:, m, c0:c0 + n0], False, True, ["onesrow", "c1row"], [k])
                    if kind == "kv%d" % bset:
                        act(lambda e, b=b, s_=s_: e.copy(out=kv[bset][:, s_, :], in_=b[:, 0:384]), [k], ["kv%d" % bset])
                    elif kind == "r":
                        activation(r_s[bset][:, s_, :], b[:, 0:256], AF.Silu, [k], ["r_s%d" % bset])
                    else:
                        activation(z_s[bset][:, s_, :], b[:, 0:512], AF.Silu, [k], ["z_s%d" % bset])
                    yield
                b, k = nb_()
                od = ODF if dir_ == 0 else ODB
                for kc in range(8):
                    mm(b[:, 0:8], xnT[:, kc, tsl], W[:, kc, od:od + 8], kc == 0, False, [wk, "xnT"], [k])
                mm(b[:, 0:8], onesrow[:, 0:128], dthi[:, m, dir_, :], False, False, ["onesrow", "dthi"], [k])
                mm(b[:, 0:8], onesrow[:, 0:128], dtlo[:, m, dir_, :], False, True, ["onesrow", "dtlo"], [k])
                dve(lambda e, b=b, s_=s_: e.tensor_copy(out=dtr[bset][:, s_, :], in_=b[:, 0:8]), [k], ["dtr%d" % bset])
                yield

        def gla_chunk(c, dir_, bset):
            tsl = slice(c * 128, (c + 1) * 128)
            M16 = cst[:, MF16 if dir_ == 0 else MB16, :]
            S16 = cst[:, SF16 if dir_ == 0 else SB16, :]
            MK = cst[:, MF if dir_ == 0 else MB, :]
            last = 127 if dir_ == 0 else 0
            mm(pG[:, 0, :], lrT[bset][:, tsl], upsb[:, dir_, :], True, False, ["lrT%d" % bset, "upsb"], ["pG0"])
            mm(pG[:, 0, :], onesrow[:, 0:128], gbrow[:, dir_, :], False, True, ["onesrow", "gbrow"], ["pG0"])
            yield
            activation(e1[:, :], pG[:, 0, :], AF.Exp, ["pG0"], ["e1"], scale=-1.0)
            activation(sp[:, :], e1[:, :], AF.Ln, ["e1", "epsc"], ["sp"], bias=epsc[:, 1:2])
            yield
            mm(pG[:, 1, :], sp[:, :], M16, True, True, ["sp", "cst"], ["pG1"])
            mm(pG[:, 2, :], S16, sp[:, :], True, True, ["sp", "cst"], ["pG2"])
            yield
            activation(Eq[:, :], pG[:, 1, :], AF.Exp, ["pG1"], ["Eq"])
            activation(Ek[:, :], pG[:, 1, :], AF.Exp, ["pG1"], ["Ek"], scale=-1.0)
            activation(Er[:, :], pG[:, 2, :], AF.Exp, ["pG2"], ["Er"])
            activation(dcol[:, :], pG[:, 1, last:last + 1], AF.Exp, ["pG1"], ["dcol"])
            yield
            dve(lambda e: e.scalar_tensor_tensor(out=qt_[:, :], in0=qT[bset][:, tsl], scalar=128.0 ** -0.5, in1=Eq[:, :], op0=ALU.mult, op1=ALU.mult),
                ["qT%d" % bset, "Eq"], ["qt_"])
            dve(lambda e: e.tensor_tensor(out=kt_[:, :], in0=kT[bset][:, tsl], in1=Ek[:, :], op=ALU.mult), ["kT%d" % bset, "Ek"], ["kt_"])
            dve(lambda e: e.tensor_tensor(out=kh_[:, :], in0=kv[bset][:, c, 0:128], in1=Er[:, :], op=ALU.mult), ["kv%d" % bset, "Er"], ["kh_"])
            yield
            mm(pG[:, 3, :], kt_[:, :], qt_[:, :], True, True, ["kt_", "qt_"], ["pG3"])
            yield
            dve(lambda e: e.tensor_tensor(out=attm[:, :], in0=pG[:, 3, :], in1=MK, op=ALU.mult), ["pG3", "cst"], ["attm"])
            yield
            mm(pO[:, 0, :], attm[:, :], kv[bset][:, c, 128:384], True, False, ["attm", "kv%d" % bset], ["pO0"])
            mm(pO[:, 0, :], qt_[:, :], Sgb[:, :], False, True, ["qt_", "Sgb"], ["pO0"])
            mm(pO[:, 1, :], kh_[:, :], kv[bset][:, c, 128:384], True, True, ["kh_", "kv%d" % bset], ["pO1"])
            yield
            dve(lambda e: e.scalar_tensor_tensor(out=Sg[:, :], in0=Sg[:, :], scalar=dcol[:, 0:1], in1=pO[:, 1, :], op0=ALU.mult, op1=ALU.add),
                ["Sg", "dcol", "pO1"], ["Sg"])
            yield
            act(lambda e: e.copy(out=Sgb[:, :], in_=Sg[:, :]), ["Sg"], ["Sgb"])
            yield

        pSf = pS
        pSb = None

        def ssd_chunk(c, dir_, bset):
            tsl = slice(c * 128, (c + 1) * 128)
            MK = cst[:, MF if dir_ == 0 else MB, :]
            SK = cst[:, SF if dir_ == 0 else SB_, :]
            activation(e2[:, :], dtr[bset][:, c, :], AF.Exp, ["dtr%d" % bset], ["e2"])
            activation(dtv[:, :], e2[:, :], AF.Ln, ["e2", "epsc"], ["dtv"], bias=epsc[:, 1:2])
            activation(ldt[:, :], dtv[:, :], AF.Ln, ["dtv"], ["ldt"])
            yield
            dve(lambda e: e.tensor_tensor(out=abf[:, :], in0=dtv[:, :], in1=negA[:, dir_, :], op=ALU.mult), ["dtv", "negA"], ["abf"])
            dve(lambda e: e.tensor_tensor(out=Lseg[:, :, :], in0=SK.unsqueeze(1).broadcast_to([128, 8, 128]),
                                          in1=abf[:, :].unsqueeze(2).broadcast_to([128, 8, 128]), op=ALU.mult), ["abf", "cst"], ["Lseg"])
            yield
            for h in range(4):
                mm(pC[:, h, :], Lseg[:, h, :], MK, True, True, ["Lseg", "cst"], ["pC"])
            yield
            mm(pS[:, 128:136], MK, abf[:, :], True, True, ["abf", "cst"], ["pSs"])
            mm(pS[:, 136:144], SK, abf[:, :], True, True, ["abf", "cst"], ["pSs"])
            mm(pS[:, 144:152], cst[:, ONES, :], abf[:, :], True, True, ["abf", "cst"], ["pSs"])
            yield
            activation(E3[:, :, :], pS[:, 128:152].rearrange("p (a b) -> p a b", b=8), AF.Exp, ["pSs"], ["E3"])
            yield
            dve(lambda e: e.tensor_tensor(out=gsc[:, :], in0=E3[:, 1, :], in1=dtv[:, :], op=ALU.mult), ["E3", "dtv"], ["gsc"])
            yield
            for h in range(4):
                activation(Wt[:, h, :], pC[:, h, :], AF.Exp, ["pC", "ldt"], ["Wt"], bias=ldt[:, h:h + 1])
            yield
            for h in range(4, 8):
                mm(pC[:, h - 4, :], Lseg[:, h, :], MK, True, True, ["Lseg", "cst"], ["pC"])
            yield
            for h in range(4, 8):
                activation(Wt[:, h, :], pC[:, h - 4, :], AF.Exp, ["pC", "ldt"], ["Wt"], bias=ldt[:, h:h + 1])
            yield
            mm(pS[:, 0:128], BT[bset][:, tsl], CT[bset][:, tsl], True, True, ["BT%d" % bset, "CT%d" % bset], ["pSc"])
            yield
            dve(lambda e: e.tensor_tensor(out=cbm[:, :], in0=pS[:, 0:128], in1=MK, op=ALU.mult), ["pSc", "cst"], ["cbm"])
            dve(lambda e: e.tensor_tensor(out=MT[:, :, :], in0=Wt[:, :, :],
                                          in1=cbm[:, :].unsqueeze(1).broadcast_to([128, 8, 128]), op=ALU.mult),
                ["Wt", "cbm"], ["MT"])
            yield
            for cc in range(4):
                tp(pSx[:, cc * 128:(cc + 1) * 128], xcT[bset][:, cc, tsl], ["xcT%d" % bset], ["pS"])
            yield
            tp(pSx[:, 512:640], BT[bset][:, tsl], ["BT%d" % bset], ["pS"])
            yield
            act(lambda e: e.copy(out=xtm[:, :], in_=pSx[:, 0:512]), ["pS"], ["xtm"])
            yield
            dve(lambda e: e.tensor_tensor(out=xh[:, :].rearrange("p (h q) -> p h q", q=64),
                                          in0=pSx[:, 0:512].rearrange("p (h q) -> p h q", q=64),
                                          in1=gsc[:, :].unsqueeze(2).broadcast_to([128, 8, 64]), op=ALU.mult), ["pS", "gsc"], ["xh"])
            yield
            act(lambda e: e.copy(out=Btm[:, :], in_=pSx[:, 512:640]), ["pS"], ["Btm"])
            yield
            for h in range(8):
                mm(pY[:, h * 64:(h + 1) * 64], MT[:, h, :], xtm[:, h * 64:(h + 1) * 64], True, True, ["MT", "xtm"], ["pY"])
            yield
            mm(pB[:, :], CT[bset][:, tsl], Ssb[:, :], True, True, ["CT%d" % bset, "Ssb"], ["pB"])
            yield
            dve(lambda e: e.tensor_tensor(out=t1[:, :].rearrange("p (h q) -> p h q", q=64),
                                          in0=pB[:, :].rearrange("p (h q) -> p h q", q=64),
                                          in1=E3[:, 0, :].unsqueeze(2).broadcast_to([128, 8, 64]), op=ALU.mult), ["pB", "E3"], ["t1"])
            yield
            mm(pB[:, :], Btm[:, :], xh[:, :], True, True, ["Btm", "xh"], ["pB"])
            yield
            dve(lambda e: e.tensor_tensor(out=Ss[:, :].rearrange("p (h q) -> p h q", q=64),
                                           in0=Ss[:, :].rearrange("p (h q) -> p h q", q=64),
                                           in1=E3[:, 2, :].unsqueeze(2).broadcast_to([128, 8, 64]), op=ALU.mult), ["Ss", "E3"], ["Ss"])
            dve(lambda e: e.tensor_tensor(out=Ss[:, :], in0=Ss[:, :], in1=pB[:, :], op=ALU.add), ["Ss", "pB"], ["Ss"])
            yield
            act(lambda e: e.copy(out=Ssb[:, :], in_=Ss[:, :]), ["Ss"], ["Ssb"])

        def interleave(*gens):
            gens = list(gens)
            while gens:
                for g in list(gens):
                    try:
                        next(g)
                    except StopIteration:
                        gens.remove(g)

        def out_pass1(c, tok0):
            i = state["oi"] % 2
            state["oi"] += 1
            act(lambda e: e.copy(out=outp[i][:, 0:256], in_=pO[:, 0, :]), ["pO0"], ["outp%d" % i])
            dve(lambda e: e.tensor_tensor(out=outp[i][:, 256:768], in0=t1[:, :], in1=pY[:, :], op=ALU.add), ["t1", "pY"], ["outp%d" % i])
            dma("sp", prev_d[tok0:tok0 + 128, :], outp[i][:, :], DS("outp%d" % i), ["outp%d" % i], [("prev", tok0)])

        def out_pass2(c, tok0, st_idx, bset):
            i = state["pi"] % 2
            state["pi"] += 1
            ei = st_idx % 2
            dma("sp", prevt[i][:, :], prev_d[tok0:tok0 + 128, :], DS("prevt%d" % i), [("prev", tok0)], ["prevt%d" % i])
            dve(lambda e: e.tensor_tensor(out=og[:, :], in0=pO[:, 0, :], in1=prevt[i][:, 0:256], op=ALU.add), ["pO0", "prevt%d" % i], ["og"])
            activation(sq_junk[:, 0:256], og[:, :], AF.Square, ["og"], ["sq_junk", "nrm"], accum_out=nrm[:, 0:1])
            activation(nrm[:, 1:2], nrm[:, 0:1], AF.Ln, ["nrm", "epsc"], ["nrm"], scale=1.0 / 256, bias=epsc[:, 0:1])
            activation(nrm[:, 1:2], nrm[:, 1:2], AF.Exp, ["nrm"], ["nrm"], scale=-0.5)
            dve(lambda e: e.scalar_tensor_tensor(out=og[:, :], in0=og[:, :], scalar=nrm[:, 1:2], in1=gnw_bc[:, :], op0=ALU.mult, op1=ALU.mult),
                ["og", "nrm", "gnw_bc"], ["og"])
            dve(lambda e: e.tensor_tensor(out=oa[:, 0:256], in0=og[:, :], in1=r_s[bset][:, c, :], op=ALU.mult), ["og", "r_s%d" % bset], ["oa"])
            dve(lambda e: e.tensor_tensor(out=xd[:, :].rearrange("p (h q) -> p h q", q=64),
                                           in0=xtm[:, :].rearrange("p (h q) -> p h q", q=64),
                                           in1=dsk_bc[:, :].unsqueeze(2).broadcast_to([128, 8, 64]), op=ALU.mult), ["xtm", "dsk_bc"], ["xd"])
            dve(lambda e: e.tensor_tensor(out=xd[:, :], in0=xd[:, :], in1=prevt[i][:, 256:768], op=ALU.add), ["xd", "prevt%d" % i], ["xd"])
            dve(lambda e: e.tensor_tensor(out=ys[:, :], in0=t1[:, :], in1=pY[:, :], op=ALU.add), ["t1", "pY"], ["ys"])
            dve(lambda e: e.tensor_tensor(out=ys[:, :], in0=ys[:, :], in1=xd[:, :], op=ALU.add), ["ys", "xd"], ["ys"])
            dve(lambda e: e.tensor_tensor(out=ys[:, :], in0=ys[:, :], in1=z_s[bset][:, c, :], op=ALU.mult), ["ys", "z_s%d" % bset], ["ys"])
            activation(sq_junk[:, 0:512], ys[:, :], AF.Square, ["ys"], ["sq_junk", "nrm"], accum_out=nrm[:, 2:3])
            activation(nrm[:, 3:4], nrm[:, 2:3], AF.Ln, ["nrm", "epsc"], ["nrm"], scale=1.0 / 512, bias=epsc[:, 0:1])
            activation(nrm[:, 3:4], nrm[:, 3:4], AF.Exp, ["nrm"], ["nrm"], scale=-0.5)
            dve(lambda e: e.scalar_tensor_tensor(out=oa[:, 256:768], in0=ys[:, :], scalar=nrm[:, 3:4], in1=snw_bc[:, :], op0=ALU.mult, op1=ALU.mult),
                ["ys", "nrm", "snw_bc"], ["oa"])
            if tok0 // 512 == 11:
                tap("oa_bf", oa[:, :], ["oa"], idx=c)
                tap("ogys", og[:, :], ["og"], idx=(c, slice(None), slice(0, 256)))
            for cc in range(6):
                tp(pCb[:, cc * 128:(cc + 1) * 128], oa[:, cc * 128:(cc + 1) * 128], ["oa"], ["pC"])
            act(lambda e: e.copy(out=ext[ei][:, :, c * 128:(c + 1) * 128], in_=pCb[:, 0:768].rearrange("p (a b) -> p a b", b=128)), ["pC"], ["ext%d" % ei])

        Sg0s = sb("Sg0s", [128, 256], F32)
        Ss0s = sb("Ss0s", [128, 512], F32)

        def drain(g):
            if g is None:
                return
            for _ in g:
                pass

        def proj_gen(m, dir_, rows_fn, ntiles, rowlen, need_rz, bset):
            for s_ in range(ntiles):
                norm_tile(rows_fn(s_), xnT, s_ * 128, "xnT")
                yield
            yield from project_block(m, dir_, ntiles * 128, rowlen, need_rz, bset)

        def zero_states():
            dve(lambda e: e.memset(Sg[:, :], 0.0), ["Sg"], ["Sg"])
            dve(lambda e: e.memset(Sgb[:, :], 0.0), ["Sgb"], ["Sgb"])
            dve(lambda e: e.memset(Ss[:, :], 0.0), ["Ss"], ["Ss"])
            dve(lambda e: e.memset(Ssb[:, :], 0.0), ["Ssb"], ["Ssb"])

        def run_ctx(dir_):
            zero_states()
            drain(proj_gen(1, dir_, lambda s_: ctx_d[s_ * 128:(s_ + 1) * 128, :], 2, 256, False, 0))
            for c in ([0, 1] if dir_ == 0 else [1, 0]):
                interleave(gla_chunk(c, dir_, 0), ssd_chunk(c, dir_, 0))
            tap("Sg_ctx%d" % dir_, Sg[:, :], ["Sg"])
            tap("Ss_ctx%d" % dir_, Ss[:, :], ["Ss"])

        def run_latent(dir_):
            sts = list(range(NST_TOTAL)) if dir_ == 0 else list(range(NST_TOTAL - 1, -1, -1))
            sts = sts[:NST_RUN]
            rows = lambda st: (lambda s_: x_d[st * 512 + s_ * 128:st * 512 + (s_ + 1) * 128, :])
            bset = 0
            drain(proj_gen(0, dir_, rows(sts[0]), 4, 64, dir_ == 0, bset))
            for i, st in enumerate(sts):
                tok0 = st * 512
                bg = proj_gen(0, dir_, rows(sts[i + 1]), 4, 64, dir_ == 0, 1 - bset) if i + 1 < len(sts) else None
                for c in ([0, 1, 2, 3] if dir_ == 0 else [3, 2, 1, 0]):
                    gens = [gla_chunk(c, dir_, bset), ssd_chunk(c, dir_, bset)]
                    while gens:
                        for g in list(gens):
                            try:
                                next(g)
                            except StopIteration:
                                gens.remove(g)
                        if bg is not None:
                            try:
                                next(bg)
                            except StopIteration:
                                bg = None
                    if dir_ == 1:
                        out_pass1(c, tok0 + c * 128)
                    else:
                        out_pass2(c, tok0 + c * 128, st, bset)
                drain(bg)
                if dir_ == 0:
                    ei = st % 2
                    dma("sp", ex_d[st].rearrange("(c p) t -> p c t", p=128), ext[ei][:, :, :], DS("ext%d" % ei),
                        ["ext%d" % ei], [("ex", st)])
                    if STOP >= 3:
                        P.cc("pool", lambda e, st=st: e.collective_compute(
                            "AllGather", ALU.bypass, replica_groups=[[0, 1, 2, 3], [4, 5, 6, 7]],
                            ins=[ex_d[st]], outs=[gx_d[st]]), ccs, [("ex", st)], ["gx"])
                bset = 1 - bset

        def run_all():
            set_mod(1, False)
            dve(lambda e: e.memset(upad[:, :, :], 0.0), ["upad"], ["upad"])
            run_ctx(0)
            dve(lambda e: e.tensor_copy(out=Sg0s[:, :], in_=Sg[:, :]), ["Sg"], ["Sg0s"])
            dve(lambda e: e.tensor_copy(out=Ss0s[:, :], in_=Ss[:, :]), ["Ss"], ["Ss0s"])
            run_ctx(1)
            set_mod(0, True)
            dve(lambda e: e.memset(upad[:, :, :], 0.0), ["upad"], ["upad"])
            if STOP >= 1:
                run_latent(1)
                tap("prev", prev_d[7680:8192, :], [("prev", 7680 + i * 128) for i in range(4)])
            if STOP >= 2:
                dve(lambda e: e.tensor_copy(out=Sg[:, :], in_=Sg0s[:, :]), ["Sg0s", "Sg"], ["Sg"])
                dve(lambda e: e.tensor_copy(out=Ss[:, :], in_=Ss0s[:, :]), ["Ss0s", "Ss"], ["Ss"])
                dve(lambda e: e.tensor_copy(out=Sgb[:, :], in_=Sg0s[:, :]), ["Sg0s", "Sgb"], ["Sgb"])
                dve(lambda e: e.tensor_copy(out=Ssb[:, :], in_=Ss0s[:, :]), ["Ss0s", "Ssb"], ["Ssb"])
                run_latent(0)

        NST_TOTAL = SEQ // 512
        NST_RUN = NST_TOTAL if not dbg else dbg.get("_nst", (NST_TOTAL,))[0]
        tap("ada_col", ada_col[:, :, :].rearrange("p a b -> p (a b)"), ["ada_col"])
        run_all()

        flush()
        es1.close()
        if STOP < 4:
            return nc
        es2 = ExitStack()
        es2.__enter__()
        cur[0] = es2
        wgs = sb("wgs", [128, 8, 2 * D], BF16)
        wpa = sb("wpa", [128, 8, D], BF16)
        wpb = sb("wpb", [128, 16, D], BF16)
        wo = sb("wo", [128, 8, D], BF16)
        cgc = sb("cgc", [128, 16], F32)
        xnT2 = sb("xnT2", [128, 8, 512], BF16)
        oaT = sb("oaT", [128, 8, 512], BF16)
        obT = sb("obT", [128, 16, 512], BF16)
        gT = sb("gT", [128, 16, 512], BF16)
        mg = sb("mg", [128, 8, 512], BF16)
        tA = sb("tA", [128, 512], F32)
        tB = sb("tB", [128, 512], F32)
        hrow = xt
        xrow = sb("xrow", [128, 4, D], F32)

        dma("pool", wgs[:, :, :], wg_d.ap().rearrange("(c p) n -> p c n", p=128), DS("wgs"), [], ["wgs"])
        dma("pool", wpa[:, :, :], wpa_d.ap().rearrange("(c p) n -> p c n", p=128), DS("wpa"), [], ["wpa"])
        dma("pool", wpb[:, :, :], wpb_d.ap().rearrange("(c p) n -> p c n", p=128), DS("wpb"), [], ["wpb"])
        dma("pool", wo[:, :, :], wout_d.ap().rearrange("(c p) n -> p c n", p=128), DS("wo"), [], ["wo"])
        for fc in range(16):
            for kc in range(8):
                mm(pG[:, 0, fc:fc + 1], wgs[:, kc, fc * 128:(fc + 1) * 128], shbf[:, kc, 0:1], kc == 0, kc == 7, ["wgs", "shbf"], ["pG"])
        dve(lambda e: e.tensor_copy(out=cgc[:, :], in_=pG[:, 0, 0:16]), ["pG"], ["cgc"])
        for kc in range(8):
            act(lambda e, kc=kc: e.activation(out=wgs[:, kc, :], in_=wgs[:, kc, :], func=AF.Identity, scale=wmod[:, kc, 0:1]),
                 ["wgs", "wmod", "cgc"], ["wgs"])

        def norm_rows(src_tile, key, dst_T, col0, dkey):
            activation(sq_junk[:, :], src_tile, AF.Square, [key], ["sq_junk", "ss"], accum_out=ss[:, 0:1])
            activation(ss[:, 1:2], ss[:, 0:1], AF.Ln, ["ss", "epsc"], ["ss"], scale=1.0 / D, bias=epsc[:, 0:1])
            activation(ss[:, 2:3], ss[:, 1:2], AF.Exp, ["ss"], ["ss"], scale=-0.5)
            act(lambda e: e.activation(out=xs_[0][:, :], in_=src_tile, func=AF.Identity, scale=ss[:, 2:3]), [key, "ss"], ["xs0"])
            for kc in range(8):
                tp(pT[:, kc, :], xs_[0][:, kc * 128:(kc + 1) * 128], ["xs0"], ["pT"])
            act(lambda e: e.copy(out=dst_T[:, :, col0:col0 + 128], in_=pT[:, :, :]), ["pT"], [dkey])

        for s4 in range(4):
            def fn_q(e, s4=s4):
                q = nc.partition_id([mybir.EngineType.SP]) % 4
                return e.dma_start(out=gxq_d[s4], in_=gx_d[q * 4 + s4])
            P.dma("sp", fn_q, DS("gxq"), ["gx"], ["gxq"])
        for blk in range(4):
            t0 = blk * 512
            for s_ in range(4):
                dma("sp", xrow[:, s_, :], xq_d[t0 + s_ * 128:t0 + (s_ + 1) * 128, :], DS("xrow"), [], ["xrow"])
            for s_ in range(4):
                norm_rows(xrow[:, s_, :], "xrow", xnT2, s_ * 128, "xnT2")
            for r in range(4):
                dma("sp", oaT[:, 2 * r:2 * r + 2, :], gxq_d[blk, r * 768:r * 768 + 256, :].rearrange("(c p) t -> p c t", p=128),
                    DS("oaT"), ["gxq"], ["oaT"])
                dma("sp", obT[:, 4 * r:4 * r + 4, :], gxq_d[blk, r * 768 + 256:r * 768 + 768, :].rearrange("(c p) t -> p c t", p=128),
                    DS("obT"), ["gxq"], ["obT"])
            if blk == 3:
                tap("oaT_bf", oaT[:, :, :], ["oaT"])
                tap("obT_bf", obT[:, :, :], ["obT"])
                tap("xnT2_bf", xnT2[:, :, :], ["xnT2"])
            for fc in range(16):
                b, k = (pA, "pA") if fc % 2 == 0 else (pB, "pB")
                for kc in range(8):
                    mm(b[:, :], wgs[:, kc, fc * 128:(fc + 1) * 128], xnT2[:, kc, :], kc == 0, kc == 7, ["wgs", "xnT2"], [k])
                activation(gT[:, fc, :], b[:, :], AF.Sigmoid, [k, "cgc"], ["gT"], bias=cgc[:, fc:fc + 1])
            for mc in range(8):
                for kc in range(8):
                    mm(pA[:, :], wpa[:, kc, mc * 128:(mc + 1) * 128], oaT[:, kc, :], kc == 0, kc == 7, ["wpa", "oaT"], ["pA"])
                for kc in range(16):
                    mm(pB[:, :], wpb[:, kc, mc * 128:(mc + 1) * 128], obT[:, kc, :], kc == 0, kc == 15, ["wpb", "obT"], ["pB"])
                dve(lambda e, mc=mc: e.tensor_tensor(out=tA[:, :], in0=pA[:, :], in1=gT[:, mc, :], op=ALU.mult), ["pA", "gT"], ["tA"])
                dve(lambda e, mc=mc: e.tensor_tensor(out=tB[:, :], in0=pB[:, :], in1=gT[:, 8 + mc, :], op=ALU.mult), ["pB", "gT"], ["tB"])
                dve(lambda e, mc=mc: e.tensor_tensor(out=mg[:, mc, :], in0=tA[:, :], in1=tB[:, :], op=ALU.add), ["tA", "tB"], ["mg"])
            if blk == 3:
                tap("gT_bf", gT[:, :, :], ["gT"])
                tap("mg_bf", mg[:, :, :], ["mg"])
            for s_ in range(4):
                hi = (blk * 4 + s_) % 2
                for half in range(2):
                    b, k = (pA, "pA") if half == 0 else (pB, "pB")
                    for kc in range(8):
                        mm(b[:, :], mg[:, kc, s_ * 128:(s_ + 1) * 128], wo[:, kc, half * 512:(half + 1) * 512], kc == 0, kc == 7, ["mg", "wo"], [k])
                    dve(lambda e, b=b, half=half, hi=hi: e.tensor_tensor(out=hrow[hi][:, half * 512:(half + 1) * 512], in0=b[:, :],
                                                                        in1=g_bc[:, 0, half * 512:(half + 1) * 512], op=ALU.mult),
                        [k, "g_bc"], ["xt%d" % hi])
                    dve(lambda e, half=half, hi=hi, s_=s_: e.tensor_tensor(out=hrow[hi][:, half * 512:(half + 1) * 512],
                                                                           in0=hrow[hi][:, half * 512:(half + 1) * 512],
                                                                           in1=xrow[:, s_, half * 512:(half + 1) * 512], op=ALU.add),
                         ["xt%d" % hi, "xrow"], ["xt%d" % hi])
                dma("sp", h_d[t0 + s_ * 128:t0 + (s_ + 1) * 128, :], hrow[hi][:, :], DS("xt%d" % hi), ["xt%d" % hi], [("h", t0 + s_ * 128)])

        tap("hq", h_d[1536:2048, :], [("h", 1536 + i * 128) for i in range(4)])
        flush()
        es2.close()
        if STOP < 5:
            return nc
        es3 = ExitStack()
        es3.__enter__()
        cur[0] = es3
        xnT2 = sb("xnT2b", [128, 8, 256], BF16)
        xrow = sb("xrowb", [128, 2, D], F32)
        wga = sb("wga", [128, 8, DFF], BF16)
        wu = sb("wu", [128, 8, DFF], BF16)
        wd = sb("wd", [128, 22, D], BF16)
        c2c = sb("c2c", [128, 22, 2], F32)
        uT = sb("uT", [128, 22, 256], BF16)
        sgt = sb("sgt", [128, 256], F32)
        yrow = sb("yrow", [128, D], F32)
        dma("pool", wga[:, :, :], wgate_d.ap().rearrange("(c p) n -> p c n", p=128), DS("wga"), [], ["wga"])
        dma("pool", wu[:, :, :], wup_d.ap().rearrange("(c p) n -> p c n", p=128), DS("wu"), [], ["wu"])
        dma("pool", wd[:, :, :], wdown_d.ap().rearrange("(c p) n -> p c n", p=128), DS("wd"), [], ["wd"])
        for wi, (wt_, wk) in enumerate([(wga, "wga"), (wu, "wu")]):
            for fc in range(22):
                for kc in range(8):
                    mm(pG[:, 0, fc * 2 + wi:fc * 2 + wi + 1], wt_[:, kc, fc * 128:(fc + 1) * 128], shbf[:, kc, 2:3], kc == 0, kc == 7,
                       [wk, "shbf"], ["pG"])
        dve(lambda e: e.tensor_copy(out=c2c[:, :, :], in_=pG[:, 0, 0:44].rearrange("p (f w) -> p f w", w=2)), ["pG"], ["c2c"])
        for (wt_, wk) in [(wga, "wga"), (wu, "wu")]:
            for kc in range(8):
                act(lambda e, wt_=wt_, kc=kc: e.activation(out=wt_[:, kc, :], in_=wt_[:, kc, :], func=AF.Identity, scale=wmod[:, kc, 2:3]),
                     [wk, "wmod", "c2c"], [wk])
        for blk in range(8):
            t0 = blk * 256
            for s_ in range(2):
                dma("sp", xrow[:, s_, :], h_d[t0 + s_ * 128:t0 + (s_ + 1) * 128, :], DS("xrow"), [("h", t0 + s_ * 128)], ["xrow"])
            for s_ in range(2):
                norm_rows(xrow[:, s_, :], "xrow", xnT2, s_ * 128, "xnT2")
            if blk == 7:
                tap("xnB_bf", xnT2[:, :, :], ["xnT2"])
                tap("hB", xrow[:, :, :], ["xrow"])
            for fc in range(22):
                for kc in range(8):
                    mm(pA[:, 0:256], wga[:, kc, fc * 128:(fc + 1) * 128], xnT2[:, kc, :], kc == 0, kc == 7, ["wga", "xnT2"], ["pA"])
                for kc in range(8):
                    mm(pB[:, 0:256], wu[:, kc, fc * 128:(fc + 1) * 128], xnT2[:, kc, :], kc == 0, kc == 7, ["wu", "xnT2"], ["pB"])
                activation(sgt[:, :], pA[:, 0:256], AF.Silu, ["pA", "c2c"], ["sgt"], bias=c2c[:, fc, 0:1])
                dve(lambda e, fc=fc: e.scalar_tensor_tensor(out=uT[:, fc, :], in0=pB[:, 0:256], scalar=c2c[:, fc, 1:2], in1=sgt[:, :], op0=ALU.add, op1=ALU.mult),
                    ["pB", "c2c", "sgt"], ["uT"])
            if blk == 7:
                tap("uT_bf", uT[:, :, :], ["uT"])
            for s_ in range(2):
                for half in range(2):
                    b, k = (pA, "pA") if half == 0 else (pB, "pB")
                    for fc in range(22):
                        mm(b[:, :], uT[:, fc, s_ * 128:(s_ + 1) * 128], wd[:, fc, half * 512:(half + 1) * 512], fc == 0, fc == 21, ["uT", "wd"], [k])
                    dve(lambda e, b=b, half=half: e.tensor_tensor(out=yrow[:, half * 512:(half + 1) * 512], in0=b[:, :],
                                                                 in1=g_bc[:, 1, half * 512:(half + 1) * 512], op=ALU.mult), [k, "g_bc"], ["yrow"])
                dve(lambda e, s_=s_: e.tensor_tensor(out=yrow[:, :], in0=yrow[:, :], in1=xrow[:, s_, :], op=ALU.add), ["yrow", "xrow"], ["yrow"])
                activation(sq_junk[:, :], yrow[:, :], AF.Square, ["yrow"], ["sq_junk", "ss"], accum_out=ss[:, 4:5])
                activation(ss[:, 5:6], ss[:, 4:5], AF.Ln, ["ss", "epsc"], ["ss"], scale=1.0 / D, bias=epsc[:, 0:1])
                activation(ss[:, 6:7], ss[:, 5:6], AF.Exp, ["ss"], ["ss"], scale=-0.5)
                hi = (blk * 2 + s_) % 2
                dve(lambda e, hi=hi: e.scalar_tensor_tensor(out=hrow[hi][:, :], in0=yrow[:, :], scalar=ss[:, 6:7], in1=fnw_bc[:, :], op0=ALU.mult, op1=ALU.mult),
                    ["yrow", "ss", "fnw_bc"], ["xt%d" % hi])
                tok = dma("sp", out_d[t0 + s_ * 128:t0 + (s_ + 1) * 128, :], hrow[hi][:, :], DS("xt%d" % hi), ["xt%d" % hi], ["out"])

        flush()
        es3.close()
    return nc


def _consts():
    p = np.arange(128)[:, None]
    f = np.arange(128)[None, :]
    I = (p == f).astype(np.float32)
    Mf = (p <= f).astype(np.float32)
    Mb = (p >= f).astype(np.float32)
    Sf = (p > f).astype(np.float32)
    Sb = (p < f).astype(np.float32)
    ones = np.ones((128, 128), np.float32)
    zeros = np.zeros((128, 128), np.float32)
    mats = [I, Mf, Mb, Sf, Sb, -Mf / 16.0, -Mb / 16.0, -Sf / 16.0, -Sb / 16.0, ones, zeros]
    return np.ascontiguousarray(np.concatenate(mats, axis=1).astype(np.float32))


def kernel(x, c, ctx, c_ctx, w_ada, b_ada, norm1_w, w_in, gla_up_f, gla_bias_f, gla_up_b, gla_bias_b,
           gla_norm_w, conv_w, conv_b, dt_bias_f, dt_bias_b, a_log_f, a_log_b, d_skip, ssm_norm_w,
           w_pa, w_pb, w_out, norm2_w, w_gate, w_up, w_down, final_norm_w, _return_maps=False):
    f = lambda a: np.ascontiguousarray(np.asarray(a, dtype=np.float32))
    x, c, ctx, c_ctx, w_in = f(x), f(c), f(ctx), f(c_ctx), f(w_in)
    win = w_in[0]
    consts = _consts()
    in_maps = []
    for core in range(NCORE):
        b, j = core // 4, core % 4
        cols = np.concatenate([
            np.arange(j * 128, (j + 1) * 128), 512 + np.arange(j * 128, (j + 1) * 128),
            1024 + np.arange(j * 256, (j + 1) * 256), 2048 + np.arange(j * 256, (j + 1) * 256),
            np.arange(3072, 3104), 3104 + np.arange(j * 512, (j + 1) * 512), 5152 + np.arange(j * 512, (j + 1) * 512),
            7200 + np.arange(j * 128, (j + 1) * 128), 7712 + np.arange(j * 128, (j + 1) * 128),
            8224 + np.arange(j * 8, (j + 1) * 8), 8256 + np.arange(j * 8, (j + 1) * 8)])
        assert cols.size == NW1
        cch = np.concatenate([np.arange(j * 512, (j + 1) * 512), 2048 + np.arange(j * 128, (j + 1) * 128),
                              2560 + np.arange(j * 128, (j + 1) * 128)])
        hs = slice(j * 128, (j + 1) * 128)
        h8 = slice(j * 8, (j + 1) * 8)
        in_maps.append({
            "x": f(x[b]), "ctx": f(ctx[b]), "cvec": f(np.stack([c[b], c_ctx])),
            "w_ada": f(w_ada[0]), "b_ada": f(b_ada[0]), "norm1_w": f(norm1_w[0]),
            "w1": f(win[:, cols]),
            "gla_up": f(np.stack([np.asarray(gla_up_f)[0][:, hs], np.asarray(gla_up_b)[0][:, hs]])),
            "gla_bias": f(np.stack([np.asarray(gla_bias_f)[0][hs], np.asarray(gla_bias_b)[0][hs]])),
            "gla_norm_w": f(gla_norm_w[0]),
            "conv_w": f(np.asarray(conv_w)[0][:, cch]), "conv_b": f(np.asarray(conv_b)[0][cch]),
            "dt_bias": f(np.stack([np.asarray(dt_bias_f)[0][h8], np.asarray(dt_bias_b)[0][h8]])),
            "a_log": f(np.stack([np.asarray(a_log_f)[0][h8], np.asarray(a_log_b)[0][h8]])),
            "d_skip": f(np.asarray(d_skip)[0][h8]),
            "ssm_norm_w": f(np.asarray(ssm_norm_w)[0][j * 512:(j + 1) * 512]),
            "w_gates": f(win[:, 8288:10336]), "w_pa": f(w_pa[0]), "w_pb": f(w_pb[0]), "w_out": f(w_out[0]),
            "norm2_w": f(norm2_w[0]), "w_gate": f(w_gate[0]), "w_up": f(w_up[0]), "w_down": f(w_down[0]),
            "final_norm_w": f(final_norm_w), "consts": consts,
            "xq": f(x[b, j * 2048:(j + 1) * 2048]),
        })
    if _return_maps:
        return in_maps
    nc = build_program(None)
    res = run_bass_kernel_spmd(nc, in_maps, core_ids=list(range(NCORE)))
    out = np.zeros((2, SEQ, D), np.float32)
    for core in range(NCORE):
        b, j = core // 4, core % 4
        out[b, j * 2048:(j + 1) * 2048] = np.asarray(res.results[core]["out"], dtype=np.float32)
    return out
```

```python
import numpy as np
import ml_dtypes
import concourse.bass as bass
import concourse.mybir as mybir
from concourse.bass_utils import run_bass_kernel_spmd

F32 = mybir.dt.float32
BF16 = mybir.dt.bfloat16
AF = mybir.ActivationFunctionType
ALU = mybir.AluOpType

D = 1024
SEQ = 8192
CTX = 256
NCORE = 8
DFF = 2816
EPS = 1e-6
OQ, OK_, OV, OR, OLF, OLB, OZ, OXS, OB, OC, ODF, ODB = 0, 128, 256, 512, 768, 784, 800, 1312, 1824, 1952, 2080, 2088
NW1 = 2096
BIG = 1.0e30


class DmaSem:
    def __init__(self, sem):
        self.sem = sem
        self.count = 0


class Prog:
    ENGS = ("pe", "act", "dve", "pool", "sp")

    def __init__(self, nc, sems):
        self.nc = nc
        self.ops = {e: [] for e in self.ENGS}
        self.seq = {e: 0 for e in self.ENGS}
        self.sem = sems
        self.waited = {e: {} for e in self.ENGS}
        self.last_w = {}
        self.readers = {}
        self.final_tokens = []

    def _deps(self, eng, reads, writes):
        toks = []
        for r in reads:
            t = self.last_w.get(r)
            if t is not None:
                toks.append(t)
        for w in writes:
            t = self.last_w.get(w)
            if t is not None:
                toks.append(t)
            toks.extend(self.readers.get(w, ()))
        waits = []
        wd = self.waited[eng]
        best = {}
        for (sk, sem, val, teng) in toks:
            if teng == eng and eng == "pe":
                continue
            if wd.get(sk, -1) >= val:
                continue
            if best.get(sk, (None, -1))[1] < val:
                best[sk] = (sem, val)
        for sk, (sem, val) in best.items():
            wd[sk] = val
            waits.append((sem, val))
        return waits

    def _commit(self, tok, reads, writes):
        for r in reads:
            self.readers.setdefault(r, []).append(tok)
        for w in writes:
            self.last_w[w] = tok
            self.readers[w] = []

    PS_ALIAS = {"pG0": "pG", "pG1": "pG", "pG2": "pG", "pG3": "pG", "pO0": "pO", "pO1": "pO", "pSs": "pS", "pSc": "pS"}
    PS_KEYS = {"pT", "pA", "pB", "pG", "pO", "pC", "pS", "pY"}

    def _norm(self, reads, writes):
        r2, w2 = [], []
        for r in reads:
            if r is None:
                continue
            r = self.PS_ALIAS.get(r, r) if isinstance(r, str) else r
            (w2 if r in self.PS_KEYS else r2).append(r)
        for w in writes:
            if w is None:
                continue
            w = self.PS_ALIAS.get(w, w) if isinstance(w, str) else w
            w2.append(w)
        return r2, w2

    def op(self, eng, fn, reads=(), writes=()):
        reads, writes = self._norm(reads, writes)
        waits = self._deps(eng, reads, writes)
        self.seq[eng] += 1
        tok = (eng, self.sem[eng], self.seq[eng], eng)
        self.ops[eng].append((waits, fn, self.sem[eng], 1))
        self._commit(tok, reads, writes)
        return tok

    def dma(self, eng, fn, dsem, reads=(), writes=()):
        waits = self._deps(eng, list(reads), list(writes))
        dsem.count += 1
        tok = (id(dsem), dsem.sem, 16 * dsem.count, "dma")
        self.ops[eng].append((waits, fn, dsem.sem, 16))
        self._commit(tok, reads, writes)
        return tok

    def cc(self, eng, fn, csem, reads=(), writes=()):
        waits = self._deps(eng, list(reads), list(writes))
        csem.count += 1
        tok = (id(csem), csem.sem, csem.count, "cc")
        self.ops[eng].append((waits, fn, csem.sem, None))
        self._commit(tok, reads, writes)
        return tok

    def seal(self, dsem):
        final = (id(dsem), dsem.sem, 16 * dsem.count, "dma")
        for k, t in list(self.last_w.items()):
            if t[0] == id(dsem):
                self.last_w[k] = final

    def barrier(self, extra):
        for eng in self.ENGS:
            waits = []
            for e2 in self.ENGS:
                if e2 != eng and self.seq[e2] > 0 and self.waited[eng].get(e2, -1) < self.seq[e2]:
                    waits.append((self.sem[e2], self.seq[e2]))
                    self.waited[eng][e2] = self.seq[e2]
            for (key, sem, val) in extra:
                if val > 0 and self.waited[eng].get(key, -1) < val:
                    waits.append((sem, val))
                    self.waited[eng][key] = val
            self.ops[eng].append((waits, None, None, None))

    def emit(self, eng, handle):
        for (waits, fn, sem, inc) in self.ops[eng]:
            for (s, v) in waits:
                handle.wait_ge(s, v)
            if fn is None:
                continue
            ins = fn(handle)
            if inc is None:
                ins.then_inc(sem)
            else:
                ins.then_inc(sem, inc)
        self.ops[eng] = []


def bcast_free(ap2d, n):
    return ap2d.broadcast_to([ap2d.shape[0], n])


def build_program(dbg=None):
    nc = bass.Bass("TRN2", target_bir_lowering=False)
    dt_in = lambda name, shape, dt=F32: nc.dram_tensor(name, list(shape), dt, kind="ExternalInput")
    x_d = dt_in("x", [SEQ, D])
    ctx_d = dt_in("ctx", [CTX, D])
    cvec_d = dt_in("cvec", [2, D])
    wada_d = dt_in("w_ada", [D, 6 * D])
    bada_d = dt_in("b_ada", [6 * D])
    n1w_d = dt_in("norm1_w", [D])
    w1_d = dt_in("w1", [D, NW1])
    up_d = dt_in("gla_up", [2, 16, 128])
    gb_d = dt_in("gla_bias", [2, 128])
    gnw_d = dt_in("gla_norm_w", [256])
    cw_d = dt_in("conv_w", [4, 768])
    cb_d = dt_in("conv_b", [768])
    dtb_d = dt_in("dt_bias", [2, 8])
    alog_d = dt_in("a_log", [2, 8])
    dsk_d = dt_in("d_skip", [8])
    snw_d = dt_in("ssm_norm_w", [512])
    wg_d = dt_in("w_gates", [D, 2 * D])
    wpa_d = dt_in("w_pa", [D, D])
    wpb_d = dt_in("w_pb", [2 * D, D])
    wout_d = dt_in("w_out", [D, D])
    n2w_d = dt_in("norm2_w", [D])
    wgate_d = dt_in("w_gate", [D, DFF])
    wup_d = dt_in("w_up", [D, DFF])
    wdown_d = dt_in("w_down", [DFF, D])
    fnw_d = dt_in("final_norm_w", [D])
    consts_d = dt_in("consts", [128, 11 * 128])
    xq_d = dt_in("xq", [2048, D])
    out_d = nc.dram_tensor("out", [2048, D], F32, kind="ExternalOutput")
    prev_d = nc.dram_tensor("prev_scr", [SEQ, 768], F32)
    ex_d = nc.dram_tensor("ex_scr", [16, 768, 512], BF16)
    gx_d = nc.dram_tensor("gx_scr", [16, 4 * 768, 512], BF16)
    h_d = nc.dram_tensor("h_scr", [2048, D], F32)
    gxq_d = nc.dram_tensor("gxq_scr", [4, 4 * 768, 512], BF16)
    dbg_t = {}
    if dbg:
        for name, shape in dbg.items():
            if name.startswith("_"):
                continue
            dbg_t[name] = nc.dram_tensor("dbg_" + name, list(shape), BF16 if name.endswith("_bf") else F32, kind="ExternalOutput")

    from contextlib import ExitStack
    es = ExitStack()
    with es:
        cur = [es]

        def sb(name, shape, dt=F32):
            return cur[0].enter_context(nc.sbuf_tensor("s_" + name, list(shape), dt))

        def ps(name, shape, dt=F32):
            return es.enter_context(nc.psum_tensor("p_" + name, list(shape), dt))

        def newsem(name):
            return es.enter_context(nc.semaphore(name))

        sems = {e: newsem("sem_" + e) for e in Prog.ENGS}
        P = Prog(nc, sems)
        dsems = {}

        def DS(name):
            if name not in dsems:
                dsems[name] = DmaSem(newsem("d_" + name))
            return dsems[name]

        def tap(name, src_ap, reads, idx=None):
            if name not in dbg_t:
                return
            dst = dbg_t[name].ap() if idx is None else dbg_t[name][idx]
            P.dma("sp", lambda e: e.dma_start(out=dst, in_=src_ap), DS("tap"), reads, [])

        cst = sb("cst", [128, 11, 128], BF16)
        cstf = sb("cstf", [128, 2, 128], F32)
        IDN, MF, MB, SF, SB_, MF16, MB16, SF16, SB16, ONES = range(10)
        ada_col = sb("ada_col", [128, 48, 2], F32)
        wmod = sb("wmod", [128, 8, 3], F32)
        shbf = sb("shbf", [128, 8, 3], BF16)
        n1col = sb("n1col", [128, 8], F32)
        n2col = sb("n2col", [128, 8], F32)
        ccol = sb("ccol", [128, 2, 8], F32)
        scol = sb("scol", [128, 2, 8], BF16)
        badac = sb("badac", [128, 48], F32)
        onesrow = sb("onesrow", [1, 512], BF16)
        onesrowf = sb("onesrowf", [1, 128], F32)
        g_bc = sb("g_bc", [128, 2, D], F32)
        fnw_bc = sb("fnw_bc", [128, D], F32)
        xt = [sb("xt%d" % i, [128, D], F32) for i in range(2)]
        sq_junk = sb("sq_junk", [128, D], BF16)
        ss = sb("ss", [128, 8], F32)
        xs_ = [sb("xs%d" % i, [128, D], BF16) for i in range(2)]
        epsc = sb("epsc", [128, 2], F32)
        es1 = ExitStack()
        es1.__enter__()
        cur[0] = es1
        w1m = sb("w1m", [128, 8, NW1], BF16)
        c1col = sb("c1col", [128, 9, 2], F32)
        c1row = sb("c1row", [1, 2, 1160 + 8], BF16)
        dtlo = sb("dtlo", [1, 2, 2, 8], BF16)
        dthi = sb("dthi", [1, 2, 2, 8], BF16)
        upsb = sb("upsb", [16, 2, 128], BF16)
        gbrow = sb("gbrow", [1, 2, 128], BF16)
        gnw_bc = sb("gnw_bc", [128, 256], F32)
        snw_bc = sb("snw_bc", [128, 512], F32)
        dsk_bc = sb("dsk_bc", [128, 8], F32)
        negA = sb("negA", [128, 2, 8], F32)
        cbcol = sb("cbcol", [128, 6], F32)
        cdiag = sb("cdiag", [128, 24, 128], BF16)
        c1lr = sb("c1lr", [16, 2, 2], F32)
        pT = ps("pT", [128, 8, 128], BF16)
        pA = ps("pA", [128, 512], F32)
        pB = ps("pB", [128, 512], F32)
        pG = ps("pG", [128, 4, 128], F32)
        pO = ps("pO", [128, 2, 256], F32)
        pC = ps("pC", [128, 4, 128], F32)
        pY = ps("pY", [128, 512], F32)
        pS = ps("pS", [128, 512], F32)
        pSx = pS[:, 192:512].bitcast(BF16)
        pCb = pC[:, :, :].rearrange("p a b -> p (a b)").bitcast(BF16)
        pTf = pT[:, :, :].rearrange("p a b -> p (a b)").bitcast(F32)
        ccs = DmaSem(newsem("ccsem"))
        def all_dma_tokens():
            toks = [(id(d), d.sem, 16 * d.count) for d in dsems.values()]
            toks.append((id(ccs), ccs.sem, ccs.count))
            return toks

        def flush():
            P.barrier(all_dma_tokens())
            with nc.Block() as block:
                @block.sync
                def _(e):
                    P.emit("sp", e)

                @block.tensor
                def _(e):
                    P.emit("pe", e)

                @block.scalar
                def _(e):
                    P.emit("act", e)

                @block.vector
                def _(e):
                    P.emit("dve", e)

                @block.gpsimd
                def _(e):
                    P.emit("pool", e)

        es_s = ExitStack()
        es_s.__enter__()
        cur[0] = es_s
        c1rowf = sb("c1rowf", [1, 2, 32], F32)
        dtbrow = sb("dtbrow", [1, 2, 8], F32)
        alog_bc = sb("alog_bc", [128, 2, 8], F32)
        cwcol = sb("cwcol", [128, 4, 6], F32)
        grow = sb("grow", [1, 2, D], F32)
        badar = sb("badar", [1, 2, D], F32)


        def act(fn, reads, writes):
            return P.op("act", fn, reads, writes)

        def dve(fn, reads, writes):
            return P.op("dve", fn, reads, writes)

        def pool(fn, reads, writes):
            return P.op("pool", fn, reads, writes)

        def pe(fn, reads, writes):
            return P.op("pe", fn, reads, writes)

        def mm(out, lhsT, rhs, start, stop, reads, writes):
            return pe(lambda e: e.matmul(out, lhsT=lhsT, rhs=rhs, start=start, stop=stop), reads, writes)

        def tp(out, in_, reads, writes):
            return pe(lambda e: e.transpose(out, in_, cst[:, IDN, :]), reads + ["cst"], writes)

        def dma(q, out, in_, dsem, reads, writes, **kw):
            return P.dma(q, lambda e: e.dma_start(out=out, in_=in_, **kw), dsem, reads, writes)

        def activation(out, in_, func, reads, writes, bias=None, scale=None, accum_out=None):
            kw = {}
            if bias is not None:
                kw["bias"] = bias
            if scale is not None:
                kw["scale"] = scale
            if accum_out is not None:
                kw["accum_out"] = accum_out
            return act(lambda e: e.activation(out=out, in_=in_, func=func, **kw), reads, writes)

        dma("pool", cst[:, :, :], consts_d.ap().rearrange("p (a b) -> p a b", b=128), DS("cst"), [], ["cst"])
        dma("sp", cstf[:, 0, :], consts_d[:, 0:128], DS("cstf"), [], ["cstf"])
        dma("sp", cstf[:, 1, :], consts_d[:, 9 * 128:10 * 128], DS("cstf"), [], ["cstf"])
        dve(lambda e: e.memset(onesrow[:, :], 1.0), [], ["onesrow"])
        dve(lambda e: e.memset(onesrowf[:, :], 1.0), [], ["onesrowf"])
        sp_ = DS("small")
        for r_ in range(2):
            dma("sp", ccol[:, r_, :], cvec_d[r_].rearrange("(c p) -> p c", p=128), sp_, [], ["ccol"], allow_slow_non_contiguous=True)
        dma("sp", badac[:, :], bada_d.ap().rearrange("(c p) -> p c", p=128), sp_, [], ["badac"], allow_slow_non_contiguous=True)
        dma("sp", n1col[:, :], n1w_d.ap().rearrange("(c p) -> p c", p=128), sp_, [], ["n1col"], allow_slow_non_contiguous=True)
        dma("sp", n2col[:, :], n2w_d.ap().rearrange("(c p) -> p c", p=128), sp_, [], ["n2col"], allow_slow_non_contiguous=True)
        for j_ in range(4):
            dma("sp", cwcol[:, j_, :], cw_d[j_].rearrange("(c p) -> p c", p=128), sp_, [], ["cwcol"], allow_slow_non_contiguous=True)
        dma("sp", cbcol[:, :], cb_d.ap().rearrange("(c p) -> p c", p=128), sp_, [], ["cbcol"], allow_slow_non_contiguous=True)
        dma("sp", gnw_bc[:, :], gnw_d.ap().partition_broadcast(128), sp_, [], ["gnw_bc"])
        dma("sp", snw_bc[:, :], snw_d.ap().partition_broadcast(128), sp_, [], ["snw_bc"])
        dma("sp", dsk_bc[:, :], dsk_d.ap().partition_broadcast(128), sp_, [], ["dsk_bc"])
        dma("sp", alog_bc[:, :, :], alog_d.ap().partition_broadcast(128), sp_, [], ["alog_bc"])
        dma("sp", fnw_bc[:, :], fnw_d.ap().partition_broadcast(128), sp_, [], ["fnw_bc"])
        dma("sp", dtbrow[:, :, :], dtb_d.ap().rearrange("(o a) b -> o a b", o=1), sp_, [], ["dtbrow"])
        dma("sp", badar[:, 0, :], bada_d.ap().rearrange("(o n) -> o n", o=1)[:, 2 * D:3 * D], sp_, [], ["badar"])
        dma("sp", badar[:, 1, :], bada_d.ap().rearrange("(o n) -> o n", o=1)[:, 5 * D:6 * D], sp_, [], ["badar"])
        dma("pool", upsb[:, :, :], up_d.ap().rearrange("a r k -> r a k"), DS("cst"), [], ["upsb"])
        dma("pool", gbrow[:, :, :], gb_d.ap().rearrange("(o a) k -> o a k", o=1), DS("cst"), [], ["gbrow"])
        dma("pool", w1m[:, :, :], w1_d.ap().rearrange("(c p) n -> p c n", p=128), DS("w1"), [], ["w1m"])

        for nm_ in ("small", "cst", "cstf"):
            P.seal(DS(nm_))
        activation(scol[:, :, :], ccol[:, :, :], AF.Silu, ["ccol"], ["scol"])
        wab = [sb("wab%d" % i, [128, 8, 512], BF16) for i in range(2)]
        for blk in range(12):
            bi = blk % 2
            dma("pool", wab[bi][:, :, :], wada_d[:, blk * 512:(blk + 1) * 512].rearrange("(c p) n -> p c n", p=128),
                DS("wab%d" % bi), [], ["wab%d" % bi])
            for c4 in range(4):
                cc = blk * 4 + c4
                for kc in range(8):
                    mm(pA[:, cc * 2:cc * 2 + 2], wab[bi][:, kc, c4 * 128:(c4 + 1) * 128], scol[:, :, kc],
                       kc == 0, kc == 7, ["wab%d" % bi, "scol"], ["pA"])
            if blk in (4, 5, 10, 11):
                gi = 0 if blk < 6 else 1
                half = blk % 2
                for kc in range(8):
                    mm(pB[0:1, :], scol[:, 0, kc:kc + 1], wab[bi][:, kc, :], kc == 0, kc == 7, ["wab%d" % bi, "scol"], ["pB"])
                dve(lambda e, gi=gi, half=half: e.tensor_tensor(out=grow[:, gi, half * 512:(half + 1) * 512], in0=pB[0:1, :],
                                                               in1=badar[:, gi, half * 512:(half + 1) * 512], op=ALU.add),
                    ["pB", "badar"], ["grow"])
        dve(lambda e: e.tensor_tensor(out=ada_col[:, :, :], in0=pA[:, 0:96].rearrange("p (c r) -> p c r", r=2),
                                      in1=badac[:, :].unsqueeze(2).broadcast_to([128, 48, 2]), op=ALU.add),
            ["pA", "badac"], ["ada_col"])
        for mi, (ncol, scb, shb, r) in enumerate([(n1col, 8, 0, 0), (n1col, 8, 0, 1), (n2col, 32, 24, 0)]):
            dve(lambda e, mi=mi, ncol=ncol, scb=scb, r=r: e.scalar_tensor_tensor(
                out=wmod[:, :, mi], in0=ada_col[:, scb:scb + 8, r], scalar=1.0, in1=ncol[:, :], op0=ALU.add, op1=ALU.mult),
                ["ada_col", "n1col", "n2col"], ["wmod"])
            dve(lambda e, mi=mi, shb=shb, r=r: e.tensor_copy(out=shbf[:, :, mi], in_=ada_col[:, shb:shb + 8, r]),
                ["ada_col"], ["shbf"])
        for gi in range(2):
            for half in range(2):
                mm(pA[:, :], onesrowf[:, :], grow[:, gi, half * 512:(half + 1) * 512], True, True, ["onesrowf", "grow"], ["pA"])
                act(lambda e, gi=gi, half=half: e.copy(out=g_bc[:, gi, half * 512:(half + 1) * 512], in_=pA[:, :]), ["pA"], ["g_bc"])
        activation(negA[:, :, :], alog_bc[:, :, :], AF.Exp, ["alog_bc"], ["negA"])
        dve(lambda e: e.tensor_scalar(out=negA[:, :, :], in0=negA[:, :, :], scalar1=-1.0, scalar2=None, op0=ALU.mult), ["negA"], ["negA"])
        for cc in range(6):
            for j in range(4):
                dve(lambda e, cc=cc, j=j: e.tensor_scalar(out=cdiag[:, cc * 4 + j, :], in0=cst[:, IDN, :],
                                                          scalar1=cwcol[:, j, cc:cc + 1], scalar2=None, op0=ALU.mult),
                    ["cst", "cwcol"], ["cdiag"])
        FM_OFFS = [OQ, OK_, OXS, OXS + 128, OXS + 256, OXS + 384, OB, OC]
        for m in range(2):
            for gi_, off in enumerate(FM_OFFS):
                for kc in range(8):
                    mm(pG[:, 0, gi_ * 2 + m:gi_ * 2 + m + 1], w1m[:, kc, off:off + 128], shbf[:, kc, m:m + 1], kc == 0, kc == 7,
                       ["w1m", "shbf"], ["pG"])
        for m in range(2):
            dve(lambda e, m=m: e.tensor_copy(out=c1col[:, 0:8, m], in_=pG[:, 0, 0:16].rearrange("p (g m) -> p g m", m=2)[:, :, m]),
                ["pG"], ["c1col"])
        for m in range(2):
            for d_ in range(2):
                off = OLF if d_ == 0 else OLB
                for kc in range(8):
                    mm(pG[0:16, 1, m * 2 + d_:m * 2 + d_ + 1], w1m[:, kc, off:off + 16], shbf[:, kc, m:m + 1], kc == 0, kc == 7,
                       ["w1m", "shbf"], ["pG"])
        dve(lambda e: e.tensor_copy(out=c1lr[:, :, :], in_=pG[0:16, 1, 0:4].rearrange("p (m d) -> p m d", d=2)), ["pG"], ["c1lr"])
        for m in range(2):
            for (o0, n0, dst) in [(OK_, 384, 0), (OR, 256, 384), (OZ, 512, 640)]:
                for kc in range(8):
                    mm(pA[0:1, 0:n0], shbf[:, kc, m:m + 1], w1m[:, kc, o0:o0 + n0], kc == 0, kc == 7, ["w1m", "shbf"], ["pA"])
                act(lambda e, m=m, n0=n0, dst=dst: e.copy(out=c1row[:, m, dst:dst + n0], in_=pA[0:1, 0:n0]), ["pA"], ["c1row"])
            for kc in range(8):
                mm(pA[0:1, 0:16], shbf[:, kc, m:m + 1], w1m[:, kc, ODF:ODF + 16], kc == 0, kc == 7, ["w1m", "shbf"], ["pA"])
            dve(lambda e, m=m: e.tensor_tensor(out=c1rowf[:, m, 0:16].rearrange("o (a b) -> o a b", b=8),
                                               in0=pA[0:1, 0:16].rearrange("o (a b) -> o a b", b=8), in1=dtbrow[:, :, :], op=ALU.add),
                ["pA", "dtbrow"], ["c1rowf"])
            dve(lambda e, m=m: e.tensor_copy(out=dthi[:, m, :, :], in_=c1rowf[:, m, 0:16].rearrange("o (a b) -> o a b", b=8)),
                ["c1rowf"], ["dthi"])
            dve(lambda e, m=m: e.tensor_tensor(out=c1rowf[:, m, 16:32].rearrange("o (a b) -> o a b", b=8),
                                               in0=c1rowf[:, m, 0:16].rearrange("o (a b) -> o a b", b=8), in1=dthi[:, m, :, :], op=ALU.subtract),
                ["c1rowf", "dthi"], ["c1rowf2"])
            dve(lambda e, m=m: e.tensor_copy(out=dtlo[:, m, :, :], in_=c1rowf[:, m, 16:32].rearrange("o (a b) -> o a b", b=8)),
                ["c1rowf2"], ["dtlo"])
        def set_mod(m, reload):
            if reload:
                dma("pool", w1m[:, :, :], w1_d.ap().rearrange("(c p) n -> p c n", p=128), DS("w1"), [], ["w1m"])
            for kc in range(8):
                act(lambda e, m=m, kc=kc: e.activation(out=w1m[:, kc, :], in_=w1m[:, kc, :], func=AF.Identity, scale=wmod[:, kc, m:m + 1]),
                    ["w1m", "wmod"], ["w1m"])
        flush()
        es_s.close()
        cur[0] = es1
        STOP = dbg.get("_stop", (99,))[0] if dbg else 99
        xnT = sb("xnT", [128, 8, 512], BF16)
        qT = [sb("qT%d" % i_, [128, 512], BF16) for i_ in range(2)]
        kT = [sb("kT%d" % i_, [128, 512], BF16) for i_ in range(2)]
        lrT = [sb("lrT%d" % i_, [16, 512], BF16) for i_ in range(2)]
        upad = sb("upad", [128, 6, 8 * 67 + 8], BF16)
        xcT = [sb("xcT%d" % i_, [128, 4, 512], BF16) for i_ in range(2)]
        BT = [sb("BT%d" % i_, [128, 512], BF16) for i_ in range(2)]
        CT = [sb("CT%d" % i_, [128, 512], BF16) for i_ in range(2)]
        kv = [sb("kv%d" % i_, [128, 4, 384], BF16) for i_ in range(2)]
        r_s = [sb("r_s%d" % i_, [128, 4, 256], BF16) for i_ in range(2)]
        z_s = [sb("z_s%d" % i_, [128, 4, 512], BF16) for i_ in range(2)]
        dtr = [sb("dtr%d" % i_, [128, 4, 8], F32) for i_ in range(2)]
        e1 = sb("e1", [128, 128], F32)
        sp = sb("sp", [128, 128], BF16)
        Eq = sb("Eq", [128, 128], F32)
        Ek = sb("Ek", [128, 128], F32)
        Er = sb("Er", [128, 128], F32)
        dcol = sb("dcol", [128, 1], F32)
        qt_ = sb("qt_", [128, 128], BF16)
        kt_ = sb("kt_", [128, 128], BF16)
        kh_ = sb("kh_", [128, 128], BF16)
        attm = sb("attm", [128, 128], BF16)
        Sg = sb("Sg", [128, 256], F32)
        Sgb = sb("Sgb", [128, 256], BF16)
        e2 = sb("e2", [128, 8], F32)
        dtv = sb("dtv", [128, 8], F32)
        ldt = sb("ldt", [128, 8], F32)
        abf = sb("abf", [128, 8], BF16)
        nb = sb("nb", [128, 8], F32)
        E3 = sb("E3", [128, 3, 8], F32)
        gsc = sb("gsc", [128, 8], F32)
        Wt = sb("Wt", [128, 8, 128], F32)
        Lseg = sb("Lseg", [128, 8, 128], BF16)
        cbm = sb("cbm", [128, 128], F32)
        MT = sb("MT", [128, 8, 128], BF16)
        xtm = sb("xtm", [128, 512], BF16)
        xh = sb("xh", [128, 512], BF16)
        Btm = sb("Btm", [128, 128], BF16)
        t1 = sb("t1", [128, 512], F32)
        Ss = sb("Ss", [128, 512], F32)
        Ssb = sb("Ssb", [128, 512], BF16)
        outp = [sb("outp%d" % i, [128, 768], F32) for i in range(2)]
        prevt = [sb("prevt%d" % i, [128, 768], F32) for i in range(2)]
        og = sb("og", [128, 256], F32)
        ys = sb("ys", [128, 512], F32)
        xd = sb("xd", [128, 512], F32)
        oa = sb("oa", [128, 768], BF16)
        nrm = sb("nrm", [128, 4], F32)
        ext = [sb("ext%d" % i, [128, 6, 512], BF16) for i in range(2)]
        dve(lambda e: e.memset(upad[:, :, :], 0.0), [], ["upad"])
        state = {"xi": 0, "oi": 0, "pi": 0, "ei": 0}

        def norm_tile(src_ap, dst_T, col0, nsub_key):
            i = state["xi"] % 2
            state["xi"] += 1
            dma("sp", xt[i][:, :], src_ap, DS("xt%d" % i), [], ["xt%d" % i])
            activation(sq_junk[:, :], xt[i][:, :], AF.Square, ["xt%d" % i], ["sq_junk", "ss"], accum_out=ss[:, 0:1])
            activation(ss[:, 1:2], ss[:, 0:1], AF.Ln, ["ss", "epsc"], ["ss"], scale=1.0 / D, bias=epsc[:, 0:1])
            activation(ss[:, 2:3], ss[:, 1:2], AF.Exp, ["ss"], ["ss"], scale=-0.5)
            act(lambda e, i=i: e.activation(out=xs_[i][:, :], in_=xt[i][:, :], func=AF.Identity, scale=ss[:, 2:3]),
                 ["xt%d" % i, "ss"], ["xs%d" % i])
            for kc in range(8):
                tp(pT[:, kc, :], xs_[i][:, kc * 128:(kc + 1) * 128], ["xs%d" % i], ["pT"])
            act(lambda e: e.copy(out=dst_T[:, :, col0:col0 + 128], in_=pT[:, :, :]), ["pT"], [nsub_key])

        dve(lambda e: e.memset(epsc[:, 0:1], EPS), [], ["epsc"])
        dve(lambda e: e.memset(epsc[:, 1:2], 1.0), [], ["epsc"])

        def project_block(m, dir_, ntok, rowlen, need_rz, bset):
            W = w1m
            wk = "w1m"
            nrow = ntok // rowlen
            banks = [pA, pTf]
            bk = [0]

            def nb_():
                b = banks[bk[0] % 2]
                k = "pA" if bk[0] % 2 == 0 else "pT"
                bk[0] += 1
                return b, k
            for gi_, off in enumerate(FM_OFFS):
                b, k = nb_()
                for kc in range(8):
                    mm(b[:, 0:ntok], W[:, kc, off:off + 128], xnT[:, kc, 0:ntok], kc == 0, kc == 7, [wk, "xnT"], [k])
                if gi_ < 2:
                    dst = (qT[bset] if gi_ == 0 else kT[bset])
                    activation(dst[:, 0:ntok], b[:, 0:ntok], AF.Identity, [k, "c1col"], ["qT%d" % bset if gi_ == 0 else "kT%d" % bset],
                               bias=c1col[:, gi_, m:m + 1])
                else:
                    cc = gi_ - 2
                    if rowlen == 64:
                        dst = upad[:, cc, 0:nrow * 67].rearrange("p (r c) -> p r c", c=67)[:, :, 2:66]
                        src = b[:, 0:ntok].rearrange("p (r c) -> p r c", c=64)
                    else:
                        dst = upad[:, cc, 2:2 + ntok]
                        src = b[:, 0:ntok]
                    activation(dst, src, AF.Identity, [k, "c1col"], ["upad"], bias=c1col[:, gi_, m:m + 1])
                yield
            off = OLF if dir_ == 0 else OLB
            b, k = nb_()
            for kc in range(8):
                mm(b[0:16, 0:ntok], W[:, kc, off:off + 16], xnT[:, kc, 0:ntok], kc == 0, kc == 7, [wk, "xnT"], [k])
            activation(lrT[bset][:, 0:ntok], b[0:16, 0:ntok], AF.Identity, [k, "c1lr"], ["lrT%d" % bset], bias=c1lr[:, m, dir_:dir_ + 1])
            yield
            for cc in range(6):
                b, k = nb_()
                for j in range(4):
                    if rowlen == 64:
                        rhs = upad[:, cc, 0:nrow * 67].rearrange("p (r c) -> p r c", c=67)[:, :, j:j + 64]
                        o_ = b[:, 0:ntok].rearrange("p (r c) -> p r c", c=64)
                    else:
                        rhs = upad[:, cc, j:j + ntok]
                        o_ = b[:, 0:ntok]
                    mm(o_, cdiag[:, cc * 4 + j, :], rhs, j == 0, j == 3, ["cdiag", "upad"], [k])
                dst = xcT[bset][:, cc, 0:ntok] if cc < 4 else (BT[bset][:, 0:ntok] if cc == 4 else CT[bset][:, 0:ntok])
                dk = "xcT%d" % bset if cc < 4 else ("BT%d" % bset if cc == 4 else "CT%d" % bset)
                activation(dst, b[:, 0:ntok], AF.Silu, [k, "cbcol"], [dk], bias=cbcol[:, cc:cc + 1])
                yield
            for s_ in range(ntok // 128):
                tsl = slice(s_ * 128, (s_ + 1) * 128)
                groups = [(OK_, 384, 0, "kv%d" % bset)]
                if need_rz:
                    groups += [(OR, 256, 384, "r"), (OZ, 512, 640, "z")]
                for (o0, n0, c0, kind) in groups:
                    b, k = nb_()
                    for kc in range(8):
                        mm(b[:, 0:n0], xnT[:, kc, tsl], W[:, kc, o0:o0 + n0], kc == 0, False, [wk, "xnT"], [k])
                    mm(b[:, 0:n0], onesrow[:, 0:128], c1row[:, m, c0:c0 + n0], False, True, ["onesrow", "c1row"], [k])
                    if kind == "kv%d" % bset:
                        act(lambda e, b=b, s_=s_: e.copy(out=kv[bset][:, s_, :], in_=b[:, 0:384]), [k], ["kv%d" % bset])
                    elif kind == "r":
                        activation(r_s[bset][:, s_, :], b[:, 0:256], AF.Silu, [k], ["r_s%d" % bset])
                    else:
                        activation(z_s[bset][:, s_, :], b[:, 0:512], AF.Silu, [k], ["z_s%d" % bset])
                    yield
                b, k = nb_()
                od = ODF if dir_ == 0 else ODB
                for kc in range(8):
                    mm(b[:, 0:8], xnT[:, kc, tsl], W[:, kc, od:od + 8], kc == 0, False, [wk, "xnT"], [k])
                mm(b[:, 0:8], onesrow[:, 0:128], dthi[:, m, dir_, :], False, False, ["onesrow", "dthi"], [k])
                mm(b[:, 0:8], onesrow[:, 0:128], dtlo[:, m, dir_, :], False, True, ["onesrow", "dtlo"], [k])
                dve(lambda e, b=b, s_=s_: e.tensor_copy(out=dtr[bset][:, s_, :], in_=b[:, 0:8]), [k], ["dtr%d" % bset])
                yield

        def gla_chunk(c, dir_, bset):
            tsl = slice(c * 128, (c + 1) * 128)
            M16 = cst[:, MF16 if dir_ == 0 else MB16, :]
            S16 = cst[:, SF16 if dir_ == 0 else SB16, :]
            MK = cst[:, MF if dir_ == 0 else MB, :]
            last = 127 if dir_ == 0 else 0
            mm(pG[:, 0, :], lrT[bset][:, tsl], upsb[:, dir_, :], True, False, ["lrT%d" % bset, "upsb"], ["pG0"])
            mm(pG[:, 0, :], onesrow[:, 0:128], gbrow[:, dir_, :], False, True, ["onesrow", "gbrow"], ["pG0"])
            yield
            activation(e1[:, :], pG[:, 0, :], AF.Exp, ["pG0"], ["e1"], scale=-1.0)
            activation(sp[:, :], e1[:, :], AF.Ln, ["e1", "epsc"], ["sp"], bias=epsc[:, 1:2])
            yield
            mm(pG[:, 1, :], sp[:, :], M16, True, True, ["sp", "cst"], ["pG1"])
            mm(pG[:, 2, :], S16, sp[:, :], True, True, ["sp", "cst"], ["pG2"])
            yield
            activation(Eq[:, :], pG[:, 1, :], AF.Exp, ["pG1"], ["Eq"])
            activation(Ek[:, :], pG[:, 1, :], AF.Exp, ["pG1"], ["Ek"], scale=-1.0)
            activation(Er[:, :], pG[:, 2, :], AF.Exp, ["pG2"], ["Er"])
            activation(dcol[:, :], pG[:, 1, last:last + 1], AF.Exp, ["pG1"], ["dcol"])
            yield
            dve(lambda e: e.scalar_tensor_tensor(out=qt_[:, :], in0=qT[bset][:, tsl], scalar=128.0 ** -0.5, in1=Eq[:, :], op0=ALU.mult, op1=ALU.mult),
                ["qT%d" % bset, "Eq"], ["qt_"])
            dve(lambda e: e.tensor_tensor(out=kt_[:, :], in0=kT[bset][:, tsl], in1=Ek[:, :], op=ALU.mult), ["kT%d" % bset, "Ek"], ["kt_"])
            dve(lambda e: e.tensor_tensor(out=kh_[:, :], in0=kv[bset][:, c, 0:128], in1=Er[:, :], op=ALU.mult), ["kv%d" % bset, "Er"], ["kh_"])
            yield
            mm(pG[:, 3, :], kt_[:, :], qt_[:, :], True, True, ["kt_", "qt_"], ["pG3"])
            yield
            dve(lambda e: e.tensor_tensor(out=attm[:, :], in0=pG[:, 3, :], in1=MK, op=ALU.mult), ["pG3", "cst"], ["attm"])
            yield
            mm(pO[:, 0, :], attm[:, :], kv[bset][:, c, 128:384], True, False, ["attm", "kv%d" % bset], ["pO0"])
            mm(pO[:, 0, :], qt_[:, :], Sgb[:, :], False, True, ["qt_", "Sgb"], ["pO0"])
            mm(pO[:, 1, :], kh_[:, :], kv[bset][:, c, 128:384], True, True, ["kh_", "kv%d" % bset], ["pO1"])
            yield
            dve(lambda e: e.scalar_tensor_tensor(out=Sg[:, :], in0=Sg[:, :], scalar=dcol[:, 0:1], in1=pO[:, 1, :], op0=ALU.mult, op1=ALU.add),
                ["Sg", "dcol", "pO1"], ["Sg"])
            yield
            act(lambda e: e.copy(out=Sgb[:, :], in_=Sg[:, :]), ["Sg"], ["Sgb"])
            yield

        pSf = pS
        pSb = None

        def ssd_chunk(c, dir_, bset):
            tsl = slice(c * 128, (c + 1) * 128)
            MK = cst[:, MF if dir_ == 0 else MB, :]
            SK = cst[:, SF if dir_ == 0 else SB_, :]
            activation(e2[:, :], dtr[bset][:, c, :], AF.Exp, ["dtr%d" % bset], ["e2"])
            activation(dtv[:, :], e2[:, :], AF.Ln, ["e2", "epsc"], ["dtv"], bias=epsc[:, 1:2])
            activation(ldt[:, :], dtv[:, :], AF.Ln, ["dtv"], ["ldt"])
            yield
            dve(lambda e: e.tensor_tensor(out=abf[:, :], in0=dtv[:, :], in1=negA[:, dir_, :], op=ALU.mult), ["dtv", "negA"], ["abf"])
            dve(lambda e: e.tensor_tensor(out=Lseg[:, :, :], in0=SK.unsqueeze(1).broadcast_to([128, 8, 128]),
                                          in1=abf[:, :].unsqueeze(2).broadcast_to([128, 8, 128]), op=ALU.mult), ["abf", "cst"], ["Lseg"])
            yield
            for h in range(4):
                mm(pC[:, h, :], Lseg[:, h, :], MK, True, True, ["Lseg", "cst"], ["pC"])
            yield
            mm(pS[:, 128:136], MK, abf[:, :], True, True, ["abf", "cst"], ["pSs"])
            mm(pS[:, 136:144], SK, abf[:, :], True, True, ["abf", "cst"], ["pSs"])
            mm(pS[:, 144:152], cst[:, ONES, :], abf[:, :], True, True, ["abf", "cst"], ["pSs"])
            yield
            activation(E3[:, :, :], pS[:, 128:152].rearrange("p (a b) -> p a b", b=8), AF.Exp, ["pSs"], ["E3"])
            yield
            dve(lambda e: e.tensor_tensor(out=gsc[:, :], in0=E3[:, 1, :], in1=dtv[:, :], op=ALU.mult), ["E3", "dtv"], ["gsc"])
            yield
            for h in range(4):
                activation(Wt[:, h, :], pC[:, h, :], AF.Exp, ["pC", "ldt"], ["Wt"], bias=ldt[:, h:h + 1])
            yield
            for h in range(4, 8):
                mm(pC[:, h - 4, :], Lseg[:, h, :], MK, True, True, ["Lseg", "cst"], ["pC"])
            yield
            for h in range(4, 8):
                activation(Wt[:, h, :], pC[:, h - 4, :], AF.Exp, ["pC", "ldt"], ["Wt"], bias=ldt[:, h:h + 1])
            yield
            mm(pS[:, 0:128], BT[bset][:, tsl], CT[bset][:, tsl], True, True, ["BT%d" % bset, "CT%d" % bset], ["pSc"])
            yield
            dve(lambda e: e.tensor_tensor(out=cbm[:, :], in0=pS[:, 0:128], in1=MK, op=ALU.mult), ["pSc", "cst"], ["cbm"])
            dve(lambda e: e.tensor_tensor(out=MT[:, :, :], in0=Wt[:, :, :],
                                          in1=cbm[:, :].unsqueeze(1).broadcast_to([128, 8, 128]), op=ALU.mult),
                ["Wt", "cbm"], ["MT"])
            yield
            for cc in range(4):
                tp(pSx[:, cc * 128:(cc + 1) * 128], xcT[bset][:, cc, tsl], ["xcT%d" % bset], ["pS"])
            yield
            tp(pSx[:, 512:640], BT[bset][:, tsl], ["BT%d" % bset], ["pS"])
            yield
            act(lambda e: e.copy(out=xtm[:, :], in_=pSx[:, 0:512]), ["pS"], ["xtm"])
            yield
            dve(lambda e: e.tensor_tensor(out=xh[:, :].rearrange("p (h q) -> p h q", q=64),
                                          in0=pSx[:, 0:512].rearrange("p (h q) -> p h q", q=64),
                                          in1=gsc[:, :].unsqueeze(2).broadcast_to([128, 8, 64]), op=ALU.mult), ["pS", "gsc"], ["xh"])
            yield
            act(lambda e: e.copy(out=Btm[:, :], in_=pSx[:, 512:640]), ["pS"], ["Btm"])
            yield
            for h in range(8):
                mm(pY[:, h * 64:(h + 1) * 64], MT[:, h, :], xtm[:, h * 64:(h + 1) * 64], True, True, ["MT", "xtm"], ["pY"])
            yield
            mm(pB[:, :], CT[bset][:, tsl], Ssb[:, :], True, True, ["CT%d" % bset, "Ssb"], ["pB"])
            yield
            dve(lambda e: e.tensor_tensor(out=t1[:, :].rearrange("p (h q) -> p h q", q=64),
                                          in0=pB[:, :].rearrange("p (h q) -> p h q", q=64),
                                          in1=E3[:, 0, :].unsqueeze(2).broadcast_to([128, 8, 64]), op=ALU.mult), ["pB", "E3"], ["t1"])
            yield
            mm(pB[:, :], Btm[:, :], xh[:, :], True, True, ["Btm", "xh"], ["pB"])
            yield
            dve(lambda e: e.tensor_tensor(out=Ss[:, :].rearrange("p (h q) -> p h q", q=64),
                                           in0=Ss[:, :].rearrange("p (h q) -> p h q", q=64),
                                           in1=E3[:, 2, :].unsqueeze(2).broadcast_to([128, 8, 64]), op=ALU.mult), ["Ss", "E3"], ["Ss"])
            dve(lambda e: e.tensor_tensor(out=Ss[:, :], in0=Ss[:, :], in1=pB[:, :], op=ALU.add), ["Ss", "pB"], ["Ss"])
            yield
            act(lambda e: e.copy(out=Ssb[:, :], in_=Ss[:, :]), ["Ss"], ["Ssb"])

        def interleave(*gens):
            gens = list(gens)
            while gens:
                for g in list(gens):
                    try:
                        next(g)
                    except StopIteration:
                        gens.remove(g)

        def out_pass1(c, tok0):
            i = state["oi"] % 2
            state["oi"] += 1
            act(lambda e: e.copy(out=outp[i][:, 0:256], in_=pO[:, 0, :]), ["pO0"], ["outp%d" % i])
            dve(lambda e: e.tensor_tensor(out=outp[i][:, 256:768], in0=t1[:, :], in1=pY[:, :], op=ALU.add), ["t1", "pY"], ["outp%d" % i])
            dma("sp", prev_d[tok0:tok0 + 128, :], outp[i][:, :], DS("outp%d" % i), ["outp%d" % i], [("prev", tok0)])

        def out_pass2(c, tok0, st_idx, bset):
            i = state["pi"] % 2
            state["pi"] += 1
            ei = st_idx % 2
            dma("sp", prevt[i][:, :], prev_d[tok0:tok0 + 128, :], DS("prevt%d" % i), [("prev", tok0)], ["prevt%d" % i])
            dve(lambda e: e.tensor_tensor(out=og[:, :], in0=pO[:, 0, :], in1=prevt[i][:, 0:256], op=ALU.add), ["pO0", "prevt%d" % i], ["og"])
            activation(sq_junk[:, 0:256], og[:, :], AF.Square, ["og"], ["sq_junk", "nrm"], accum_out=nrm[:, 0:1])
            activation(nrm[:, 1:2], nrm[:, 0:1], AF.Ln, ["nrm", "epsc"], ["nrm"], scale=1.0 / 256, bias=epsc[:, 0:1])
            activation(nrm[:, 1:2], nrm[:, 1:2], AF.Exp, ["nrm"], ["nrm"], scale=-0.5)
            dve(lambda e: e.scalar_tensor_tensor(out=og[:, :], in0=og[:, :], scalar=nrm[:, 1:2], in1=gnw_bc[:, :], op0=ALU.mult, op1=ALU.mult),
                ["og", "nrm", "gnw_bc"], ["og"])
            dve(lambda e: e.tensor_tensor(out=oa[:, 0:256], in0=og[:, :], in1=r_s[bset][:, c, :], op=ALU.mult), ["og", "r_s%d" % bset], ["oa"])
            dve(lambda e: e.tensor_tensor(out=xd[:, :].rearrange("p (h q) -> p h q", q=64),
                                           in0=xtm[:, :].rearrange("p (h q) -> p h q", q=64),
                                           in1=dsk_bc[:, :].unsqueeze(2).broadcast_to([128, 8, 64]), op=ALU.mult), ["xtm", "dsk_bc"], ["xd"])
            dve(lambda e: e.tensor_tensor(out=xd[:, :], in0=xd[:, :], in1=prevt[i][:, 256:768], op=ALU.add), ["xd", "prevt%d" % i], ["xd"])
            dve(lambda e: e.tensor_tensor(out=ys[:, :], in0=t1[:, :], in1=pY[:, :], op=ALU.add), ["t1", "pY"], ["ys"])
            dve(lambda e: e.tensor_tensor(out=ys[:, :], in0=ys[:, :], in1=xd[:, :], op=ALU.add), ["ys", "xd"], ["ys"])
            dve(lambda e: e.tensor_tensor(out=ys[:, :], in0=ys[:, :], in1=z_s[bset][:, c, :], op=ALU.mult), ["ys", "z_s%d" % bset], ["ys"])
            activation(sq_junk[:, 0:512], ys[:, :], AF.Square, ["ys"], ["sq_junk", "nrm"], accum_out=nrm[:, 2:3])
            activation(nrm[:, 3:4], nrm[:, 2:3], AF.Ln, ["nrm", "epsc"], ["nrm"], scale=1.0 / 512, bias=epsc[:, 0:1])
            activation(nrm[:, 3:4], nrm[:, 3:4], AF.Exp, ["nrm"], ["nrm"], scale=-0.5)
            dve(lambda e: e.scalar_tensor_tensor(out=oa[:, 256:768], in0=ys[:, :], scalar=nrm[:, 3:4], in1=snw_bc[:, :], op0=ALU.mult, op1=ALU.mult),
                ["ys", "nrm", "snw_bc"], ["oa"])
            if tok0 // 512 == 11:
                tap("oa_bf", oa[:, :], ["oa"], idx=c)
                tap("ogys", og[:, :], ["og"], idx=(c, slice(None), slice(0, 256)))
            for cc in range(6):
                tp(pCb[:, cc * 128:(cc + 1) * 128], oa[:, cc * 128:(cc + 1) * 128], ["oa"], ["pC"])
            act(lambda e: e.copy(out=ext[ei][:, :, c * 128:(c + 1) * 128], in_=pCb[:, 0:768].rearrange("p (a b) -> p a b", b=128)), ["pC"], ["ext%d" % ei])

        Sg0s = sb("Sg0s", [128, 256], F32)
        Ss0s = sb("Ss0s", [128, 512], F32)

        def drain(g):
            if g is None:
                return
            for _ in g:
                pass

        def proj_gen(m, dir_, rows_fn, ntiles, rowlen, need_rz, bset):
            for s_ in range(ntiles):
                norm_tile(rows_fn(s_), xnT, s_ * 128, "xnT")
                yield
            yield from project_block(m, dir_, ntiles * 128, rowlen, need_rz, bset)

        def zero_states():
            dve(lambda e: e.memset(Sg[:, :], 0.0), ["Sg"], ["Sg"])
            dve(lambda e: e.memset(Sgb[:, :], 0.0), ["Sgb"], ["Sgb"])
            dve(lambda e: e.memset(Ss[:, :], 0.0), ["Ss"], ["Ss"])
            dve(lambda e: e.memset(Ssb[:, :], 0.0), ["Ssb"], ["Ssb"])

        def run_ctx(dir_):
            zero_states()
            drain(proj_gen(1, dir_, lambda s_: ctx_d[s_ * 128:(s_ + 1) * 128, :], 2, 256, False, 0))
            for c in ([0, 1] if dir_ == 0 else [1, 0]):
                interleave(gla_chunk(c, dir_, 0), ssd_chunk(c, dir_, 0))
            tap("Sg_ctx%d" % dir_, Sg[:, :], ["Sg"])
            tap("Ss_ctx%d" % dir_, Ss[:, :], ["Ss"])

        def run_latent(dir_):
            sts = list(range(NST_TOTAL)) if dir_ == 0 else list(range(NST_TOTAL - 1, -1, -1))
            sts = sts[:NST_RUN]
            rows = lambda st: (lambda s_: x_d[st * 512 + s_ * 128:st * 512 + (s_ + 1) * 128, :])
            bset = 0
            drain(proj_gen(0, dir_, rows(sts[0]), 4, 64, dir_ == 0, bset))
            for i, st in enumerate(sts):
                tok0 = st * 512
                bg = proj_gen(0, dir_, rows(sts[i + 1]), 4, 64, dir_ == 0, 1 - bset) if i + 1 < len(sts) else None
                for c in ([0, 1, 2, 3] if dir_ == 0 else [3, 2, 1, 0]):
                    gens = [gla_chunk(c, dir_, bset), ssd_chunk(c, dir_, bset)]
                    while gens:
                        for g in list(gens):
                            try:
                                next(g)
                            except StopIteration:
                                gens.remove(g)
                        if bg is not None:
                            try:
                                next(bg)
                            except StopIteration:
                                bg = None
                    if dir_ == 1:
                        out_pass1(c, tok0 + c * 128)
                    else:
                        out_pass2(c, tok0 + c * 128, st, bset)
                drain(bg)
                if dir_ == 0:
                    ei = st % 2
                    dma("sp", ex_d[st].rearrange("(c p) t -> p c t", p=128), ext[ei][:, :, :], DS("ext%d" % ei),
                        ["ext%d" % ei], [("ex", st)])
                    if STOP >= 3:
                        P.cc("pool", lambda e, st=st: e.collective_compute(
                            "AllGather", ALU.bypass, replica_groups=[[0, 1, 2, 3], [4, 5, 6, 7]],
                            ins=[ex_d[st]], outs=[gx_d[st]]), ccs, [("ex", st)], ["gx"])
                bset = 1 - bset

        def run_all():
            set_mod(1, False)
            dve(lambda e: e.memset(upad[:, :, :], 0.0), ["upad"], ["upad"])
            run_ctx(0)
            dve(lambda e: e.tensor_copy(out=Sg0s[:, :], in_=Sg[:, :]), ["Sg"], ["Sg0s"])
            dve(lambda e: e.tensor_copy(out=Ss0s[:, :], in_=Ss[:, :]), ["Ss"], ["Ss0s"])
            run_ctx(1)
            set_mod(0, True)
            dve(lambda e: e.memset(upad[:, :, :], 0.0), ["upad"], ["upad"])
            if STOP >= 1:
                run_latent(1)
                tap("prev", prev_d[7680:8192, :], [("prev", 7680 + i * 128) for i in range(4)])
            if STOP >= 2:
                dve(lambda e: e.tensor_copy(out=Sg[:, :], in_=Sg0s[:, :]), ["Sg0s", "Sg"], ["Sg"])
                dve(lambda e: e.tensor_copy(out=Ss[:, :], in_=Ss0s[:, :]), ["Ss0s", "Ss"], ["Ss"])
                dve(lambda e: e.tensor_copy(out=Sgb[:, :], in_=Sg0s[:, :]), ["Sg0s", "Sgb"], ["Sgb"])
                dve(lambda e: e.tensor_copy(out=Ssb[:, :], in_=Ss0s[:, :]), ["Ss0s", "Ssb"], ["Ssb"])
                run_latent(0)

        NST_TOTAL = SEQ // 512
        NST_RUN = NST_TOTAL if not dbg else dbg.get("_nst", (NST_TOTAL,))[0]
        tap("ada_col", ada_col[:, :, :].rearrange("p a b -> p (a b)"), ["ada_col"])
        run_all()

        flush()
        es1.close()
        if STOP < 4:
            return nc
        es2 = ExitStack()
        es2.__enter__()
        cur[0] = es2
        wgs = sb("wgs", [128, 8, 2 * D], BF16)
        wpa = sb("wpa", [128, 8, D], BF16)
        wpb = sb("wpb", [128, 16, D], BF16)
        wo = sb("wo", [128, 8, D], BF16)
        cgc = sb("cgc", [128, 16], F32)
        xnT2 = sb("xnT2", [128, 8, 512], BF16)
        oaT = sb("oaT", [128, 8, 512], BF16)
        obT = sb("obT", [128, 16, 512], BF16)
        gT = sb("gT", [128, 16, 512], BF16)
        mg = sb("mg", [128, 8, 512], BF16)
        tA = sb("tA", [128, 512], F32)
        tB = sb("tB", [128, 512], F32)
        hrow = xt
        xrow = sb("xrow", [128, 4, D], F32)

        dma("pool", wgs[:, :, :], wg_d.ap().rearrange("(c p) n -> p c n", p=128), DS("wgs"), [], ["wgs"])
        dma("pool", wpa[:, :, :], wpa_d.ap().rearrange("(c p) n -> p c n", p=128), DS("wpa"), [], ["wpa"])
        dma("pool", wpb[:, :, :], wpb_d.ap().rearrange("(c p) n -> p c n", p=128), DS("wpb"), [], ["wpb"])
        dma("pool", wo[:, :, :], wout_d.ap().rearrange("(c p) n -> p c n", p=128), DS("wo"), [], ["wo"])
        for fc in range(16):
            for kc in range(8):
                mm(pG[:, 0, fc:fc + 1], wgs[:, kc, fc * 128:(fc + 1) * 128], shbf[:, kc, 0:1], kc == 0, kc == 7, ["wgs", "shbf"], ["pG"])
        dve(lambda e: e.tensor_copy(out=cgc[:, :], in_=pG[:, 0, 0:16]), ["pG"], ["cgc"])
        for kc in range(8):
            act(lambda e, kc=kc: e.activation(out=wgs[:, kc, :], in_=wgs[:, kc, :], func=AF.Identity, scale=wmod[:, kc, 0:1]),
                 ["wgs", "wmod", "cgc"], ["wgs"])

        def norm_rows(src_tile, key, dst_T, col0, dkey):
            activation(sq_junk[:, :], src_tile, AF.Square, [key], ["sq_junk", "ss"], accum_out=ss[:, 0:1])
            activation(ss[:, 1:2], ss[:, 0:1], AF.Ln, ["ss", "epsc"], ["ss"], scale=1.0 / D, bias=epsc[:, 0:1])
            activation(ss[:, 2:3], ss[:, 1:2], AF.Exp, ["ss"], ["ss"], scale=-0.5)
            act(lambda e: e.activation(out=xs_[0][:, :], in_=src_tile, func=AF.Identity, scale=ss[:, 2:3]), [key, "ss"], ["xs0"])
            for kc in range(8):
                tp(pT[:, kc, :], xs_[0][:, kc * 128:(kc + 1) * 128], ["xs0"], ["pT"])
            act(lambda e: e.copy(out=dst_T[:, :, col0:col0 + 128], in_=pT[:, :, :]), ["pT"], [dkey])

        BK4 = [(pA, "pA"), (pB, "pB"), (pY, "pY"), (pS, "pS")]
        BK2 = [((pA, "pA"), (pB, "pB")), ((pY, "pY"), (pS, "pS"))]
        for s4 in range(4):
            def fn_q(e, s4=s4):
                q = nc.partition_id([mybir.EngineType.SP]) % 4
                return e.dma_start(out=gxq_d[s4], in_=gx_d[q * 4 + s4])
            P.dma("sp", fn_q, DS("gxq"), ["gx"], ["gxq"])
        for blk in range(4):
            t0 = blk * 512
            for s_ in range(4):
                dma("sp", xrow[:, s_, :], xq_d[t0 + s_ * 128:t0 + (s_ + 1) * 128, :], DS("xrow"), [], ["xrow"])
            for s_ in range(4):
                norm_rows(xrow[:, s_, :], "xrow", xnT2, s_ * 128, "xnT2")
            for r in range(4):
                dma("sp", oaT[:, 2 * r:2 * r + 2, :], gxq_d[blk, r * 768:r * 768 + 256, :].rearrange("(c p) t -> p c t", p=128),
                    DS("oaT"), ["gxq"], ["oaT"])
                dma("sp", obT[:, 4 * r:4 * r + 4, :], gxq_d[blk, r * 768 + 256:r * 768 + 768, :].rearrange("(c p) t -> p c t", p=128),
                    DS("obT"), ["gxq"], ["obT"])
            if blk == 3:
                tap("oaT_bf", oaT[:, :, :], ["oaT"])
                tap("obT_bf", obT[:, :, :], ["obT"])
                tap("xnT2_bf", xnT2[:, :, :], ["xnT2"])
            for fc in range(16):
                b, k = BK4[fc % 4]
                for kc in range(8):
                    mm(b[:, :], wgs[:, kc, fc * 128:(fc + 1) * 128], xnT2[:, kc, :], kc == 0, kc == 7, ["wgs", "xnT2"], [k])
                activation(gT[:, fc, :], b[:, :], AF.Sigmoid, [k, "cgc"], ["gT"], bias=cgc[:, fc:fc + 1])
            for mc in range(8):
                (bA, kA), (bB, kB) = BK2[mc % 2]
                for kc in range(8):
                    mm(bA[:, :], wpa[:, kc, mc * 128:(mc + 1) * 128], oaT[:, kc, :], kc == 0, kc == 7, ["wpa", "oaT"], [kA])
                for kc in range(16):
                    mm(bB[:, :], wpb[:, kc, mc * 128:(mc + 1) * 128], obT[:, kc, :], kc == 0, kc == 15, ["wpb", "obT"], [kB])
                dve(lambda e, mc=mc, bA=bA: e.tensor_tensor(out=tA[:, :], in0=bA[:, :], in1=gT[:, mc, :], op=ALU.mult), [kA, "gT"], ["tA"])
                dve(lambda e, mc=mc, bB=bB: e.tensor_tensor(out=tB[:, :], in0=bB[:, :], in1=gT[:, 8 + mc, :], op=ALU.mult), [kB, "gT"], ["tB"])
                dve(lambda e, mc=mc: e.tensor_tensor(out=mg[:, mc, :], in0=tA[:, :], in1=tB[:, :], op=ALU.add), ["tA", "tB"], ["mg"])
            if blk == 3:
                tap("gT_bf", gT[:, :, :], ["gT"])
                tap("mg_bf", mg[:, :, :], ["mg"])
            for s_ in range(4):
                hi = (blk * 4 + s_) % 2
                for half in range(2):
                    b, k = BK4[(s_ * 2 + half) % 4]
                    for kc in range(8):
                        mm(b[:, :], mg[:, kc, s_ * 128:(s_ + 1) * 128], wo[:, kc, half * 512:(half + 1) * 512], kc == 0, kc == 7, ["mg", "wo"], [k])
                    dve(lambda e, b=b, half=half, hi=hi: e.tensor_tensor(out=hrow[hi][:, half * 512:(half + 1) * 512], in0=b[:, :],
                                                                        in1=g_bc[:, 0, half * 512:(half + 1) * 512], op=ALU.mult),
                        [k, "g_bc"], ["xt%d" % hi])
                    dve(lambda e, half=half, hi=hi, s_=s_: e.tensor_tensor(out=hrow[hi][:, half * 512:(half + 1) * 512],
                                                                           in0=hrow[hi][:, half * 512:(half + 1) * 512],
                                                                           in1=xrow[:, s_, half * 512:(half + 1) * 512], op=ALU.add),
                         ["xt%d" % hi, "xrow"], ["xt%d" % hi])
                dma("sp", h_d[t0 + s_ * 128:t0 + (s_ + 1) * 128, :], hrow[hi][:, :], DS("xt%d" % hi), ["xt%d" % hi], [("h", t0 + s_ * 128)])

        tap("hq", h_d[1536:2048, :], [("h", 1536 + i * 128) for i in range(4)])
        flush()
        es2.close()
        if STOP < 5:
            return nc
        es3 = ExitStack()
        es3.__enter__()
        cur[0] = es3
        xnT2 = sb("xnT2b", [128, 8, 256], BF16)
        xrow = sb("xrowb", [128, 2, D], F32)
        wga = sb("wga", [128, 8, DFF], BF16)
        wu = sb("wu", [128, 8, DFF], BF16)
        wd = sb("wd", [128, 22, D], BF16)
        c2c = sb("c2c", [128, 22, 2], F32)
        uT = sb("uT", [128, 22, 256], BF16)
        sgt2 = [sb("sgt%d" % i_, [128, 256], F32) for i_ in range(2)]
        yrow = sb("yrow", [128, D], F32)
        dma("pool", wga[:, :, :], wgate_d.ap().rearrange("(c p) n -> p c n", p=128), DS("wga"), [], ["wga"])
        dma("pool", wu[:, :, :], wup_d.ap().rearrange("(c p) n -> p c n", p=128), DS("wu"), [], ["wu"])
        dma("pool", wd[:, :, :], wdown_d.ap().rearrange("(c p) n -> p c n", p=128), DS("wd"), [], ["wd"])
        for wi, (wt_, wk) in enumerate([(wga, "wga"), (wu, "wu")]):
            for fc in range(22):
                for kc in range(8):
                    mm(pG[:, 0, fc * 2 + wi:fc * 2 + wi + 1], wt_[:, kc, fc * 128:(fc + 1) * 128], shbf[:, kc, 2:3], kc == 0, kc == 7,
                       [wk, "shbf"], ["pG"])
        dve(lambda e: e.tensor_copy(out=c2c[:, :, :], in_=pG[:, 0, 0:44].rearrange("p (f w) -> p f w", w=2)), ["pG"], ["c2c"])
        for (wt_, wk) in [(wga, "wga"), (wu, "wu")]:
            for kc in range(8):
                act(lambda e, wt_=wt_, kc=kc: e.activation(out=wt_[:, kc, :], in_=wt_[:, kc, :], func=AF.Identity, scale=wmod[:, kc, 2:3]),
                     [wk, "wmod", "c2c"], [wk])
        for blk in range(8):
            t0 = blk * 256
            for s_ in range(2):
                dma("sp", xrow[:, s_, :], h_d[t0 + s_ * 128:t0 + (s_ + 1) * 128, :], DS("xrow"), [("h", t0 + s_ * 128)], ["xrow"])
            for s_ in range(2):
                norm_rows(xrow[:, s_, :], "xrow", xnT2, s_ * 128, "xnT2")
            if blk == 7:
                tap("xnB_bf", xnT2[:, :, :], ["xnT2"])
                tap("hB", xrow[:, :, :], ["xrow"])
            for fc in range(22):
                (bA, kA), (bB, kB) = BK2[fc % 2]
                sg_ = sgt2[fc % 2]
                for kc in range(8):
                    mm(bA[:, 0:256], wga[:, kc, fc * 128:(fc + 1) * 128], xnT2[:, kc, :], kc == 0, kc == 7, ["wga", "xnT2"], [kA])
                for kc in range(8):
                    mm(bB[:, 0:256], wu[:, kc, fc * 128:(fc + 1) * 128], xnT2[:, kc, :], kc == 0, kc == 7, ["wu", "xnT2"], [kB])
                activation(sg_[:, :], bA[:, 0:256], AF.Silu, [kA, "c2c"], ["sgt%d" % (fc % 2)], bias=c2c[:, fc, 0:1])
                dve(lambda e, fc=fc, bB=bB, sg_=sg_: e.scalar_tensor_tensor(out=uT[:, fc, :], in0=bB[:, 0:256], scalar=c2c[:, fc, 1:2], in1=sg_[:, :], op0=ALU.add, op1=ALU.mult),
                    [kB, "c2c", "sgt%d" % (fc % 2)], ["uT"])
            if blk == 7:
                tap("uT_bf", uT[:, :, :], ["uT"])
            for s_ in range(2):
                for half in range(2):
                    b, k = BK4[(s_ * 2 + half) % 4]
                    for fc in range(22):
                        mm(b[:, :], uT[:, fc, s_ * 128:(s_ + 1) * 128], wd[:, fc, half * 512:(half + 1) * 512], fc == 0, fc == 21, ["uT", "wd"], [k])
                    dve(lambda e, b=b, half=half: e.tensor_tensor(out=yrow[:, half * 512:(half + 1) * 512], in0=b[:, :],
                                                                 in1=g_bc[:, 1, half * 512:(half + 1) * 512], op=ALU.mult), [k, "g_bc"], ["yrow"])
                dve(lambda e, s_=s_: e.tensor_tensor(out=yrow[:, :], in0=yrow[:, :], in1=xrow[:, s_, :], op=ALU.add), ["yrow", "xrow"], ["yrow"])
                activation(sq_junk[:, :], yrow[:, :], AF.Square, ["yrow"], ["sq_junk", "ss"], accum_out=ss[:, 4:5])
                activation(ss[:, 5:6], ss[:, 4:5], AF.Ln, ["ss", "epsc"], ["ss"], scale=1.0 / D, bias=epsc[:, 0:1])
                activation(ss[:, 6:7], ss[:, 5:6], AF.Exp, ["ss"], ["ss"], scale=-0.5)
                hi = (blk * 2 + s_) % 2
                dve(lambda e, hi=hi: e.scalar_tensor_tensor(out=hrow[hi][:, :], in0=yrow[:, :], scalar=ss[:, 6:7], in1=fnw_bc[:, :], op0=ALU.mult, op1=ALU.mult),
                    ["yrow", "ss", "fnw_bc"], ["xt%d" % hi])
                tok = dma("sp", out_d[t0 + s_ * 128:t0 + (s_ + 1) * 128, :], hrow[hi][:, :], DS("xt%d" % hi), ["xt%d" % hi], ["out"])

        flush()
        es3.close()
    return nc


def _consts():
    p = np.arange(128)[:, None]
    f = np.arange(128)[None, :]
    I = (p == f).astype(np.float32)
    Mf = (p <= f).astype(np.float32)
    Mb = (p >= f).astype(np.float32)
    Sf = (p > f).astype(np.float32)
    Sb = (p < f).astype(np.float32)
    ones = np.ones((128, 128), np.float32)
    zeros = np.zeros((128, 128), np.float32)
    mats = [I, Mf, Mb, Sf, Sb, -Mf / 16.0, -Mb / 16.0, -Sf / 16.0, -Sb / 16.0, ones, zeros]
    return np.ascontiguousarray(np.concatenate(mats, axis=1).astype(np.float32))


def kernel(x, c, ctx, c_ctx, w_ada, b_ada, norm1_w, w_in, gla_up_f, gla_bias_f, gla_up_b, gla_bias_b,
           gla_norm_w, conv_w, conv_b, dt_bias_f, dt_bias_b, a_log_f, a_log_b, d_skip, ssm_norm_w,
           w_pa, w_pb, w_out, norm2_w, w_gate, w_up, w_down, final_norm_w, _return_maps=False):
    f = lambda a: np.ascontiguousarray(np.asarray(a, dtype=np.float32))
    x, c, ctx, c_ctx, w_in = f(x), f(c), f(ctx), f(c_ctx), f(w_in)
    win = w_in[0]
    consts = _consts()
    in_maps = []
    for core in range(NCORE):
        b, j = core // 4, core % 4
        cols = np.concatenate([
            np.arange(j * 128, (j + 1) * 128), 512 + np.arange(j * 128, (j + 1) * 128),
            1024 + np.arange(j * 256, (j + 1) * 256), 2048 + np.arange(j * 256, (j + 1) * 256),
            np.arange(3072, 3104), 3104 + np.arange(j * 512, (j + 1) * 512), 5152 + np.arange(j * 512, (j + 1) * 512),
            7200 + np.arange(j * 128, (j + 1) * 128), 7712 + np.arange(j * 128, (j + 1) * 128),
            8224 + np.arange(j * 8, (j + 1) * 8), 8256 + np.arange(j * 8, (j + 1) * 8)])
        assert cols.size == NW1
        cch = np.concatenate([np.arange(j * 512, (j + 1) * 512), 2048 + np.arange(j * 128, (j + 1) * 128),
                              2560 + np.arange(j * 128, (j + 1) * 128)])
        hs = slice(j * 128, (j + 1) * 128)
        h8 = slice(j * 8, (j + 1) * 8)
        in_maps.append({
            "x": f(x[b]), "ctx": f(ctx[b]), "cvec": f(np.stack([c[b], c_ctx])),
            "w_ada": f(w_ada[0]), "b_ada": f(b_ada[0]), "norm1_w": f(norm1_w[0]),
            "w1": f(win[:, cols]),
            "gla_up": f(np.stack([np.asarray(gla_up_f)[0][:, hs], np.asarray(gla_up_b)[0][:, hs]])),
            "gla_bias": f(np.stack([np.asarray(gla_bias_f)[0][hs], np.asarray(gla_bias_b)[0][hs]])),
            "gla_norm_w": f(gla_norm_w[0]),
            "conv_w": f(np.asarray(conv_w)[0][:, cch]), "conv_b": f(np.asarray(conv_b)[0][cch]),
            "dt_bias": f(np.stack([np.asarray(dt_bias_f)[0][h8], np.asarray(dt_bias_b)[0][h8]])),
            "a_log": f(np.stack([np.asarray(a_log_f)[0][h8], np.asarray(a_log_b)[0][h8]])),
            "d_skip": f(np.asarray(d_skip)[0][h8]),
            "ssm_norm_w": f(np.asarray(ssm_norm_w)[0][j * 512:(j + 1) * 512]),
            "w_gates": f(win[:, 8288:10336]), "w_pa": f(w_pa[0]), "w_pb": f(w_pb[0]), "w_out": f(w_out[0]),
            "norm2_w": f(norm2_w[0]), "w_gate": f(w_gate[0]), "w_up": f(w_up[0]), "w_down": f(w_down[0]),
            "final_norm_w": f(final_norm_w), "consts": consts,
            "xq": f(x[b, j * 2048:(j + 1) * 2048]),
        })
    if _return_maps:
        return in_maps
    nc = build_program(None)
    res = run_bass_kernel_spmd(nc, in_maps, core_ids=list(range(NCORE)))
    out = np.zeros((2, SEQ, D), np.float32)
    for core in range(NCORE):
        b, j = core // 4, core % 4
        out[b, j * 2048:(j + 1) * 2048] = np.asarray(res.results[core]["out"], dtype=np.float32)
    return out
```
